# Optimizing a Trainium2 kernel written in Bass

```python
import jax, jax.numpy as jnp
from jax import lax
import numpy as np

D_MODEL = 1024
BATCH = 16
SEQ = 4096
DEPTH = 4

N_META = 16
EPS = 1e-6
CONV_WIDTH = D_MODEL // 2
CONV_GROUPS = 8
SHORT_CONV_K = 3
LRU_WIDTH = D_MODEL // 2
LRU_HEADS = 8
LRU_HEAD_DIM = LRU_WIDTH // LRU_HEADS
LRU_CONV_K = 4
LRU_C = 8.0
EVEN_IN = 3 * CONV_WIDTH + 2 * LRU_WIDTH
EVEN_MIX = CONV_WIDTH + LRU_WIDTH
MLA_HEADS = 16
QK_NOPE = 64
QK_ROPE = 32
QK_HEAD = QK_NOPE + QK_ROPE
V_HEAD = 64
Q_LORA = 384
KV_LORA = 256
ODD_IN = Q_LORA + KV_LORA + QK_ROPE
ROPE_BASE = 10000.0
ATTN_BLOCK = 128
D_FF = 2816
FFN_CONV_K = 3
N_EVEN = (DEPTH + 1) // 2
N_ODD = DEPTH // 2

kernel_name = "hybrid_conv_rglru_mla_convffn"


def rms_norm(x, g):
    xf = x.astype(jnp.float32)
    y = xf * lax.rsqrt(jnp.mean(xf * xf, axis=-1, keepdims=True) + EPS)
    return (y * g.astype(jnp.float32)).astype(x.dtype)


def causal_dwconv(x, w):
    k_width = w.shape[0]
    t_len = x.shape[1]
    xp = jnp.pad(x, ((0, 0), (k_width - 1, 0), (0, 0)))
    y = xp[:, 0:t_len] * w[0]
    for k in range(1, k_width):
        y = y + xp[:, k:k + t_len] * w[k]
    return y


def rope_tables(t_len):
    pos = jnp.arange(t_len, dtype=jnp.float32)
    inv_freq = ROPE_BASE ** (-jnp.arange(0, QK_ROPE, 2, dtype=jnp.float32) / QK_ROPE)
    ang = pos[:, None] * inv_freq[None, :]
    return jnp.cos(ang), jnp.sin(ang)


def apply_rope(x, cos, sin):
    xf = x.astype(jnp.float32)
    x1, x2 = jnp.split(xf, 2, axis=-1)
    out = jnp.concatenate([x1 * cos - x2 * sin, x2 * cos + x1 * sin], axis=-1)
    return out.astype(x.dtype)


def rg_lru(xc, r_w, r_b, i_w, i_b, lam):
    b, t, _ = xc.shape
    xh = xc.reshape(b, t, LRU_HEADS, LRU_HEAD_DIM)
    r = jax.nn.sigmoid(jnp.einsum('bthi,hij->bthj', xh, r_w).reshape(b, t, LRU_WIDTH) + r_b)
    i = jax.nn.sigmoid(jnp.einsum('bthi,hij->bthj', xh, i_w).reshape(b, t, LRU_WIDTH) + i_b)
    log_a = -LRU_C * r.astype(jnp.float32) * jax.nn.softplus(-lam.astype(jnp.float32))
    a = jnp.exp(log_a)
    mult = jnp.sqrt(-jnp.expm1(2.0 * log_a))
    u = mult * (i * xc).astype(jnp.float32)

    def combine(left, right):
        a1, b1 = left
        a2, b2 = right
        return a1 * a2, a2 * b1 + b2

    _, h = lax.associative_scan(combine, (a, u), axis=1)
    return h.astype(xc.dtype)


def even_layer(x, norm, w_in, conv_a, conv_b, conv_b_bias, r_w, r_b, i_w, i_b, lam, w_out):
    h = rms_norm(x, norm)
    u = h @ w_in
    gb, gc, xa, xb, gate = jnp.split(
        u, [CONV_WIDTH, 2 * CONV_WIDTH, 3 * CONV_WIDTH, 3 * CONV_WIDTH + LRU_WIDTH], axis=-1)
    y_a = gb * causal_dwconv(gc * xa, conv_a)
    xc = causal_dwconv(xb, conv_b) + conv_b_bias
    y_b = jax.nn.gelu(gate) * rg_lru(xc, r_w, r_b, i_w, i_b, lam)
    return x + jnp.concatenate([y_a, y_b], axis=-1) @ w_out


def causal_block_attention(q, k, v):
    b, t, nh, dq = q.shape
    nb = -(-t // ATTN_BLOCK)
    tp = nb * ATTN_BLOCK
    pad = ((0, 0), (0, tp - t), (0, 0), (0, 0))
    q, k, v = jnp.pad(q, pad), jnp.pad(k, pad), jnp.pad(v, pad)
    qb = q.reshape(b, nb, ATTN_BLOCK, nh, dq).transpose(1, 0, 2, 3, 4)
    key_pos = jnp.arange(tp)
    scale = QK_HEAD ** -0.5
    neg = jnp.finfo(jnp.float32).min

    def one_block(args):
        q_blk, blk = args
        s = jnp.einsum('bqhd,bkhd->bhqk', q_blk, k).astype(jnp.float32) * scale
        q_pos = blk * ATTN_BLOCK + jnp.arange(ATTN_BLOCK)
        mask = key_pos[None, :] <= q_pos[:, None]
        s = jnp.where(mask[None, None], s, neg)
        p = jax.nn.softmax(s, axis=-1).astype(v.dtype)
        return jnp.einsum('bhqk,bkhd->bqhd', p, v)

    out = lax.map(one_block, (qb, jnp.arange(nb)))
    out = out.transpose(1, 0, 2, 3, 4).reshape(b, tp, nh, V_HEAD)
    return out[:, :t]


def odd_layer(x, cos, sin, norm, w_in, q_norm, kv_norm, w_uq, w_ukv, w_out):
    b, t, _ = x.shape
    h = rms_norm(x, norm)
    u = h @ w_in
    cq, ckv, k_r = jnp.split(u, [Q_LORA, Q_LORA + KV_LORA], axis=-1)
    q = (rms_norm(cq, q_norm) @ w_uq).reshape(b, t, MLA_HEADS, QK_HEAD)
    q_nope, q_rope = jnp.split(q, [QK_NOPE], axis=-1)
    q_rope = apply_rope(q_rope, cos[:, None, :], sin[:, None, :])
    kv = (rms_norm(ckv, kv_norm) @ w_ukv).reshape(b, t, MLA_HEADS, QK_NOPE + V_HEAD)
    k_nope, v = jnp.split(kv, [QK_NOPE], axis=-1)
    k_rope = apply_rope(k_r, cos, sin)
    k_rope = jnp.broadcast_to(k_rope[:, :, None, :], (b, t, MLA_HEADS, QK_ROPE))
    qf = jnp.concatenate([q_nope, q_rope], axis=-1)
    kf = jnp.concatenate([k_nope, k_rope], axis=-1)
    o = causal_block_attention(qf, kf, v).reshape(b, t, MLA_HEADS * V_HEAD)
    return x + o @ w_out


def ffn_layer(x, norm, w_up, conv_w, conv_b, w_down):
    h = rms_norm(x, norm)
    u = causal_dwconv(h @ w_up, conv_w) + conv_b
    a, g = jnp.split(u, 2, axis=-1)
    return x + (jax.nn.silu(a) * g) @ w_down


def setup_inputs(seed: int = 0) -> dict:
    key = jax.random.key(seed)
    ks = iter(jax.random.split(key, 40))

    def nrm(shape, scale):
        return jax.random.normal(next(ks), shape, jnp.float32) * scale

    def gain(shape):
        return 1.0 + nrm(shape, 0.01)

    u = jax.random.uniform(next(ks), (N_EVEN, LRU_WIDTH), jnp.float32, 0.9, 0.999)
    a_base = u ** (1.0 / LRU_C)
    lam = jnp.log(a_base) - jnp.log1p(-a_base)
    return {
        "x": nrm((BATCH, SEQ, D_MODEL), 1.0),
        "meta_tokens": nrm((N_META, D_MODEL), 1.0),
        "ev_norm": gain((N_EVEN, D_MODEL)),
        "ev_w_in": nrm((N_EVEN, D_MODEL, EVEN_IN), D_MODEL ** -0.5),
        "ev_conv_a": nrm((N_EVEN, SHORT_CONV_K, CONV_WIDTH), SHORT_CONV_K ** -0.5),
        "ev_conv_b": nrm((N_EVEN, LRU_CONV_K, LRU_WIDTH), LRU_CONV_K ** -0.5),
        "ev_conv_b_bias": nrm((N_EVEN, LRU_WIDTH), 0.02),
        "ev_gate_r_w": nrm((N_EVEN, LRU_HEADS, LRU_HEAD_DIM, LRU_HEAD_DIM), LRU_HEAD_DIM ** -0.5),
        "ev_gate_r_b": nrm((N_EVEN, LRU_WIDTH), 0.02),
        "ev_gate_i_w": nrm((N_EVEN, LRU_HEADS, LRU_HEAD_DIM, LRU_HEAD_DIM), LRU_HEAD_DIM ** -0.5),
        "ev_gate_i_b": nrm((N_EVEN, LRU_WIDTH), 0.02),
        "ev_lru_lambda": lam,
        "ev_w_out": nrm((N_EVEN, EVEN_MIX, D_MODEL), EVEN_MIX ** -0.5),
        "od_norm": gain((N_ODD, D_MODEL)),
        "od_w_in": nrm((N_ODD, D_MODEL, ODD_IN), D_MODEL ** -0.5),
        "od_q_norm": gain((N_ODD, Q_LORA)),
        "od_kv_norm": gain((N_ODD, KV_LORA)),
        "od_w_uq": nrm((N_ODD, Q_LORA, MLA_HEADS * QK_HEAD), Q_LORA ** -0.5),
        "od_w_ukv": nrm((N_ODD, KV_LORA, MLA_HEADS * (QK_NOPE + V_HEAD)), KV_LORA ** -0.5),
        "od_w_out": nrm((N_ODD, MLA_HEADS * V_HEAD, D_MODEL), (MLA_HEADS * V_HEAD) ** -0.5),
        "ffn_norm": gain((DEPTH, D_MODEL)),
        "ffn_w_up": nrm((DEPTH, D_MODEL, 2 * D_FF), D_MODEL ** -0.5),
        "ffn_conv_w": nrm((DEPTH, FFN_CONV_K, 2 * D_FF), FFN_CONV_K ** -0.5),
        "ffn_conv_b": nrm((DEPTH, 2 * D_FF), 0.02),
        "ffn_w_down": nrm((DEPTH, D_FF, D_MODEL), D_FF ** -0.5),
        "final_norm": gain((D_MODEL,)),
    }


def reference(x, meta_tokens, ev_norm, ev_w_in, ev_conv_a, ev_conv_b, ev_conv_b_bias,
              ev_gate_r_w, ev_gate_r_b, ev_gate_i_w, ev_gate_i_b, ev_lru_lambda, ev_w_out,
              od_norm, od_w_in, od_q_norm, od_kv_norm, od_w_uq, od_w_ukv, od_w_out,
              ffn_norm, ffn_w_up, ffn_conv_w, ffn_conv_b, ffn_w_down, final_norm):
    b = x.shape[0]
    meta = jnp.broadcast_to(meta_tokens[None].astype(x.dtype), (b, N_META, D_MODEL))
    h = jnp.concatenate([meta, x], axis=1)
    cos, sin = rope_tables(h.shape[1])
    for layer in range(DEPTH):
        j = layer // 2
        if layer % 2 == 0:
            h = even_layer(h, ev_norm[j], ev_w_in[j], ev_conv_a[j], ev_conv_b[j], ev_conv_b_bias[j],
                           ev_gate_r_w[j], ev_gate_r_b[j], ev_gate_i_w[j], ev_gate_i_b[j],
                           ev_lru_lambda[j], ev_w_out[j])
        else:
            h = odd_layer(h, cos, sin, od_norm[j], od_w_in[j], od_q_norm[j], od_kv_norm[j],
                          od_w_uq[j], od_w_ukv[j], od_w_out[j])
        h = ffn_layer(h, ffn_norm[layer], ffn_w_up[layer], ffn_conv_w[layer], ffn_conv_b[layer],
                      ffn_w_down[layer])
    h = rms_norm(h, final_norm)
    return h[:, N_META:]
```

```python
import numpy as np
from contextlib import ExitStack
import concourse.bass as bass
import concourse.mybir as mybir
from concourse.bass_utils import run_bass_kernel_spmd

F32 = mybir.dt.float32
BF16 = mybir.dt.bfloat16
AF = mybir.ActivationFunctionType
ALU = mybir.AluOpType

D = 1024
NMETA = 16
EPS = 1e-6
CW = 512
EVEN_IN = 2560
NH = 16
QKN, QKR, QKH, VH = 64, 32, 96, 64
QL, KVL = 384, 256
ODD_IN = QL + KVL + QKR
DFF = 2816
NCORES = 8


class Buf:
    __slots__ = ("lw", "rd", "excl")

    def __init__(self, excl=False):
        self.lw = None
        self.rd = {}
        self.excl = excl


def bufs(n):
    return [Buf() for _ in range(n)]


class FW:
    ENGS = ("pe", "dve", "act", "pool", "sp")

    def __init__(self, nc, n_dma_sems=32, same_engine_sync=True):
        self.nc = nc
        self.eng = {"pe": nc.tensor, "dve": nc.vector, "act": nc.scalar, "pool": nc.gpsimd, "sp": nc.sync}
        self.sems = {}
        self.cnt = {}
        for e in self.ENGS:
            self.sems[e] = nc.alloc_semaphore("clk_" + e)
            self.cnt[e] = 0
        self.dma_sems = []
        for i in range(n_dma_sems):
            k = "dma%d" % i
            self.sems[k] = nc.alloc_semaphore("clk_" + k)
            self.cnt[k] = 0
            self.dma_sems.append(k)
        self.dma_rr = 0
        self.seen = {e: {} for e in self.ENGS}
        self.same_engine_sync = same_engine_sync
        self.n_inst = 0

    def _need(self, e, key, val, needs):
        if key == e and (e == "pe" or not self.same_engine_sync):
            return
        if self.seen[e].get(key, 0) >= val:
            return
        if needs.get(key, 0) < val:
            needs[key] = val

    def _deps(self, e, reads, writes):
        needs = {}
        for b in reads:
            if b.lw is not None:
                self._need(e, b.lw[0], b.lw[1], needs)
            if b.excl:
                for k, v in b.rd.items():
                    if k != e:
                        self._need(e, k, v, needs)
        for b in writes:
            if b.lw is not None:
                self._need(e, b.lw[0], b.lw[1], needs)
            for k, v in b.rd.items():
                self._need(e, k, v, needs)
        return needs

    def _emit_waits(self, e, needs):
        eng = self.eng[e]
        for k, v in needs.items():
            eng.wait_ge(self.sems[k], v)
            self.seen[e][k] = v

    def _mark(self, key, val, reads, writes):
        for b in reads:
            if b.rd.get(key, 0) < val:
                b.rd[key] = val
        for b in writes:
            b.lw = (key, val)
            b.rd = {}

    def op(self, e, fn, reads=(), writes=()):
        needs = self._deps(e, reads, writes)
        self._emit_waits(e, needs)
        inst = fn(self.eng[e])
        self.cnt[e] += 1
        inst.then_inc(self.sems[e], 1)
        self._mark(e, self.cnt[e], reads, writes)
        self.n_inst += 1
        return inst

    def dma(self, out, in_, reads=(), writes=(), q="sp", **kw):
        key = self.dma_sems[self.dma_rr]
        self.dma_rr = (self.dma_rr + 1) % len(self.dma_sems)
        needs = self._deps(q, reads, writes)
        if self.cnt[key] > 0:
            self._need(q, key, self.cnt[key], needs)
        self._emit_waits(q, needs)
        inst = self.eng[q].dma_start(out=out, in_=in_, **kw)
        self.cnt[key] += 16
        inst.then_inc(self.sems[key], 16)
        self._mark(key, self.cnt[key], reads, writes)
        self.n_inst += 1
        return inst

    def barrier(self):
        for e in self.ENGS:
            needs = {}
            for k, v in self.cnt.items():
                if v > 0 and k != e:
                    self._need(e, k, v, needs)
            self._emit_waits(e, needs)


def make_tiles(T, smax=509):
    n = -(-T // smax)
    b = [round(i * T / n) for i in range(n + 1)]
    return [(b[i], b[i + 1] - b[i]) for i in range(n)]


def build_program(NSEQ=2, SEQ=4096, DEPTH=4, same_engine_sync=True, dbg=0):
    T = SEQ + NMETA
    NEV = (DEPTH + 1) // 2
    NOD = DEPTH // 2
    nc = bass.Bass("TRN2", target_bir_lowering=False)
    fw = FW(nc, same_engine_sync=same_engine_sync)

    uid = {"n": 0}

    def sbt(name, shape, dt):
        uid["n"] += 1
        return nc.sbuf_tensor("%s_u%d" % (name, uid["n"]), shape, dt)

    def din(name, shape):
        return nc.dram_tensor(name, list(shape), F32, kind="ExternalInput").ap()

    x_in = din("x", (NSEQ, SEQ, D))
    meta_in = din("meta_tokens", (NMETA, D))
    ev_norm = din("ev_norm", (NEV, D))
    ev_w_in = din("ev_w_in", (NEV, D, EVEN_IN))
    ev_conv_a = din("ev_conv_a", (NEV, 3, CW))
    ev_conv_b = din("ev_conv_b", (NEV, 4, CW))
    ev_conv_b_bias = din("ev_conv_b_bias", (NEV, CW))
    ev_gate_r_w = din("ev_gate_r_w", (NEV, 8, 64, 64))
    ev_gate_r_b = din("ev_gate_r_b", (NEV, CW))
    ev_gate_i_w = din("ev_gate_i_w", (NEV, 8, 64, 64))
    ev_gate_i_b = din("ev_gate_i_b", (NEV, CW))
    ev_lam = din("ev_lru_lambda", (NEV, CW))
    ev_w_out = din("ev_w_out", (NEV, D, D))
    if NOD:
        od_norm = din("od_norm", (NOD, D))
        od_w_in = din("od_w_in", (NOD, D, ODD_IN))
        od_q_norm = din("od_q_norm", (NOD, QL))
        od_kv_norm = din("od_kv_norm", (NOD, KVL))
        od_w_uq = din("od_w_uq", (NOD, QL, NH * QKH))
        od_w_ukv = din("od_w_ukv", (NOD, KVL, NH * (QKN + VH)))
        od_w_out = din("od_w_out", (NOD, D, D))
        rope_in = din("rope_cs", (128, 2, T))
    ffn_norm = din("ffn_norm", (DEPTH, D))
    ffn_w_up = din("ffn_w_up", (DEPTH, D, 2 * DFF))
    ffn_conv_w = din("ffn_conv_w", (DEPTH, 3, 2 * DFF))
    ffn_conv_b = din("ffn_conv_b", (DEPTH, 2 * DFF))
    ffn_w_down = din("ffn_w_down", (DEPTH, DFF, D))
    final_norm = din("final_norm", (D,))
    out_ap = nc.dram_tensor("out", [NSEQ, SEQ, D], F32, kind="ExternalOutput").ap()

    XP = [nc.dram_tensor("xres%d" % i, [NSEQ, D, T], F32, kind="Internal").ap() for i in range(2)]
    tiles = make_tiles(T)
    NT = len(tiles)
    SM = max(S for _, S in tiles)
    XB = [[bufs(NT) for _ in range(NSEQ)] for _ in range(2)]
    if NOD:
        QS = nc.dram_tensor("qs", [NSEQ, NH * QKN, T], BF16, kind="Internal").ap()
        QR = nc.dram_tensor("qr", [NSEQ, NH * QKR, T], BF16, kind="Internal").ap()
        KS = nc.dram_tensor("ks", [NSEQ, NH * QKN, T], BF16, kind="Internal").ap()
        KR = nc.dram_tensor("kr", [NSEQ, QKR, T], BF16, kind="Internal").ap()
        VS = nc.dram_tensor("vs", [NSEQ, T, NH * VH], BF16, kind="Internal").ap()
        OS = nc.dram_tensor("os", [NSEQ, D, T], BF16, kind="Internal").ap()
        QKVB = [Buf() for _ in range(NSEQ)]
        OSB = [Buf() for _ in range(NSEQ)]

    ps = nc.alloc_psum_tensor("ps", [128, 8, 512], F32)
    psb = [Buf(excl=True) for _ in range(8)]
    pstate = {"i": 0}

    def bank():
        lo = pstate.get("lo", 0)
        i = pstate["i"]
        if i < lo:
            i = lo
        pstate["i"] = i + 1 if i + 1 < 8 else lo
        return ps[:, i, :], psb[i]

    ones_bf = nc.alloc_sbuf_tensor("ones_bf", [128, 128], BF16)
    ones_f = nc.alloc_sbuf_tensor("ones_f", [128, 64], F32)
    ident = nc.alloc_sbuf_tensor("ident", [128, 128], F32)
    tri = nc.alloc_sbuf_tensor("tri", [128, 128], BF16)
    cb = Buf()
    fw.op("dve", lambda e: e.memset(ones_bf[:], 1.0), writes=[cb])
    fw.op("dve", lambda e: e.memset(ones_f[:], 1.0), writes=[cb])
    fw.op("dve", lambda e: e.memset(ident[:], 1.0), writes=[cb])
    fw.op("dve", lambda e: e.memset(tri[:], 1.0), writes=[cb])
    fw.op("pool", lambda e: e.affine_select(out=ident[:], in_=ident[:], pattern=[[-1, 128]], compare_op=ALU.is_equal,
                                            fill=0.0, base=0, channel_multiplier=1), reads=[cb], writes=[cb])
    fw.op("pool", lambda e: e.affine_select(out=tri[:], in_=tri[:], pattern=[[1, 128]], compare_op=ALU.is_ge,
                                            fill=0.0, base=0, channel_multiplier=-1), reads=[cb], writes=[cb])

    def load_vec(es, name, src, nchunk):
        t = es.enter_context(sbt(name, [128, nchunk], F32))
        b = Buf()
        fw.dma(t[:], src.rearrange("(c p) -> p c", p=128), writes=[b], allow_slow_non_contiguous=True)
        return t, b

    class StageA:
        def __init__(self, es, H, nslots=2):
            self.H = H
            W = SM + H
            self.W = W
            self.n = nslots
            self.nx = nslots + 1
            self.xt = [es.enter_context(sbt("xt%d" % i, [128, 8, W], F32)) for i in range(self.nx)]
            self.xtb = [bufs(8) for _ in range(self.nx)]
            self.hb = [es.enter_context(sbt("hb%d" % i, [128, 8, W], BF16)) for i in range(nslots)]
            self.hbb = [Buf() for _ in range(nslots)]
            self.sq = es.enter_context(sbt("sq", [128, 8, W], BF16))
            self.sqb = Buf()
            self.rs = [es.enter_context(sbt("rs%d" % i, [128, W], F32)) for i in range(nslots)]
            self.rsb = [Buf() for _ in range(nslots)]
            self.k = 0
            self.kx = 0
            self.pref = {}

        def prefetch(self, src, srcb, s, ti):
            t0, S = tiles[ti]
            H = self.H
            sx = self.kx % self.nx
            self.kx += 1
            xt, xtb = self.xt[sx], self.xtb[sx]
            W = S + H
            h0 = min(H, t0)
            if h0 < H:
                fw.op("pool", lambda e: e.memset(xt[:, :, 0:H - h0], 0.0), writes=xtb)
            rb = [srcb[s][ti]] + ([srcb[s][ti - 1]] if (h0 and ti > 0) else [])
            for half in range(2):
                c0 = half * 4
                fw.dma(xt[:, c0:c0 + 4, H - h0:W],
                       src[s, c0 * 128:(c0 + 4) * 128, t0 - h0:t0 + S].rearrange("(c p) t -> p c t", p=128),
                       reads=rb, writes=xtb[c0:c0 + 4])
            self.pref[(s, ti)] = sx

        def run(self, src, srcb, s, ti, gvec, gb, make_h=True, out_f32=None):
            t0, S = tiles[ti]
            H = self.H
            if (s, ti) not in self.pref:
                self.prefetch(src, srcb, s, ti)
            sx = self.pref.pop((s, ti))
            sl = self.k % self.n
            self.k += 1
            xt, xtb = self.xt[sx], self.xtb[sx]
            W = S + H
            for half in range(2):
                c0 = half * 4
                fw.op("act", lambda e: e.activation(out=self.sq[:, c0:c0 + 4, 0:W], in_=xt[:, c0:c0 + 4, 0:W], func=AF.Square),
                      reads=xtb[c0:c0 + 4], writes=[self.sqb])
            pa, pb = bank()
            for c in range(8):
                fw.op("pe", lambda e: e.matmul(pa[:, 0:W], ones_bf[:], self.sq[:, c, 0:W], start=(c == 0), stop=(c == 7)),
                      reads=[self.sqb, cb], writes=[pb])
            rs, rsb = self.rs[sl], self.rsb[sl]
            fw.op("act", lambda e: e.activation(out=rs[:, 0:W], in_=pa[:, 0:W], func=AF.Sqrt, scale=1.0 / D, bias=EPS),
                  reads=[pb], writes=[rsb])
            fw.op("dve", lambda e: e.reciprocal(out=rs[:, 0:W], in_=rs[:, 0:W]), reads=[rsb], writes=[rsb])
            if make_h:
                hb, hbb = self.hb[sl], self.hbb[sl]
                for c in range(8):
                    fw.op("dve", lambda e: e.scalar_tensor_tensor(out=hb[:, c, 0:W], in0=xt[:, c, 0:W], scalar=gvec[:, c:c + 1],
                                                                  in1=rs[:, 0:W], op0=ALU.mult, op1=ALU.mult),
                          reads=[xtb[c], rsb, gb], writes=[hbb])
            return sl, sx

    wl_hist = []

    def load_w_cast(dst, src, blist, rows_per=None):
        b = Buf()
        gate = [wl_hist[-4]] if len(wl_hist) >= 4 else []
        wl_hist.append(b)
        blist.append(b)
        fw.dma(dst, src, reads=gate, writes=[b], q="pool")

    def prologue(dst, dstb):
        with ExitStack() as es:
            xin = [es.enter_context(sbt("pxin%d" % i, [128, 4, D], F32)) for i in range(2)]
            xinb = [Buf() for _ in range(2)]
            st = [es.enter_context(sbt("pst%d" % i, [128, 8, 512], F32)) for i in range(2)]
            stb = [Buf() for _ in range(2)]
            mt = es.enter_context(sbt("pmeta", [16, D], F32))
            mtb = Buf()
            fw.dma(mt[:], meta_in[:, :], writes=[mtb])
            allb = [b for s in range(NSEQ) for b in dstb[s]]
            pa, pb = bank()
            for c in range(8):
                fw.op("pe", lambda e: e.transpose(pa[:, c * 16:(c + 1) * 16], mt[:, c * 128:(c + 1) * 128], ident[0:16, 0:16]),
                      reads=[mtb, cb], writes=[pb])
            fw.op("dve", lambda e: e.tensor_copy(st[0][:, :, 0:16], pa[:, 0:128].rearrange("p (c t) -> p c t", c=8)),
                  reads=[pb], writes=[stb[0]])
            for s in range(NSEQ):
                fw.dma(dst[s, :, 0:16].rearrange("(c p) t -> p c t", p=128), st[0][:, :, 0:16], reads=[stb[0]], writes=[Buf()])
            k = 0
            for s in range(NSEQ):
                for t0 in range(0, SEQ, 512):
                    n = min(512, SEQ - t0)
                    nb = n // 128
                    sl = k % 2
                    k += 1
                    fw.dma(xin[sl][:, 0:nb, :], x_in[s, t0:t0 + n, :].rearrange("(b p) d -> p b d", p=128), writes=[xinb[sl]])
                    for c in range(8):
                        pa, pb = bank()
                        for b_ in range(nb):
                            fw.op("pe", lambda e: e.transpose(pa[:, b_ * 128:(b_ + 1) * 128], xin[sl][:, b_, c * 128:(c + 1) * 128], ident[:]),
                                  reads=[xinb[sl], cb], writes=[pb])
                        eng = "dve" if c % 2 == 0 else "act"
                        if eng == "dve":
                            fw.op("dve", lambda e: e.tensor_copy(st[sl][:, c, 0:n], pa[:, 0:n]), reads=[pb], writes=[stb[sl]])
                        else:
                            fw.op("act", lambda e: e.activation(out=st[sl][:, c, 0:n], in_=pa[:, 0:n], func=AF.Copy), reads=[pb], writes=[stb[sl]])
                    fw.dma(dst[s, :, NMETA + t0:NMETA + t0 + n].rearrange("(c p) t -> p c t", p=128), st[sl][:, :, 0:n],
                           reads=[stb[sl]], writes=[Buf()])
        fw.barrier()

    def even_phase(j, src, srcb, dst, dstb):
        H = 3
        with ExitStack() as es:
            A = StageA(es, H)
            g_t, g_b = load_vec(es, "ev_g", ev_norm[j], 8)
            win = es.enter_context(sbt("ev_win", [128, 8, EVEN_IN], BF16))
            winb = []
            for c in range(8):
                load_w_cast(win[:, c, :], ev_w_in[j, c * 128:(c + 1) * 128, :], winb)
            wout = es.enter_context(sbt("ev_wout", [128, 8, D], BF16))
            woutb = []
            for c in range(8):
                load_w_cast(wout[:, c, :], ev_w_out[j, c * 128:(c + 1) * 128, :], woutb)
            wg = es.enter_context(sbt("ev_wg", [128, 2, 4, 128], BF16))
            wgb = Buf()
            fw.op("pool", lambda e: e.memset(wg[:], 0.0), writes=[wgb])
            for gi, gw in enumerate((ev_gate_r_w, ev_gate_i_w)):
                for c in range(4):
                    for hh in range(2):
                        fw.dma(wg[hh * 64:(hh + 1) * 64, gi, c, hh * 64:(hh + 1) * 64], gw[j, 2 * c + hh, :, :], writes=[wgb], q="pool")
            cva = es.enter_context(sbt("ev_cva", [128, 4, 3], F32))
            cvb = es.enter_context(sbt("ev_cvb", [128, 4, 4], F32))
            vb = Buf()
            for kk in range(3):
                fw.dma(cva[:, :, kk], ev_conv_a[j, kk].rearrange("(c p) -> p c", p=128), writes=[vb], allow_slow_non_contiguous=True)
            for kk in range(4):
                fw.dma(cvb[:, :, kk], ev_conv_b[j, kk].rearrange("(c p) -> p c", p=128), writes=[vb], allow_slow_non_contiguous=True)
            bbias, _b1 = load_vec(es, "ev_bb", ev_conv_b_bias[j], 4)
            rbias, _b2 = load_vec(es, "ev_rb", ev_gate_r_b[j], 4)
            ibias, _b3 = load_vec(es, "ev_ib", ev_gate_i_b[j], 4)
            lam, _b4 = load_vec(es, "ev_lam", ev_lam[j], 4)
            tv = [es.enter_context(sbt("ev_tv%d" % i, [128, 4], F32)) for i in range(5)]
            ca1 = es.enter_context(sbt("ev_ca1", [128, 4], F32))
            ca2 = es.enter_context(sbt("ev_ca2", [128, 4], F32))
            cab = Buf()
            y_, ay, e_, z_, p_ = tv
            def dv(fn, rd=()):
                fw.op("dve", fn, reads=[cab] + list(rd), writes=[cab])
            dv(lambda e: e.tensor_scalar(out=y_[:], in0=lam[:], scalar1=-1.0, scalar2=None, op0=ALU.mult), [_b4])
            dv(lambda e: e.tensor_tensor(out=ay[:], in0=y_[:], in1=lam[:], op=ALU.max), [_b4])
            fw.op("act", lambda e: e.activation(out=e_[:], in_=ay[:], func=AF.Exp, scale=-1.0), reads=[cab], writes=[cab])
            dv(lambda e: e.tensor_scalar(out=z_[:], in0=e_[:], scalar1=2.0, scalar2=None, op0=ALU.add))
            dv(lambda e: e.reciprocal(out=z_[:], in_=z_[:]))
            dv(lambda e: e.tensor_tensor(out=z_[:], in0=z_[:], in1=e_[:], op=ALU.mult))
            dv(lambda e: e.tensor_tensor(out=e_[:], in0=z_[:], in1=z_[:], op=ALU.mult))
            dv(lambda e: e.memset(p_[:], 1.0 / 15.0))
            for kk in (13, 11, 9, 7, 5, 3, 1):
                dv(lambda e: e.tensor_tensor(out=p_[:], in0=p_[:], in1=e_[:], op=ALU.mult))
                dv(lambda e: e.tensor_scalar(out=p_[:], in0=p_[:], scalar1=1.0 / kk, scalar2=None, op0=ALU.add))
            dv(lambda e: e.tensor_tensor(out=p_[:], in0=p_[:], in1=z_[:], op=ALU.mult))
            dv(lambda e: e.tensor_scalar(out=y_[:], in0=y_[:], scalar1=0.0, scalar2=None, op0=ALU.max))
            dv(lambda e: e.scalar_tensor_tensor(out=y_[:], in0=p_[:], scalar=2.0, in1=y_[:], op0=ALU.mult, op1=ALU.add))
            dv(lambda e: e.tensor_scalar(out=ca1[:], in0=y_[:], scalar1=-8.0, scalar2=None, op0=ALU.mult))
            dv(lambda e: e.tensor_scalar(out=ca2[:], in0=y_[:], scalar1=-16.0, scalar2=None, op0=ALU.mult))
            vecb = [vb, _b1, _b2, _b3, cab]

            WM = SM + H
            def sb1(name, dt=F32, shape=None):
                return es.enter_context(sbt(name, shape or [128, 4, WM], dt))
            gcs = sb1("gcs"); gcsb = bufs(4)
            cvt = sb1("cvat"); cvtb = bufs(4)
            xc = sb1("xc"); xcb = bufs(4)
            xcbf = sb1("xcbf", BF16); xcbfb = bufs(4)
            rr = sb1("rr"); rrb = bufs(4)
            ii = sb1("ii"); iib = bufs(4)
            mm = sb1("mm"); mmb = bufs(4)
            gg = sb1("gg"); ggb = bufs(4)
            hs = sb1("hs"); hsb = bufs(4)
            carry = es.enter_context(sbt("carry", [128, 4], F32)); carb = bufs(4)
            ymix = [es.enter_context(sbt("ymix%d" % i, [128, 8, SM], BF16)) for i in range(2)]
            ymb = [bufs(8) for _ in range(2)]

            order = [(s, ti) for s in range(NSEQ) for ti in range(NT)]
            for o_ in order[0:2]:
                A.prefetch(src, srcb, o_[0], o_[1])
            slot_next = A.run(src, srcb, order[0][0], order[0][1], g_t, g_b)
            for oi, (s, ti) in enumerate(order):
                t0, S = tiles[ti]
                W = S + H
                sl = slot_next
                if oi + 2 < len(order):
                    A.prefetch(src, srcb, order[oi + 2][0], order[oi + 2][1])
                if oi + 1 < len(order):
                    slot_next = A.run(src, srcb, order[oi + 1][0], order[oi + 1][1], g_t, g_b)
                hb, hbb, xt, xtb = A.hb[sl[0]], A.hbb[sl[0]], A.xt[sl[1]], A.xtb[sl[1]]
                z = oi % 2

                def proj(col0):
                    pa, pb = bank()
                    for c in range(8):
                        fw.op("pe", lambda e: e.matmul(pa[:, 0:W], win[:, c, col0:col0 + 128], hb[:, c, 0:W], start=(c == 0), stop=(c == 7)),
                              reads=[hbb] + winb, writes=[pb])
                    return pa, pb

                for c in range(4):
                    pa, pb = proj(2048 + c * 128)
                    fw.op("act", lambda e: e.activation(out=gg[:, c, 0:S], in_=pa[:, H:W], func=AF.Gelu_apprx_tanh),
                          reads=[pb], writes=[ggb[c]])
                for c in range(4):
                    pa, pb = proj(512 + c * 128)
                    fw.op("act", lambda e: e.activation(out=gcs[:, c, 0:W], in_=pa[:, 0:W], func=AF.Copy), reads=[pb], writes=[gcsb[c]])
                    pa, pb = proj(1024 + c * 128)
                    fw.op("dve", lambda e: e.tensor_tensor(out=gcs[:, c, 0:W], in0=pa[:, 0:W], in1=gcs[:, c, 0:W], op=ALU.mult),
                          reads=[pb, gcsb[c]], writes=[gcsb[c]])
                    fw.op("act", lambda e: e.activation(out=cvt[:, c, 0:S], in_=gcs[:, c, H:W], func=AF.Identity, scale=cva[:, c, 2:3]),
                          reads=[gcsb[c]] + vecb, writes=[cvtb[c]])
                    for kk in (1, 0):
                        sh = 2 - kk
                        fw.op("dve", lambda e: e.scalar_tensor_tensor(out=cvt[:, c, 0:S], in0=gcs[:, c, H - sh:W - sh], scalar=cva[:, c, kk:kk + 1],
                                                                      in1=cvt[:, c, 0:S], op0=ALU.mult, op1=ALU.add),
                              reads=[gcsb[c], cvtb[c]] + vecb, writes=[cvtb[c]])
                    pa, pb = proj(c * 128)
                    fw.op("dve", lambda e: e.tensor_tensor(out=ymix[z][:, c, 0:S], in0=pa[:, H:W], in1=cvt[:, c, 0:S], op=ALU.mult),
                          reads=[pb, cvtb[c]], writes=[ymb[z][c]])
                for c in range(4):
                    pa, pb = proj(1536 + c * 128)
                    fw.op("act", lambda e: e.activation(out=xc[:, c, 0:S], in_=pa[:, H:W], func=AF.Identity, scale=cvb[:, c, 3:4], bias=bbias[:, c:c + 1]),
                          reads=[pb] + vecb, writes=[xcb[c]])
                    for kk in (2, 1, 0):
                        sh = 3 - kk
                        fw.op("dve", lambda e: e.scalar_tensor_tensor(out=xc[:, c, 0:S], in0=pa[:, H - sh:W - sh], scalar=cvb[:, c, kk:kk + 1],
                                                                      in1=xc[:, c, 0:S], op0=ALU.mult, op1=ALU.add),
                              reads=[pb, xcb[c]] + vecb, writes=[xcb[c]])
                    fw.op("act", lambda e: e.activation(out=xcbf[:, c, 0:S], in_=xc[:, c, 0:S], func=AF.Copy), reads=[xcb[c]], writes=[xcbfb[c]])
                for c in range(4):
                    for gi, (dstt, dstb_, bias_t) in enumerate(((rr, rrb, rbias), (ii, iib, ibias))):
                        pa, pb = bank()
                        fw.op("pe", lambda e: e.matmul(pa[:, 0:S], wg[:, gi, c, :], xcbf[:, c, 0:S], start=True, stop=True),
                              reads=[xcbfb[c], wgb], writes=[pb])
                        fw.op("act", lambda e: e.activation(out=dstt[:, c, 0:S], in_=pa[:, 0:S], func=AF.Sigmoid, bias=bias_t[:, c:c + 1]),
                              reads=[pb] + vecb, writes=[dstb_[c]])
                for c in range(4):
                    fw.op("act", lambda e: e.activation(out=mm[:, c, 0:S], in_=rr[:, c, 0:S], func=AF.Exp, scale=ca2[:, c:c + 1]),
                          reads=[rrb[c]] + vecb, writes=[mmb[c]])
                    fw.op("act", lambda e: e.activation(out=rr[:, c, 0:S], in_=rr[:, c, 0:S], func=AF.Exp, scale=ca1[:, c:c + 1]),
                          reads=[rrb[c]] + vecb, writes=[rrb[c]])
                for c in range(4):
                    fw.op("act", lambda e: e.activation(out=mm[:, c, 0:S], in_=mm[:, c, 0:S], func=AF.Sqrt, scale=-1.0, bias=1.0),
                          reads=[mmb[c]], writes=[mmb[c]])
                for c in range(4):
                    fw.op("dve", lambda e: e.tensor_tensor(out=ii[:, c, 0:S], in0=ii[:, c, 0:S], in1=xc[:, c, 0:S], op=ALU.mult),
                          reads=[iib[c], xcb[c]], writes=[iib[c]])
                for c in range(4):
                    fw.op("dve", lambda e: e.tensor_tensor(out=ii[:, c, 0:S], in0=ii[:, c, 0:S], in1=mm[:, c, 0:S], op=ALU.mult),
                          reads=[iib[c], mmb[c]], writes=[iib[c]])
                for c in range(4):
                    init = 0.0 if ti == 0 else carry[:, c:c + 1]
                    fw.op("dve", lambda e: e.tensor_tensor_scan(out=hs[:, c, 0:S], data0=rr[:, c, 0:S], data1=ii[:, c, 0:S],
                                                                initial=init, op0=ALU.mult, op1=ALU.add),
                          reads=[rrb[c], iib[c], carb[c]], writes=[hsb[c]])
                for c in range(4):
                    fw.op("pool", lambda e: e.tensor_copy(carry[:, c:c + 1], hs[:, c, S - 1:S]), reads=[hsb[c]], writes=[carb[c]])
                    fw.op("dve", lambda e: e.tensor_tensor(out=ymix[z][:, 4 + c, 0:S], in0=gg[:, c, 0:S], in1=hs[:, c, 0:S], op=ALU.mult),
                          reads=[ggb[c], hsb[c]], writes=[ymb[z][4 + c]])
                for m in range(8):
                    pa, pb = bank()
                    for c in range(8):
                        fw.op("pe", lambda e: e.matmul(pa[:, 0:S], wout[:, c, m * 128:(m + 1) * 128], ymix[z][:, c, 0:S], start=(c == 0), stop=(c == 7)),
                              reads=[ymb[z][c]] + woutb, writes=[pb])
                    fw.op("dve", lambda e: e.tensor_tensor(out=xt[:, m, H:W], in0=xt[:, m, H:W], in1=pa[:, 0:S], op=ALU.add),
                          reads=[pb, xtb[m]], writes=[xtb[m]])
                for half in range(2):
                    c0 = half * 4
                    fw.dma(dst[s, c0 * 128:(c0 + 4) * 128, t0:t0 + S].rearrange("(c p) t -> p c t", p=128), xt[:, c0:c0 + 4, H:W],
                           reads=xtb[c0:c0 + 4], writes=[dstb[s][ti]], q="pool")
        fw.barrier()

    def ffn_phase(l, half, src, srcb, dst, dstb):
        H = 2
        NJ = 11
        j0 = half * NJ
        with ExitStack() as es:
            A = StageA(es, H)
            g_t, g_b = load_vec(es, "ff_g", ffn_norm[l], 8)
            wup = es.enter_context(sbt("ff_wup", [128, 8, 2, NJ * 128], BF16))
            JH = ((0, 6), (6, NJ))
            wupb = [[], []]
            for jh, (ja, jb) in enumerate(JH):
                for c in range(8):
                    for ag in range(2):
                        col = ag * DFF + (j0 + ja) * 128
                        load_w_cast(wup[:, c, ag, ja * 128:jb * 128], ffn_w_up[l, c * 128:(c + 1) * 128, col:col + (jb - ja) * 128], wupb[jh])
            wdn = es.enter_context(sbt("ff_wdn", [128, NJ, D], BF16))
            wdnb = []
            for jj in range(NJ):
                load_w_cast(wdn[:, jj, :], ffn_w_down[l, (j0 + jj) * 128:(j0 + jj + 1) * 128, :], wdnb)
            cw = es.enter_context(sbt("ff_cw", [128, 2, NJ, 3], F32))
            cbias = es.enter_context(sbt("ff_cb", [128, 2, NJ], F32))
            vb = Buf()
            for ag in range(2):
                col = ag * DFF + j0 * 128
                for kk in range(3):
                    fw.dma(cw[:, ag, :, kk], ffn_conv_w[l, kk, col:col + NJ * 128].rearrange("(c p) -> p c", p=128), writes=[vb], allow_slow_non_contiguous=True)
                fw.dma(cbias[:, ag, :], ffn_conv_b[l, col:col + NJ * 128].rearrange("(c p) -> p c", p=128), writes=[vb], allow_slow_non_contiguous=True)
            cvt = [[es.enter_context(sbt("ff_cv%d_%d" % (ag, i), [128, SM], F32)) for i in range(3)] for ag in range(2)]
            cvtb = [bufs(3) for _ in range(2)]
            sa = [es.enter_context(sbt("ff_sa%d" % i, [128, SM], F32)) for i in range(3)]
            sab = bufs(3)
            mb = [es.enter_context(sbt("ff_m%d" % i, [128, NJ, SM], BF16)) for i in range(2)]
            mbb = [bufs(NJ) for _ in range(2)]
            if half == 1:
                xr = [es.enter_context(sbt("ff_xr%d" % i, [128, 8, SM], F32)) for i in range(2)]
                xrb = [bufs(8) for _ in range(2)]
            q = 0
            order = [(s, ti) for s in range(NSEQ) for ti in range(NT)]
            for o_ in order[0:2]:
                A.prefetch(src, srcb, o_[0], o_[1])
            slot_next = A.run(src, srcb, order[0][0], order[0][1], g_t, g_b)
            for oi, (s, ti) in enumerate(order):
                t0, S = tiles[ti]
                W = S + H
                sl = slot_next
                if oi + 2 < len(order):
                    A.prefetch(src, srcb, order[oi + 2][0], order[oi + 2][1])
                z = oi % 2
                if half == 1:
                    for hf in range(2):
                        c0 = hf * 4
                        fw.dma(xr[z][:, c0:c0 + 4, 0:S], dst[s, c0 * 128:(c0 + 4) * 128, t0:t0 + S].rearrange("(c p) t -> p c t", p=128),
                               reads=[dstb[s][ti]], writes=xrb[z][c0:c0 + 4])
                if oi + 1 < len(order):
                    slot_next = A.run(src, srcb, order[oi + 1][0], order[oi + 1][1], g_t, g_b)
                hb, hbb, xt, xtb = A.hb[sl[0]], A.hbb[sl[0]], A.xt[sl[1]], A.xtb[sl[1]]
                for jj in range(NJ):
                    r = q % 3
                    q += 1
                    pab = []
                    for ag in range(2):
                        pa, pb = bank()
                        for c in range(8):
                            fw.op("pe", lambda e: e.matmul(pa[:, 0:W], wup[:, c, ag, jj * 128:(jj + 1) * 128], hb[:, c, 0:W], start=(c == 0), stop=(c == 7)),
                                  reads=[hbb] + wupb[0 if jj < 6 else 1], writes=[pb])
                        pab.append((pa, pb))
                    for ag in range(2):
                        pa, pb = pab[ag]
                        fw.op("act", lambda e: e.activation(out=cvt[ag][r][:, 0:S], in_=pa[:, H:W], func=AF.Identity, scale=cw[:, ag, jj, 2:3], bias=cbias[:, ag, jj:jj + 1]),
                              reads=[pb, vb], writes=[cvtb[ag][r]])
                    for kk in (1, 0):
                        sh = 2 - kk
                        for ag in range(2):
                            pa, pb = pab[ag]
                            fw.op("dve", lambda e: e.scalar_tensor_tensor(out=cvt[ag][r][:, 0:S], in0=pa[:, H - sh:W - sh], scalar=cw[:, ag, jj, kk:kk + 1],
                                                                          in1=cvt[ag][r][:, 0:S], op0=ALU.mult, op1=ALU.add),
                                  reads=[pb, cvtb[ag][r], vb], writes=[cvtb[ag][r]])
                    fw.op("act", lambda e: e.activation(out=sa[r][:, 0:S], in_=cvt[0][r][:, 0:S], func=AF.Silu), reads=[cvtb[0][r]], writes=[sab[r]])
                    fw.op("pool", lambda e: e.tensor_tensor(out=mb[z][:, jj, 0:S], in0=sa[r][:, 0:S], in1=cvt[1][r][:, 0:S], op=ALU.mult),
                          reads=[sab[r], cvtb[1][r]], writes=[mbb[z][jj]])
                for m in range(8):
                    pa, pb = bank()
                    for jj in range(NJ):
                        fw.op("pe", lambda e: e.matmul(pa[:, 0:S], wdn[:, jj, m * 128:(m + 1) * 128], mb[z][:, jj, 0:S], start=(jj == 0), stop=(jj == NJ - 1)),
                              reads=[mbb[z][jj]] + wdnb, writes=[pb])
                    if half == 0:
                        fw.op("dve", lambda e: e.tensor_tensor(out=xt[:, m, H:W], in0=xt[:, m, H:W], in1=pa[:, 0:S], op=ALU.add),
                              reads=[pb, xtb[m]], writes=[xtb[m]])
                    else:
                        fw.op("dve", lambda e: e.tensor_tensor(out=xr[z][:, m, 0:S], in0=xr[z][:, m, 0:S], in1=pa[:, 0:S], op=ALU.add),
                              reads=[pb, xrb[z][m]], writes=[xrb[z][m]])
                for hf in range(2):
                    c0 = hf * 4
                    if half == 0:
                        fw.dma(dst[s, c0 * 128:(c0 + 4) * 128, t0:t0 + S].rearrange("(c p) t -> p c t", p=128), xt[:, c0:c0 + 4, H:W],
                               reads=xtb[c0:c0 + 4], writes=[dstb[s][ti]], q="pool")
                    else:
                        fw.dma(dst[s, c0 * 128:(c0 + 4) * 128, t0:t0 + S].rearrange("(c p) t -> p c t", p=128), xr[z][:, c0:c0 + 4, 0:S],
                               reads=xrb[z][c0:c0 + 4], writes=[dstb[s][ti]], q="pool")
        fw.barrier()

    def odd_proj_phase(j, src, srcb):
        with ExitStack() as es:
            A = StageA(es, 0)
            g_t, g_b = load_vec(es, "od_g", od_norm[j], 8)
            gq, gqb = load_vec(es, "od_gq", od_q_norm[j], 3)
            gkv, gkvb = load_vec(es, "od_gkv", od_kv_norm[j], 2)
            win = es.enter_context(sbt("od_win", [128, 8, ODD_IN + 32], BF16))
            winb = []
            for c in range(8):
                rows = od_w_in[j, c * 128:(c + 1) * 128, :]
                load_w_cast(win[:, c, 0:ODD_IN], rows, winb)
                load_w_cast(win[:, c, ODD_IN:ODD_IN + 16], rows[:, 656:672], winb)
                load_w_cast(win[:, c, ODD_IN + 16:ODD_IN + 32], rows[:, 640:656], winb)
            wqn = es.enter_context(sbt("od_wqn", [128, 3, NH, QKN], BF16))
            wqr = es.enter_context(sbt("od_wqr", [128, 3, 2, NH, QKR], BF16))
            wqb = []
            for c in range(3):
                rows = od_w_uq[j, c * 128:(c + 1) * 128, :].rearrange("p (h e) -> p h e", h=NH)
                load_w_cast(wqn[:, c, :, :], rows[:, :, 0:64], wqb)
                load_w_cast(wqr[:, c, 0, :, :], rows[:, :, 64:96], wqb)
                load_w_cast(wqr[:, c, 1, :, 0:16], rows[:, :, 80:96], wqb)
                load_w_cast(wqr[:, c, 1, :, 16:32], rows[:, :, 64:80], wqb)
            wkn = es.enter_context(sbt("od_wkn", [128, 2, NH, QKN], BF16))
            wv = es.enter_context(sbt("od_wv", [128, 2, NH, VH], BF16))
            wkvb = []
            for c in range(2):
                rows = od_w_ukv[j, c * 128:(c + 1) * 128, :].rearrange("p (h e) -> p h e", h=NH)
                load_w_cast(wkn[:, c, :, :], rows[:, :, 0:64], wkvb)
                load_w_cast(wv[:, c, :, :], rows[:, :, 64:128], wkvb)
            cs = es.enter_context(sbt("od_cs", [128, 2, T], F32))
            csb = Buf()
            fw.dma(cs[:, 0, :], rope_in[:, 0, :], writes=[csb])
            fw.dma(cs[:, 1, :], rope_in[:, 1, :], writes=[csb])
            S_ = SM
            def sb(name, shape, dt=F32, n=2):
                return [es.enter_context(sbt("%s%d" % (name, i), shape, dt)) for i in range(n)]
            cl = sb("od_cl", [128, 5, S_]); clb = [bufs(5) for _ in range(2)]
            csq = sb("od_csq", [128, 5, S_], BF16); csqb = [bufs(5) for _ in range(2)]
            rsq = sb("od_rsq", [128, 2, S_]); rsqb = [bufs(2) for _ in range(2)]
            cn = sb("od_cn", [128, 5, S_], BF16); cnb = [bufs(5) for _ in range(2)]
            krt = sb("od_kr", [32, 3, S_]); krb = [Buf() for _ in range(2)]
            krbf = sb("od_krbf", [32, S_], BF16); krbfb = [Buf() for _ in range(2)]
            qst = sb("od_qst", [128, S_], BF16, n=4); qstb = bufs(4)
            rt = sb("od_rt", [128, 2, S_], F32, n=2); rtb = bufs(2)
            vst = sb("od_vst", [128, NH * VH], BF16, n=2); vstb = bufs(2)
            k = 0
            qi = 0
            ri = 0
            vi = 0
            order = [(s, ti) for s in range(NSEQ) for ti in range(NT)]
            for o_ in order[0:2]:
                A.prefetch(src, srcb, o_[0], o_[1])
            slot_next = A.run(src, srcb, order[0][0], order[0][1], g_t, g_b)
            for oi, (s, ti) in enumerate(order):
                if True:
                    t0, S = tiles[ti]
                    sl = slot_next
                    if oi + 2 < len(order):
                        A.prefetch(src, srcb, order[oi + 2][0], order[oi + 2][1])
                    if oi + 1 < len(order):
                        slot_next = A.run(src, srcb, order[oi + 1][0], order[oi + 1][1], g_t, g_b)
                    hb, hbb = A.hb[sl[0]], A.hbb[sl[0]]
                    z = k % 2
                    k += 1
                    for c in range(5):
                        pa, pb = bank()
                        for kc in range(8):
                            fw.op("pe", lambda e: e.matmul(pa[:, 0:S], win[:, kc, c * 128:(c + 1) * 128], hb[:, kc, 0:S], start=(kc == 0), stop=(kc == 7)),
                                  reads=[hbb] + winb, writes=[pb])
                        fw.op("dve", lambda e: e.tensor_copy(cl[z][:, c, 0:S], pa[:, 0:S]), reads=[pb], writes=[clb[z][c]])
                        fw.op("act", lambda e: e.activation(out=csq[z][:, c, 0:S], in_=cl[z][:, c, 0:S], func=AF.Square), reads=[clb[z][c]], writes=[csqb[z][c]])
                    pk = []
                    for v in range(2):
                        pa, pb = bank()
                        col = 640 if v == 0 else ODD_IN
                        for kc in range(8):
                            fw.op("pe", lambda e: e.matmul(pa[0:32, 0:S], win[:, kc, col:col + 32], hb[:, kc, 0:S], start=(kc == 0), stop=(kc == 7)),
                                  reads=[hbb] + winb, writes=[pb])
                        pk.append((pa, pb))
                    fw.op("dve", lambda e: e.tensor_tensor(out=krt[z][:, 0, 0:S], in0=pk[0][0][0:32, 0:S], in1=cs[0:32, 0, t0:t0 + S], op=ALU.mult),
                          reads=[pk[0][1], csb], writes=[krb[z]])
                    fw.op("dve", lambda e: e.tensor_tensor(out=krt[z][:, 1, 0:S], in0=pk[1][0][0:32, 0:S], in1=cs[0:32, 1, t0:t0 + S], op=ALU.mult),
                          reads=[pk[1][1], csb], writes=[krb[z]])
                    fw.op("dve", lambda e: e.tensor_tensor(out=krbf[z][:, 0:S], in0=krt[z][:, 0, 0:S], in1=krt[z][:, 1, 0:S], op=ALU.add),
                          reads=[krb[z]], writes=[krbfb[z]])
                    fw.dma(KR[s, :, t0:t0 + S], krbf[z][:, 0:S], reads=[krbfb[z]], writes=[Buf()])
                    for li, (c0, ncx, dim, gv, gvb) in enumerate(((0, 3, QL, gq, gqb), (3, 2, KVL, gkv, gkvb))):
                        pa, pb = bank()
                        for c in range(ncx):
                            fw.op("pe", lambda e: e.matmul(pa[:, 0:S], ones_bf[:], csq[z][:, c0 + c, 0:S], start=(c == 0), stop=(c == ncx - 1)),
                                  reads=[csqb[z][c0 + c], cb], writes=[pb])
                        fw.op("act", lambda e: e.activation(out=rsq[z][:, li, 0:S], in_=pa[:, 0:S], func=AF.Sqrt, scale=1.0 / dim, bias=EPS),
                              reads=[pb], writes=[rsqb[z][li]])
                        fw.op("dve", lambda e: e.reciprocal(out=rsq[z][:, li, 0:S], in_=rsq[z][:, li, 0:S]), reads=[rsqb[z][li]], writes=[rsqb[z][li]])
                        for c in range(ncx):
                            fw.op("dve", lambda e: e.scalar_tensor_tensor(out=cn[z][:, c0 + c, 0:S], in0=cl[z][:, c0 + c, 0:S], scalar=gv[:, c:c + 1],
                                                                          in1=rsq[z][:, li, 0:S], op0=ALU.mult, op1=ALU.mult),
                                  reads=[clb[z][c0 + c], rsqb[z][li], gvb], writes=[cnb[z][c0 + c]])
                    for m in range(8):
                        pa, pb = bank()
                        for c in range(3):
                            fw.op("pe", lambda e: e.matmul(pa[:, 0:S], wqn[:, c, 2 * m:2 * m + 2, :], cn[z][:, c, 0:S], start=(c == 0), stop=(c == 2)),
                                  reads=[cnb[z][c]] + wqb, writes=[pb])
                        u = qi % 4
                        qi += 1
                        fw.op("act", lambda e: e.activation(out=qst[u][:, 0:S], in_=pa[:, 0:S], func=AF.Copy), reads=[pb], writes=[qstb[u]])
                        fw.dma(QS[s, m * 128:(m + 1) * 128, t0:t0 + S], qst[u][:, 0:S], reads=[qstb[u]], writes=[Buf()])
                    for m in range(4):
                        pr = []
                        for v in range(2):
                            pa, pb = bank()
                            for c in range(3):
                                fw.op("pe", lambda e: e.matmul(pa[:, 0:S], wqr[:, c, v, 4 * m:4 * m + 4, :], cn[z][:, c, 0:S], start=(c == 0), stop=(c == 2)),
                                      reads=[cnb[z][c]] + wqb, writes=[pb])
                            pr.append((pa, pb))
                        w_ = ri % 2
                        ri += 1
                        u = qi % 4
                        qi += 1
                        fw.op("dve", lambda e: e.tensor_tensor(out=rt[w_][:, 0, 0:S], in0=pr[0][0][:, 0:S], in1=cs[:, 0, t0:t0 + S], op=ALU.mult),
                              reads=[pr[0][1], csb], writes=[rtb[w_]])
                        fw.op("dve", lambda e: e.tensor_tensor(out=rt[w_][:, 1, 0:S], in0=pr[1][0][:, 0:S], in1=cs[:, 1, t0:t0 + S], op=ALU.mult),
                              reads=[pr[1][1], csb], writes=[rtb[w_]])
                        fw.op("dve", lambda e: e.tensor_tensor(out=qst[u][:, 0:S], in0=rt[w_][:, 0, 0:S], in1=rt[w_][:, 1, 0:S], op=ALU.add),
                              reads=[rtb[w_]], writes=[qstb[u]])
                        fw.dma(QR[s, m * 128:(m + 1) * 128, t0:t0 + S], qst[u][:, 0:S], reads=[qstb[u]], writes=[Buf()])
                    for m in range(8):
                        pa, pb = bank()
                        for c in range(2):
                            fw.op("pe", lambda e: e.matmul(pa[:, 0:S], wkn[:, c, 2 * m:2 * m + 2, :], cn[z][:, 3 + c, 0:S], start=(c == 0), stop=(c == 1)),
                                  reads=[cnb[z][3 + c]] + wkvb, writes=[pb])
                        u = qi % 4
                        qi += 1
                        fw.op("act", lambda e: e.activation(out=qst[u][:, 0:S], in_=pa[:, 0:S], func=AF.Copy), reads=[pb], writes=[qstb[u]])
                        fw.dma(KS[s, m * 128:(m + 1) * 128, t0:t0 + S], qst[u][:, 0:S], reads=[qstb[u]], writes=[Buf()])
                    for b0 in range(0, S, 128):
                        nb_ = min(128, S - b0)
                        u = vi % 2
                        vi += 1
                        for hv in range(2):
                            pa, pb = bank()
                            for c in range(2):
                                fw.op("pe", lambda e: e.matmul(pa[0:nb_, 0:512], cn[z][:, 3 + c, b0:b0 + nb_], wv[:, c, hv * 8:(hv + 1) * 8, :], start=(c == 0), stop=(c == 1)),
                                      reads=[cnb[z][3 + c]] + wkvb, writes=[pb])
                            if hv == 0:
                                fw.op("act", lambda e: e.activation(out=vst[u][0:nb_, 0:512], in_=pa[0:nb_, 0:512], func=AF.Copy), reads=[pb], writes=[vstb[u]])
                            else:
                                fw.op("dve", lambda e: e.tensor_copy(vst[u][0:nb_, 512:1024], pa[0:nb_, 0:512]), reads=[pb], writes=[vstb[u]])
                        fw.dma(VS[s, t0 + b0:t0 + b0 + nb_, :], vst[u][0:nb_, :], reads=[vstb[u]], writes=[Buf()])
        fw.barrier()

    def attn_phase():
        scale = QKH ** -0.5
        NKT = -(-T // 128)
        qtiles = [(q0, min(512, T - q0)) for q0 in range(0, T, 512)]
        with ExitStack() as es:
            qT = [es.enter_context(sbt("at_q%d" % i, [QKH, T], BF16)) for i in range(2)]
            kT = [es.enter_context(sbt("at_k%d" % i, [QKH, T], BF16)) for i in range(2)]
            va = [es.enter_context(sbt("at_v%d" % i, [128, NKT, VH + 1], BF16)) for i in range(2)]
            ib = [Buf() for _ in range(2)]
            ibq, ibq2, ibk, ibk2, ibv2 = (bufs(2) for _ in range(5))
            for i in range(2):
                fw.op("pool", lambda e: e.memset(va[i][:], 1.0), writes=[ib[i], ibv2[i]])
            ot = [es.enter_context(sbt("at_o%d" % i, [64, T], BF16)) for i in range(2)]
            otb = bufs(2)
            NP = 6
            LOOK = 3
            EDEF = 9
            pT = [es.enter_context(sbt("at_pp%d" % i, [128, 512], BF16)) for i in range(NP)]
            pTb = bufs(NP)
            rl = [es.enter_context(sbt("at_rll%d" % i, [128, 512], F32)) for i in range(3)]
            rlb = bufs(3)
            rb_ = [es.enter_context(sbt("at_rbb%d" % i, [64, 512], F32)) for i in range(3)]
            rbb = bufs(3)
            k = 0
            st = {"pi": 0, "ui": 0, "oi": 0}
            pstate["lo"] = 2
            pend = []

            def tick():
                for p_ in pend:
                    p_[0] -= 1
                while pend and pend[0][0] <= 0:
                    pend.pop(0)[1]()

            def flush():
                while pend:
                    pend.pop(0)[1]()

            heads = [(s, h) for s in range(NSEQ) for h in range(NH)]

            def emit_loads(idx):
                s, h = heads[idx]
                z = idx % 2
                fw.dma(qT[z][0:QKN, :], QS[s, h * QKN:(h + 1) * QKN, :], reads=[QKVB[s]], writes=[ibq[z]])
                fw.dma(qT[z][QKN:QKH, :], QR[s, h * QKR:(h + 1) * QKR, :], reads=[QKVB[s]], writes=[ibq2[z]])
                fw.dma(kT[z][0:QKN, :], KS[s, h * QKN:(h + 1) * QKN, :], reads=[QKVB[s]], writes=[ibk[z]])
                fw.dma(kT[z][QKN:QKH, :], KR[s, :, :], reads=[QKVB[s]], writes=[ibk2[z]])
                nfull = T // 128
                if nfull:
                    fw.dma(va[z][:, 0:nfull, 0:VH], VS[s, 0:nfull * 128, h * VH:(h + 1) * VH].rearrange("(n p) v -> p n v", p=128),
                           reads=[QKVB[s]], writes=[ib[z]])
                if T % 128:
                    r_ = T % 128
                    fw.dma(va[z][0:r_, nfull, 0:VH], VS[s, nfull * 128:T, h * VH:(h + 1) * VH], reads=[QKVB[s]], writes=[ibv2[z]])

            units = []
            for idx, (s, h) in enumerate(heads):
                for qi_, (q0, nq) in enumerate(qtiles):
                    kts = [kt for kt in range(NKT) if kt * 128 < q0 + nq]
                    bi = st["oi"] % 2
                    st["oi"] += 1
                    for j_, kt in enumerate(kts):
                        units.append(dict(q0=q0, nq=nq, kt=kt, first=(j_ == 0), last=(j_ == len(kts) - 1), bi=bi, idx=idx, z=idx % 2,
                                          head_last=(qi_ == len(qtiles) - 1 and j_ == len(kts) - 1)))

            def emit_s(u_):
                q0, nq, kt, z = u_["q0"], u_["nq"], u_["kt"], u_["z"]
                inb = [ib[z], ibq[z], ibq2[z], ibk[z], ibk2[z], ibv2[z]]
                k0 = kt * 128
                nk = min(128, T - k0)
                off = max(0, k0 - q0)
                n = nq - off
                pa, pb = bank()
                fw.op("pe", lambda e: e.matmul(pa[0:nk, 0:n], kT[z][:, k0:k0 + nk], qT[z][:, q0 + off:q0 + nq], start=True, stop=True),
                      reads=inb, writes=[pb])
                u = st["pi"] % NP
                st["pi"] += 1
                fw.op("act", lambda e: e.activation(out=pT[u][0:nk, 0:n], in_=pa[0:nk, 0:n], func=AF.Exp, scale=scale),
                      reads=[pb], writes=[pTb[u]])
                if k0 + nk - 1 > q0 + off:
                    nd = min(nk, n)
                    fw.op("pool", lambda e: e.tensor_tensor(out=pT[u][0:nk, 0:nd], in0=pT[u][0:nk, 0:nd], in1=tri[0:nk, 0:nd], op=ALU.mult),
                          reads=[pTb[u], cb], writes=[pTb[u]])
                u_.update(u=u, nk=nk, off=off, n=n)

            def emit_pv(u_):
                q0, nq, kt, u, nk, off, n, z = (u_[x] for x in ("q0", "nq", "kt", "u", "nk", "off", "n", "z"))
                inb = [ib[z], ibq[z], ibq2[z], ibk[z], ibk2[z], ibv2[z]]
                po, pob = ps[:, u_["bi"], :], psb[u_["bi"]]
                if u_["first"]:
                    for p_ in [p_ for p_ in pend if p_[2] == u_["bi"]]:
                        pend.remove(p_)
                        p_[1]()
                fw.op("pe", lambda e: e.matmul(po[0:VH + 1, off:nq], va[z][0:nk, kt, :], pT[u][0:nk, 0:n], start=u_["first"], stop=u_["last"]),
                      reads=[pTb[u]] + inb, writes=[pob])
                if u_["last"]:
                    w_ = st["ui"] % 3
                    st["ui"] += 1
                    fw.op("dve", lambda e: e.reciprocal(out=rl[w_][64:65, 0:nq], in_=po[64:65, 0:nq]), reads=[pob], writes=[rlb[w_]])
                    hl = u_["head_last"]
                    s_, h_ = heads[u_["idx"]]

                    def epi(w_=w_, po=po, pob=pob, q0=q0, nq=nq, zz=z, hl=hl, s_=s_, h_=h_):
                        pa, pb = bank()
                        fw.op("pe", lambda e: e.matmul(pa[0:64, 0:nq], ones_f[64:65, 0:64], rl[w_][64:65, 0:nq], start=True, stop=True),
                              reads=[rlb[w_], cb], writes=[pb])
                        fw.op("act", lambda e: e.activation(out=rb_[w_][:, 0:nq], in_=pa[0:64, 0:nq], func=AF.Copy), reads=[pb], writes=[rbb[w_]])
                        fw.op("dve", lambda e: e.tensor_tensor(out=ot[zz][:, q0:q0 + nq], in0=po[0:64, 0:nq], in1=rb_[w_][:, 0:nq], op=ALU.mult),
                              reads=[pob, rbb[w_]], writes=[otb[zz]])
                        if hl:
                            fw.dma(OS[s_, h_ * VH:(h_ + 1) * VH, :], ot[zz][:, :], reads=[otb[zz]], writes=[Buf()], q="pool")
                    pend.append([EDEF, epi, u_["bi"]])
                if u_["head_last"] and u_["idx"] + 2 < len(heads):
                    emit_loads(u_["idx"] + 2)

            emit_loads(0)
            if len(heads) > 1:
                emit_loads(1)
            nU = len(units)
            for i in range(nU + LOOK):
                if i < nU:
                    emit_s(units[i])
                if i >= LOOK:
                    emit_pv(units[i - LOOK])
                    tick()
            flush()
        pstate["lo"] = 0
        fw.barrier()

    def odd_out_phase(j, src, srcb, dst, dstb):
        with ExitStack() as es:
            wout = es.enter_context(sbt("oo_wout", [128, 8, D], BF16))
            woutb = []
            for c in range(8):
                load_w_cast(wout[:, c, :], od_w_out[j, c * 128:(c + 1) * 128, :], woutb)
            S_ = SM
            xt = [es.enter_context(sbt("oo_xt%d" % i, [128, 8, S_], F32)) for i in range(2)]
            xtb = [bufs(8) for _ in range(2)]
            o_t = [es.enter_context(sbt("oo_o%d" % i, [128, 8, S_], BF16)) for i in range(2)]
            otb = [Buf() for _ in range(2)]
            k = 0
            for s in range(NSEQ):
                for ti in range(NT):
                    t0, S = tiles[ti]
                    z = k % 2
                    k += 1
                    for hf in range(2):
                        c0 = hf * 4
                        fw.dma(xt[z][:, c0:c0 + 4, 0:S], src[s, c0 * 128:(c0 + 4) * 128, t0:t0 + S].rearrange("(c p) t -> p c t", p=128),
                               reads=[srcb[s][ti]], writes=xtb[z][c0:c0 + 4])
                    fw.dma(o_t[z][:, :, 0:S], OS[s, :, t0:t0 + S].rearrange("(c p) t -> p c t", p=128), reads=[OSB[s]], writes=[otb[z]])
                    for m in range(8):
                        pa, pb = bank()
                        for c in range(8):
                            fw.op("pe", lambda e: e.matmul(pa[:, 0:S], wout[:, c, m * 128:(m + 1) * 128], o_t[z][:, c, 0:S], start=(c == 0), stop=(c == 7)),
                                  reads=[otb[z]] + woutb, writes=[pb])
                        fw.op("dve", lambda e: e.tensor_tensor(out=xt[z][:, m, 0:S], in0=xt[z][:, m, 0:S], in1=pa[:, 0:S], op=ALU.add),
                              reads=[pb, xtb[z][m]], writes=[xtb[z][m]])
                    for hf in range(2):
                        c0 = hf * 4
                        fw.dma(dst[s, c0 * 128:(c0 + 4) * 128, t0:t0 + S].rearrange("(c p) t -> p c t", p=128), xt[z][:, c0:c0 + 4, 0:S],
                               reads=xtb[z][c0:c0 + 4], writes=[dstb[s][ti]], q="pool")
        fw.barrier()

    def final_phase(src, srcb):
        outb = Buf()
        with ExitStack() as es:
            g_t, g_b = load_vec(es, "fin_g", final_norm, 8)
            xt = [es.enter_context(sbt("fi_xt%d" % i, [128, 8, 512], F32)) for i in range(2)]
            xtb = [bufs(8) for _ in range(2)]
            sq = es.enter_context(sbt("fi_sq", [128, 8, 512], BF16))
            sqb = Buf()
            rs = [es.enter_context(sbt("fi_rs%d" % i, [128, 512], F32)) for i in range(2)]
            rsb = bufs(2)
            yo = [es.enter_context(sbt("fi_yo%d" % i, [128, D], F32)) for i in range(3)]
            yob = bufs(3)
            allsrc = lambda s: list(srcb[s])
            k = 0
            yi = 0
            for s in range(NSEQ):
                for t0 in range(0, SEQ, 512):
                    n = min(512, SEQ - t0)
                    z = k % 2
                    k += 1
                    for hf in range(2):
                        c0 = hf * 4
                        fw.dma(xt[z][:, c0:c0 + 4, 0:n], src[s, c0 * 128:(c0 + 4) * 128, NMETA + t0:NMETA + t0 + n].rearrange("(c p) t -> p c t", p=128),
                               reads=allsrc(s), writes=xtb[z][c0:c0 + 4])
                    for hf in range(2):
                        c0 = hf * 4
                        fw.op("act", lambda e: e.activation(out=sq[:, c0:c0 + 4, 0:n], in_=xt[z][:, c0:c0 + 4, 0:n], func=AF.Square),
                              reads=xtb[z][c0:c0 + 4], writes=[sqb])
                    pa, pb = bank()
                    for c in range(8):
                        fw.op("pe", lambda e: e.matmul(pa[:, 0:n], ones_bf[:], sq[:, c, 0:n], start=(c == 0), stop=(c == 7)), reads=[sqb, cb], writes=[pb])
                    fw.op("act", lambda e: e.activation(out=rs[z][:, 0:n], in_=pa[:, 0:n], func=AF.Sqrt, scale=1.0 / D, bias=EPS), reads=[pb], writes=[rsb[z]])
                    fw.op("dve", lambda e: e.reciprocal(out=rs[z][:, 0:n], in_=rs[z][:, 0:n]), reads=[rsb[z]], writes=[rsb[z]])
                    for c in range(8):
                        fw.op("dve", lambda e: e.scalar_tensor_tensor(out=xt[z][:, c, 0:n], in0=xt[z][:, c, 0:n], scalar=g_t[:, c:c + 1], in1=rs[z][:, 0:n],
                                                                      op0=ALU.mult, op1=ALU.mult),
                              reads=[xtb[z][c], rsb[z], g_b], writes=[xtb[z][c]])
                    for b0 in range(0, n, 128):
                        u = yi % 3
                        yi += 1
                        for hv in range(2):
                            pa, pb = bank()
                            for cc in range(4):
                                c = hv * 4 + cc
                                fw.op("pe", lambda e: e.transpose(pa[:, cc * 128:(cc + 1) * 128], xt[z][:, c, b0:b0 + 128], ident[:]),
                                      reads=[xtb[z][c], cb], writes=[pb])
                            if hv == 0:
                                fw.op("act", lambda e: e.activation(out=yo[u][:, 0:512], in_=pa[:, 0:512], func=AF.Copy), reads=[pb], writes=[yob[u]])
                            else:
                                fw.op("dve", lambda e: e.tensor_copy(yo[u][:, 512:1024], pa[:, 0:512]), reads=[pb], writes=[yob[u]])
                        fw.dma(out_ap[s, t0 + b0:t0 + b0 + 128, :], yo[u][:, :], reads=[yob[u]], writes=[Buf()])
        fw.barrier()

    prologue(XP[0], XB[0])
    cur = 0
    for layer in range(DEPTH):
        j = layer // 2
        a, b = cur, 1 - cur
        if layer % 2 == 0:
            even_phase(j, XP[a], XB[a], XP[b], XB[b])
        else:
            odd_proj_phase(j, XP[a], XB[a])
            if dbg != 1:
                attn_phase()
            if dbg == 0:
                odd_out_phase(j, XP[a], XB[a], XP[b], XB[b])
            else:
                a, b = b, a
                cur = 1 - cur
        ffn_phase(layer, 0, XP[b], XB[b], XP[a], XB[a])
        ffn_phase(layer, 1, XP[b], XB[b], XP[a], XB[a])
    final_phase(XP[cur], XB[cur])
    return nc, fw


def rope_tables(T):
    pos = np.arange(T, dtype=np.float32)
    inv_freq = (np.float32(10000.0) ** (-np.arange(0, QKR, 2, dtype=np.float32) / np.float32(QKR))).astype(np.float32)
    ang = (pos[None, :] * inv_freq[:, None]).astype(np.float32)
    cos = np.cos(ang).astype(np.float32)
    sin = np.sin(ang).astype(np.float32)
    tab = np.zeros((128, 2, T), np.float32)
    for p in range(128):
        f = p % 16
        tab[p, 0] = cos[f]
        tab[p, 1] = -sin[f] if (p % 32) < 16 else sin[f]
    return tab


_WEIGHT_NAMES = ["meta_tokens", "ev_norm", "ev_w_in", "ev_conv_a", "ev_conv_b", "ev_conv_b_bias", "ev_gate_r_w", "ev_gate_r_b",
                 "ev_gate_i_w", "ev_gate_i_b", "ev_lru_lambda", "ev_w_out", "od_norm", "od_w_in", "od_q_norm", "od_kv_norm",
                 "od_w_uq", "od_w_ukv", "od_w_out", "ffn_norm", "ffn_w_up", "ffn_conv_w", "ffn_conv_b", "ffn_w_down", "final_norm"]


def run(inputs, ncores=NCORES, depth=4, same_engine_sync=True, trace=False, dbg=0):
    x = np.ascontiguousarray(np.asarray(inputs["x"], dtype=np.float32))
    B, SEQ, _ = x.shape
    nseq = B // ncores
    nc, fw = build_program(NSEQ=nseq, SEQ=SEQ, DEPTH=depth, same_engine_sync=same_engine_sync, dbg=dbg)
    nod = depth // 2
    shared = {}
    for n in _WEIGHT_NAMES:
        if n.startswith("od_") and nod == 0:
            continue
        shared[n] = np.ascontiguousarray(np.asarray(inputs[n], dtype=np.float32))
    if nod:
        shared["rope_cs"] = rope_tables(SEQ + NMETA)
    in_maps = []
    for c in range(ncores):
        m = dict(shared)
        m["x"] = x[c * nseq:(c + 1) * nseq]
        in_maps.append(m)
    res = run_bass_kernel_spmd(nc, in_maps, core_ids=list(range(ncores)), trace=trace)
    out = np.concatenate([r["out"] for r in res.results], axis=0)
    return out, res


def kernel(**inputs):
    out, _ = run(inputs)
    return out.astype(np.float32)
```

```python
import numpy as np
from contextlib import ExitStack
import concourse.bass as bass
import concourse.mybir as mybir
from concourse.bass_utils import run_bass_kernel_spmd

F32 = mybir.dt.float32
BF16 = mybir.dt.bfloat16
AF = mybir.ActivationFunctionType
ALU = mybir.AluOpType

D = 1024
NMETA = 16
EPS = 1e-6
CW = 512
EVEN_IN = 2560
NH = 16
QKN, QKR, QKH, VH = 64, 32, 96, 64
QL, KVL = 384, 256
ODD_IN = QL + KVL + QKR
DFF = 2816
NCORES = 8


class Buf:
    __slots__ = ("lw", "rd", "excl")

    def __init__(self, excl=False):
        self.lw = None
        self.rd = {}
        self.excl = excl


def bufs(n):
    return [Buf() for _ in range(n)]


class FW:
    ENGS = ("pe", "dve", "act", "pool", "sp")

    def __init__(self, nc, n_dma_sems=32, same_engine_sync=True):
        self.nc = nc
        self.eng = {"pe": nc.tensor, "dve": nc.vector, "act": nc.scalar, "pool": nc.gpsimd, "sp": nc.sync}
        self.sems = {}
        self.cnt = {}
        for e in self.ENGS:
            self.sems[e] = nc.alloc_semaphore("clk_" + e)
            self.cnt[e] = 0
        self.dma_sems = []
        for i in range(n_dma_sems):
            k = "dma%d" % i
            self.sems[k] = nc.alloc_semaphore("clk_" + k)
            self.cnt[k] = 0
            self.dma_sems.append(k)
        self.dma_rr = 0
        self.seen = {e: {} for e in self.ENGS}
        self.same_engine_sync = same_engine_sync
        self.n_inst = 0

    def _need(self, e, key, val, needs):
        if key == e and (e == "pe" or not self.same_engine_sync):
            return
        if self.seen[e].get(key, 0) >= val:
            return
        if needs.get(key, 0) < val:
            needs[key] = val

    def _deps(self, e, reads, writes):
        needs = {}
        for b in reads:
            if b.lw is not None:
                self._need(e, b.lw[0], b.lw[1], needs)
            if b.excl:
                for k, v in b.rd.items():
                    if k != e:
                        self._need(e, k, v, needs)
        for b in writes:
            if b.lw is not None:
                self._need(e, b.lw[0], b.lw[1], needs)
            for k, v in b.rd.items():
                self._need(e, k, v, needs)
        return needs

    def _emit_waits(self, e, needs):
        eng = self.eng[e]
        for k, v in needs.items():
            eng.wait_ge(self.sems[k], v)
            self.seen[e][k] = v

    def _mark(self, key, val, reads, writes):
        for b in reads:
            if b.rd.get(key, 0) < val:
                b.rd[key] = val
        for b in writes:
            b.lw = (key, val)
            b.rd = {}

    def op(self, e, fn, reads=(), writes=()):
        needs = self._deps(e, reads, writes)
        self._emit_waits(e, needs)
        inst = fn(self.eng[e])
        self.cnt[e] += 1
        inst.then_inc(self.sems[e], 1)
        self._mark(e, self.cnt[e], reads, writes)
        self.n_inst += 1
        return inst

    def dma(self, out, in_, reads=(), writes=(), q="sp", **kw):
        key = self.dma_sems[self.dma_rr]
        self.dma_rr = (self.dma_rr + 1) % len(self.dma_sems)
        needs = self._deps(q, reads, writes)
        if self.cnt[key] > 0:
            self._need(q, key, self.cnt[key], needs)
        self._emit_waits(q, needs)
        inst = self.eng[q].dma_start(out=out, in_=in_, **kw)
        self.cnt[key] += 16
        inst.then_inc(self.sems[key], 16)
        self._mark(key, self.cnt[key], reads, writes)
        self.n_inst += 1
        return inst

    def barrier(self):
        for e in self.ENGS:
            needs = {}
            for k, v in self.cnt.items():
                if v > 0 and k != e:
                    self._need(e, k, v, needs)
            self._emit_waits(e, needs)


def make_tiles(T, smax=509):
    n = -(-T // smax)
    b = [round(i * T / n) for i in range(n + 1)]
    return [(b[i], b[i + 1] - b[i]) for i in range(n)]


def build_program(NSEQ=2, SEQ=4096, DEPTH=4, same_engine_sync=True, dbg=0):
    T = SEQ + NMETA
    NEV = (DEPTH + 1) // 2
    NOD = DEPTH // 2
    nc = bass.Bass("TRN2", target_bir_lowering=False)
    fw = FW(nc, same_engine_sync=same_engine_sync)

    uid = {"n": 0}

    def sbt(name, shape, dt):
        uid["n"] += 1
        return nc.sbuf_tensor("%s_u%d" % (name, uid["n"]), shape, dt)

    def din(name, shape):
        return nc.dram_tensor(name, list(shape), F32, kind="ExternalInput").ap()

    x_in = din("x", (NSEQ, SEQ, D))
    meta_in = din("meta_tokens", (NMETA, D))
    ev_norm = din("ev_norm", (NEV, D))
    ev_w_in = din("ev_w_in", (NEV, D, EVEN_IN))
    ev_conv_a = din("ev_conv_a", (NEV, 3, CW))
    ev_conv_b = din("ev_conv_b", (NEV, 4, CW))
    ev_conv_b_bias = din("ev_conv_b_bias", (NEV, CW))
    ev_gate_r_w = din("ev_gate_r_w", (NEV, 8, 64, 64))
    ev_gate_r_b = din("ev_gate_r_b", (NEV, CW))
    ev_gate_i_w = din("ev_gate_i_w", (NEV, 8, 64, 64))
    ev_gate_i_b = din("ev_gate_i_b", (NEV, CW))
    ev_lam = din("ev_lru_lambda", (NEV, CW))
    ev_w_out = din("ev_w_out", (NEV, D, D))
    if NOD:
        od_norm = din("od_norm", (NOD, D))
        od_w_in = din("od_w_in", (NOD, D, ODD_IN))
        od_q_norm = din("od_q_norm", (NOD, QL))
        od_kv_norm = din("od_kv_norm", (NOD, KVL))
        od_w_uq = din("od_w_uq", (NOD, QL, NH * QKH))
        od_w_ukv = din("od_w_ukv", (NOD, KVL, NH * (QKN + VH)))
        od_w_out = din("od_w_out", (NOD, D, D))
        rope_in = din("rope_cs", (128, 2, T))
    ffn_norm = din("ffn_norm", (DEPTH, D))
    ffn_w_up = din("ffn_w_up", (DEPTH, D, 2 * DFF))
    ffn_conv_w = din("ffn_conv_w", (DEPTH, 3, 2 * DFF))
    ffn_conv_b = din("ffn_conv_b", (DEPTH, 2 * DFF))
    ffn_w_down = din("ffn_w_down", (DEPTH, DFF, D))
    final_norm = din("final_norm", (D,))
    out_ap = nc.dram_tensor("out", [NSEQ, SEQ, D], F32, kind="ExternalOutput").ap()

    XP = [nc.dram_tensor("xres%d" % i, [NSEQ, D, T], F32, kind="Internal").ap() for i in range(2)]
    tiles = make_tiles(T)
    NT = len(tiles)
    SM = max(S for _, S in tiles)
    XB = [[bufs(NT) for _ in range(NSEQ)] for _ in range(2)]
    if NOD:
        QS = nc.dram_tensor("qs", [NSEQ, NH * QKN, T], BF16, kind="Internal").ap()
        QR = nc.dram_tensor("qr", [NSEQ, NH * QKR, T], BF16, kind="Internal").ap()
        KS = nc.dram_tensor("ks", [NSEQ, NH * QKN, T], BF16, kind="Internal").ap()
        KR = nc.dram_tensor("kr", [NSEQ, QKR, T], BF16, kind="Internal").ap()
        VS = nc.dram_tensor("vs", [NSEQ, T, NH * VH], BF16, kind="Internal").ap()
        OS = nc.dram_tensor("os", [NSEQ, D, T], BF16, kind="Internal").ap()
        QKVB = [Buf() for _ in range(NSEQ)]
        OSB = [Buf() for _ in range(NSEQ)]

    ps = nc.alloc_psum_tensor("ps", [128, 8, 512], F32)
    psb = [Buf(excl=True) for _ in range(8)]
    pstate = {"i": 0}

    def bank():
        lo = pstate.get("lo", 0)
        i = pstate["i"]
        if i < lo:
            i = lo
        pstate["i"] = i + 1 if i + 1 < 8 else lo
        return ps[:, i, :], psb[i]

    ones_bf = nc.alloc_sbuf_tensor("ones_bf", [128, 128], BF16)
    ones_f = nc.alloc_sbuf_tensor("ones_f", [128, 64], F32)
    ident = nc.alloc_sbuf_tensor("ident", [128, 128], F32)
    tri = nc.alloc_sbuf_tensor("tri", [128, 128], BF16)
    cb = Buf()
    fw.op("dve", lambda e: e.memset(ones_bf[:], 1.0), writes=[cb])
    fw.op("dve", lambda e: e.memset(ones_f[:], 1.0), writes=[cb])
    fw.op("dve", lambda e: e.memset(ident[:], 1.0), writes=[cb])
    fw.op("dve", lambda e: e.memset(tri[:], 1.0), writes=[cb])
    fw.op("pool", lambda e: e.affine_select(out=ident[:], in_=ident[:], pattern=[[-1, 128]], compare_op=ALU.is_equal,
                                            fill=0.0, base=0, channel_multiplier=1), reads=[cb], writes=[cb])
    fw.op("pool", lambda e: e.affine_select(out=tri[:], in_=tri[:], pattern=[[1, 128]], compare_op=ALU.is_ge,
                                            fill=0.0, base=0, channel_multiplier=-1), reads=[cb], writes=[cb])

    def load_vec(es, name, src, nchunk):
        t = es.enter_context(sbt(name, [128, nchunk], F32))
        b = Buf()
        fw.dma(t[:], src.rearrange("(c p) -> p c", p=128), writes=[b], allow_slow_non_contiguous=True)
        return t, b

    class StageA:
        def __init__(self, es, H, nslots=2):
            self.H = H
            W = SM + H
            self.W = W
            self.n = nslots
            self.nx = nslots + 1
            self.xt = [es.enter_context(sbt("xt%d" % i, [128, 8, W], F32)) for i in range(self.nx)]
            self.xtb = [bufs(8) for _ in range(self.nx)]
            self.hb = [es.enter_context(sbt("hb%d" % i, [128, 8, W], BF16)) for i in range(nslots)]
            self.hbb = [Buf() for _ in range(nslots)]
            self.sq = es.enter_context(sbt("sq", [128, 8, W], BF16))
            self.sqb = Buf()
            self.rs = [es.enter_context(sbt("rs%d" % i, [128, W], F32)) for i in range(nslots)]
            self.rsb = [Buf() for _ in range(nslots)]
            self.k = 0
            self.kx = 0
            self.pref = {}

        def prefetch(self, src, srcb, s, ti):
            t0, S = tiles[ti]
            H = self.H
            sx = self.kx % self.nx
            self.kx += 1
            xt, xtb = self.xt[sx], self.xtb[sx]
            W = S + H
            h0 = min(H, t0)
            if h0 < H:
                fw.op("pool", lambda e: e.memset(xt[:, :, 0:H - h0], 0.0), writes=xtb)
            rb = [srcb[s][ti]] + ([srcb[s][ti - 1]] if (h0 and ti > 0) else [])
            for half in range(2):
                c0 = half * 4
                fw.dma(xt[:, c0:c0 + 4, H - h0:W],
                       src[s, c0 * 128:(c0 + 4) * 128, t0 - h0:t0 + S].rearrange("(c p) t -> p c t", p=128),
                       reads=rb, writes=xtb[c0:c0 + 4])
            self.pref[(s, ti)] = sx

        def run(self, src, srcb, s, ti, gvec, gb, make_h=True, out_f32=None):
            t0, S = tiles[ti]
            H = self.H
            if (s, ti) not in self.pref:
                self.prefetch(src, srcb, s, ti)
            sx = self.pref.pop((s, ti))
            sl = self.k % self.n
            self.k += 1
            xt, xtb = self.xt[sx], self.xtb[sx]
            W = S + H
            for half in range(2):
                c0 = half * 4
                fw.op("act", lambda e: e.activation(out=self.sq[:, c0:c0 + 4, 0:W], in_=xt[:, c0:c0 + 4, 0:W], func=AF.Square),
                      reads=xtb[c0:c0 + 4], writes=[self.sqb])
            pa, pb = bank()
            for c in range(8):
                fw.op("pe", lambda e: e.matmul(pa[:, 0:W], ones_bf[:], self.sq[:, c, 0:W], start=(c == 0), stop=(c == 7)),
                      reads=[self.sqb, cb], writes=[pb])
            rs, rsb = self.rs[sl], self.rsb[sl]
            fw.op("act", lambda e: e.activation(out=rs[:, 0:W], in_=pa[:, 0:W], func=AF.Sqrt, scale=1.0 / D, bias=EPS),
                  reads=[pb], writes=[rsb])
            fw.op("dve", lambda e: e.reciprocal(out=rs[:, 0:W], in_=rs[:, 0:W]), reads=[rsb], writes=[rsb])
            if make_h:
                hb, hbb = self.hb[sl], self.hbb[sl]
                for c in range(8):
                    fw.op("dve", lambda e: e.scalar_tensor_tensor(out=hb[:, c, 0:W], in0=xt[:, c, 0:W], scalar=gvec[:, c:c + 1],
                                                                  in1=rs[:, 0:W], op0=ALU.mult, op1=ALU.mult),
                          reads=[xtb[c], rsb, gb], writes=[hbb])
            return sl, sx

    wl_hist = []

    def load_w_cast(dst, src, blist, rows_per=None):
        b = Buf()
        gate = [wl_hist[-4]] if len(wl_hist) >= 4 else []
        wl_hist.append(b)
        blist.append(b)
        fw.dma(dst, src, reads=gate, writes=[b], q="pool")

    def prologue(dst, dstb):
        with ExitStack() as es:
            xin = [es.enter_context(sbt("pxin%d" % i, [128, 4, D], F32)) for i in range(2)]
            xinb = [Buf() for _ in range(2)]
            st = [es.enter_context(sbt("pst%d" % i, [128, 8, 512], F32)) for i in range(2)]
            stb = [Buf() for _ in range(2)]
            mt = es.enter_context(sbt("pmeta", [16, D], F32))
            mtb = Buf()
            fw.dma(mt[:], meta_in[:, :], writes=[mtb])
            allb = [b for s in range(NSEQ) for b in dstb[s]]
            pa, pb = bank()
            for c in range(8):
                fw.op("pe", lambda e: e.transpose(pa[:, c * 16:(c + 1) * 16], mt[:, c * 128:(c + 1) * 128], ident[0:16, 0:16]),
                      reads=[mtb, cb], writes=[pb])
            fw.op("dve", lambda e: e.tensor_copy(st[0][:, :, 0:16], pa[:, 0:128].rearrange("p (c t) -> p c t", c=8)),
                  reads=[pb], writes=[stb[0]])
            for s in range(NSEQ):
                fw.dma(dst[s, :, 0:16].rearrange("(c p) t -> p c t", p=128), st[0][:, :, 0:16], reads=[stb[0]], writes=[Buf()])
            k = 0
            for s in range(NSEQ):
                for t0 in range(0, SEQ, 512):
                    n = min(512, SEQ - t0)
                    nb = n // 128
                    sl = k % 2
                    k += 1
                    fw.dma(xin[sl][:, 0:nb, :], x_in[s, t0:t0 + n, :].rearrange("(b p) d -> p b d", p=128), writes=[xinb[sl]])
                    for c in range(8):
                        pa, pb = bank()
                        for b_ in range(nb):
                            fw.op("pe", lambda e: e.transpose(pa[:, b_ * 128:(b_ + 1) * 128], xin[sl][:, b_, c * 128:(c + 1) * 128], ident[:]),
                                  reads=[xinb[sl], cb], writes=[pb])
                        eng = "dve" if c % 2 == 0 else "act"
                        if eng == "dve":
                            fw.op("dve", lambda e: e.tensor_copy(st[sl][:, c, 0:n], pa[:, 0:n]), reads=[pb], writes=[stb[sl]])
                        else:
                            fw.op("act", lambda e: e.activation(out=st[sl][:, c, 0:n], in_=pa[:, 0:n], func=AF.Copy), reads=[pb], writes=[stb[sl]])
                    fw.dma(dst[s, :, NMETA + t0:NMETA + t0 + n].rearrange("(c p) t -> p c t", p=128), st[sl][:, :, 0:n],
                           reads=[stb[sl]], writes=[Buf()])
        fw.barrier()

    def even_phase(j, src, srcb, dst, dstb):
        H = 3
        with ExitStack() as es:
            A = StageA(es, H)
            g_t, g_b = load_vec(es, "ev_g", ev_norm[j], 8)
            win = es.enter_context(sbt("ev_win", [128, 8, EVEN_IN], BF16))
            winb = []
            for c in range(8):
                load_w_cast(win[:, c, :], ev_w_in[j, c * 128:(c + 1) * 128, :], winb)
            wout = es.enter_context(sbt("ev_wout", [128, 8, D], BF16))
            woutb = []
            for c in range(8):
                load_w_cast(wout[:, c, :], ev_w_out[j, c * 128:(c + 1) * 128, :], woutb)
            wg = es.enter_context(sbt("ev_wg", [128, 2, 4, 128], BF16))
            wgb = Buf()
            fw.op("pool", lambda e: e.memset(wg[:], 0.0), writes=[wgb])
            for gi, gw in enumerate((ev_gate_r_w, ev_gate_i_w)):
                for c in range(4):
                    for hh in range(2):
                        fw.dma(wg[hh * 64:(hh + 1) * 64, gi, c, hh * 64:(hh + 1) * 64], gw[j, 2 * c + hh, :, :], writes=[wgb], q="pool")
            cva = es.enter_context(sbt("ev_cva", [128, 4, 3], F32))
            cvb = es.enter_context(sbt("ev_cvb", [128, 4, 4], F32))
            vb = Buf()
            for kk in range(3):
                fw.dma(cva[:, :, kk], ev_conv_a[j, kk].rearrange("(c p) -> p c", p=128), writes=[vb], allow_slow_non_contiguous=True)
            for kk in range(4):
                fw.dma(cvb[:, :, kk], ev_conv_b[j, kk].rearrange("(c p) -> p c", p=128), writes=[vb], allow_slow_non_contiguous=True)
            bbias, _b1 = load_vec(es, "ev_bb", ev_conv_b_bias[j], 4)
            rbias, _b2 = load_vec(es, "ev_rb", ev_gate_r_b[j], 4)
            ibias, _b3 = load_vec(es, "ev_ib", ev_gate_i_b[j], 4)
            lam, _b4 = load_vec(es, "ev_lam", ev_lam[j], 4)
            tv = [es.enter_context(sbt("ev_tv%d" % i, [128, 4], F32)) for i in range(5)]
            ca1 = es.enter_context(sbt("ev_ca1", [128, 4], F32))
            ca2 = es.enter_context(sbt("ev_ca2", [128, 4], F32))
            cab = Buf()
            y_, ay, e_, z_, p_ = tv
            def dv(fn, rd=()):
                fw.op("dve", fn, reads=[cab] + list(rd), writes=[cab])
            dv(lambda e: e.tensor_scalar(out=y_[:], in0=lam[:], scalar1=-1.0, scalar2=None, op0=ALU.mult), [_b4])
            dv(lambda e: e.tensor_tensor(out=ay[:], in0=y_[:], in1=lam[:], op=ALU.max), [_b4])
            fw.op("act", lambda e: e.activation(out=e_[:], in_=ay[:], func=AF.Exp, scale=-1.0), reads=[cab], writes=[cab])
            dv(lambda e: e.tensor_scalar(out=z_[:], in0=e_[:], scalar1=2.0, scalar2=None, op0=ALU.add))
            dv(lambda e: e.reciprocal(out=z_[:], in_=z_[:]))
            dv(lambda e: e.tensor_tensor(out=z_[:], in0=z_[:], in1=e_[:], op=ALU.mult))
            dv(lambda e: e.tensor_tensor(out=e_[:], in0=z_[:], in1=z_[:], op=ALU.mult))
            dv(lambda e: e.memset(p_[:], 1.0 / 15.0))
            for kk in (13, 11, 9, 7, 5, 3, 1):
                dv(lambda e: e.tensor_tensor(out=p_[:], in0=p_[:], in1=e_[:], op=ALU.mult))
                dv(lambda e: e.tensor_scalar(out=p_[:], in0=p_[:], scalar1=1.0 / kk, scalar2=None, op0=ALU.add))
            dv(lambda e: e.tensor_tensor(out=p_[:], in0=p_[:], in1=z_[:], op=ALU.mult))
            dv(lambda e: e.tensor_scalar(out=y_[:], in0=y_[:], scalar1=0.0, scalar2=None, op0=ALU.max))
            dv(lambda e: e.scalar_tensor_tensor(out=y_[:], in0=p_[:], scalar=2.0, in1=y_[:], op0=ALU.mult, op1=ALU.add))
            dv(lambda e: e.tensor_scalar(out=ca1[:], in0=y_[:], scalar1=-8.0, scalar2=None, op0=ALU.mult))
            dv(lambda e: e.tensor_scalar(out=ca2[:], in0=y_[:], scalar1=-16.0, scalar2=None, op0=ALU.mult))
            vecb = [vb, _b1, _b2, _b3, cab]

            WM = SM + H
            def sb1(name, dt=F32, shape=None):
                return es.enter_context(sbt(name, shape or [128, 4, WM], dt))
            gcs = sb1("gcs"); gcsb = bufs(4)
            cvt = sb1("cvat"); cvtb = bufs(4)
            xc = sb1("xc"); xcb = bufs(4)
            xcbf = sb1("xcbf", BF16); xcbfb = bufs(4)
            rr = sb1("rr"); rrb = bufs(4)
            ii = sb1("ii"); iib = bufs(4)
            mm = sb1("mm"); mmb = bufs(4)
            gg = sb1("gg"); ggb = bufs(4)
            hs = sb1("hs"); hsb = bufs(4)
            carry = es.enter_context(sbt("carry", [128, 4], F32)); carb = bufs(4)
            ymix = [es.enter_context(sbt("ymix%d" % i, [128, 8, SM], BF16)) for i in range(2)]
            ymb = [bufs(8) for _ in range(2)]

            order = [(s, ti) for s in range(NSEQ) for ti in range(NT)]
            for o_ in order[0:2]:
                A.prefetch(src, srcb, o_[0], o_[1])
            slot_next = A.run(src, srcb, order[0][0], order[0][1], g_t, g_b)
            for oi, (s, ti) in enumerate(order):
                t0, S = tiles[ti]
                W = S + H
                sl = slot_next
                if oi + 2 < len(order):
                    A.prefetch(src, srcb, order[oi + 2][0], order[oi + 2][1])
                if oi + 1 < len(order):
                    slot_next = A.run(src, srcb, order[oi + 1][0], order[oi + 1][1], g_t, g_b)
                hb, hbb, xt, xtb = A.hb[sl[0]], A.hbb[sl[0]], A.xt[sl[1]], A.xtb[sl[1]]
                z = oi % 2

                def proj(col0):
                    pa, pb = bank()
                    for c in range(8):
                        fw.op("pe", lambda e: e.matmul(pa[:, 0:W], win[:, c, col0:col0 + 128], hb[:, c, 0:W], start=(c == 0), stop=(c == 7)),
                              reads=[hbb] + winb, writes=[pb])
                    return pa, pb

                for c in range(4):
                    pa, pb = proj(2048 + c * 128)
                    fw.op("act", lambda e: e.activation(out=gg[:, c, 0:S], in_=pa[:, H:W], func=AF.Gelu_apprx_tanh),
                          reads=[pb], writes=[ggb[c]])
                for c in range(4):
                    pa, pb = proj(512 + c * 128)
                    fw.op("act", lambda e: e.activation(out=gcs[:, c, 0:W], in_=pa[:, 0:W], func=AF.Copy), reads=[pb], writes=[gcsb[c]])
                    pa, pb = proj(1024 + c * 128)
                    fw.op("dve", lambda e: e.tensor_tensor(out=gcs[:, c, 0:W], in0=pa[:, 0:W], in1=gcs[:, c, 0:W], op=ALU.mult),
                          reads=[pb, gcsb[c]], writes=[gcsb[c]])
                    fw.op("act", lambda e: e.activation(out=cvt[:, c, 0:S], in_=gcs[:, c, H:W], func=AF.Identity, scale=cva[:, c, 2:3]),
                          reads=[gcsb[c]] + vecb, writes=[cvtb[c]])
                    for kk in (1, 0):
                        sh = 2 - kk
                        fw.op("dve", lambda e: e.scalar_tensor_tensor(out=cvt[:, c, 0:S], in0=gcs[:, c, H - sh:W - sh], scalar=cva[:, c, kk:kk + 1],
                                                                      in1=cvt[:, c, 0:S], op0=ALU.mult, op1=ALU.add),
                              reads=[gcsb[c], cvtb[c]] + vecb, writes=[cvtb[c]])
                    pa, pb = proj(c * 128)
                    fw.op("dve", lambda e: e.tensor_tensor(out=ymix[z][:, c, 0:S], in0=pa[:, H:W], in1=cvt[:, c, 0:S], op=ALU.mult),
                          reads=[pb, cvtb[c]], writes=[ymb[z][c]])
                for c in range(4):
                    pa, pb = proj(1536 + c * 128)
                    fw.op("act", lambda e: e.activation(out=xc[:, c, 0:S], in_=pa[:, H:W], func=AF.Identity, scale=cvb[:, c, 3:4], bias=bbias[:, c:c + 1]),
                          reads=[pb] + vecb, writes=[xcb[c]])
                    for kk in (2, 1, 0):
                        sh = 3 - kk
                        fw.op("dve", lambda e: e.scalar_tensor_tensor(out=xc[:, c, 0:S], in0=pa[:, H - sh:W - sh], scalar=cvb[:, c, kk:kk + 1],
                                                                      in1=xc[:, c, 0:S], op0=ALU.mult, op1=ALU.add),
                              reads=[pb, xcb[c]] + vecb, writes=[xcb[c]])
                    fw.op("act", lambda e: e.activation(out=xcbf[:, c, 0:S], in_=xc[:, c, 0:S], func=AF.Copy), reads=[xcb[c]], writes=[xcbfb[c]])
                for c in range(4):
                    for gi, (dstt, dstb_, bias_t) in enumerate(((rr, rrb, rbias), (ii, iib, ibias))):
                        pa, pb = bank()
                        fw.op("pe", lambda e: e.matmul(pa[:, 0:S], wg[:, gi, c, :], xcbf[:, c, 0:S], start=True, stop=True),
                              reads=[xcbfb[c], wgb], writes=[pb])
                        fw.op("act", lambda e: e.activation(out=dstt[:, c, 0:S], in_=pa[:, 0:S], func=AF.Sigmoid, bias=bias_t[:, c:c + 1]),
                              reads=[pb] + vecb, writes=[dstb_[c]])
                for c in range(4):
                    fw.op("act", lambda e: e.activation(out=mm[:, c, 0:S], in_=rr[:, c, 0:S], func=AF.Exp, scale=ca2[:, c:c + 1]),
                          reads=[rrb[c]] + vecb, writes=[mmb[c]])
                    fw.op("act", lambda e: e.activation(out=rr[:, c, 0:S], in_=rr[:, c, 0:S], func=AF.Exp, scale=ca1[:, c:c + 1]),
                          reads=[rrb[c]] + vecb, writes=[rrb[c]])
                for c in range(4):
                    fw.op("act", lambda e: e.activation(out=mm[:, c, 0:S], in_=mm[:, c, 0:S], func=AF.Sqrt, scale=-1.0, bias=1.0),
                          reads=[mmb[c]], writes=[mmb[c]])
                for c in range(4):
                    fw.op("dve", lambda e: e.tensor_tensor(out=ii[:, c, 0:S], in0=ii[:, c, 0:S], in1=xc[:, c, 0:S], op=ALU.mult),
                          reads=[iib[c], xcb[c]], writes=[iib[c]])
                for c in range(4):
                    fw.op("dve", lambda e: e.tensor_tensor(out=ii[:, c, 0:S], in0=ii[:, c, 0:S], in1=mm[:, c, 0:S], op=ALU.mult),
                          reads=[iib[c], mmb[c]], writes=[iib[c]])
                for c in range(4):
                    init = 0.0 if ti == 0 else carry[:, c:c + 1]
                    fw.op("dve", lambda e: e.tensor_tensor_scan(out=hs[:, c, 0:S], data0=rr[:, c, 0:S], data1=ii[:, c, 0:S],
                                                                initial=init, op0=ALU.mult, op1=ALU.add),
                          reads=[rrb[c], iib[c], carb[c]], writes=[hsb[c]])
                for c in range(4):
                    fw.op("pool", lambda e: e.tensor_copy(carry[:, c:c + 1], hs[:, c, S - 1:S]), reads=[hsb[c]], writes=[carb[c]])
                    fw.op("dve", lambda e: e.tensor_tensor(out=ymix[z][:, 4 + c, 0:S], in0=gg[:, c, 0:S], in1=hs[:, c, 0:S], op=ALU.mult),
                          reads=[ggb[c], hsb[c]], writes=[ymb[z][4 + c]])
                for m in range(8):
                    pa, pb = bank()
                    for c in range(8):
                        fw.op("pe", lambda e: e.matmul(pa[:, 0:S], wout[:, c, m * 128:(m + 1) * 128], ymix[z][:, c, 0:S], start=(c == 0), stop=(c == 7)),
                              reads=[ymb[z][c]] + woutb, writes=[pb])
                    fw.op("dve", lambda e: e.tensor_tensor(out=xt[:, m, H:W], in0=xt[:, m, H:W], in1=pa[:, 0:S], op=ALU.add),
                          reads=[pb, xtb[m]], writes=[xtb[m]])
                for half in range(2):
                    c0 = half * 4
                    fw.dma(dst[s, c0 * 128:(c0 + 4) * 128, t0:t0 + S].rearrange("(c p) t -> p c t", p=128), xt[:, c0:c0 + 4, H:W],
                           reads=xtb[c0:c0 + 4], writes=[dstb[s][ti]], q="pool")
        fw.barrier()

    def ffn_phase(l, half, src, srcb, dst, dstb):
        H = 2
        NJ = 11
        j0 = half * NJ
        with ExitStack() as es:
            A = StageA(es, H)
            g_t, g_b = load_vec(es, "ff_g", ffn_norm[l], 8)
            wup = es.enter_context(sbt("ff_wup", [128, 8, 2, NJ * 128], BF16))
            JH = ((0, 6), (6, NJ))
            wupb = [[], []]
            for jh, (ja, jb) in enumerate(JH):
                for c in range(8):
                    for ag in range(2):
                        col = ag * DFF + (j0 + ja) * 128
                        load_w_cast(wup[:, c, ag, ja * 128:jb * 128], ffn_w_up[l, c * 128:(c + 1) * 128, col:col + (jb - ja) * 128], wupb[jh])
            wdn = es.enter_context(sbt("ff_wdn", [128, NJ, D], BF16))
            wdnb = []
            for jj in range(NJ):
                load_w_cast(wdn[:, jj, :], ffn_w_down[l, (j0 + jj) * 128:(j0 + jj + 1) * 128, :], wdnb)
            cw = es.enter_context(sbt("ff_cw", [128, 2, NJ, 3], F32))
            cbias = es.enter_context(sbt("ff_cb", [128, 2, NJ], F32))
            vb = Buf()
            for ag in range(2):
                col = ag * DFF + j0 * 128
                for kk in range(3):
                    fw.dma(cw[:, ag, :, kk], ffn_conv_w[l, kk, col:col + NJ * 128].rearrange("(c p) -> p c", p=128), writes=[vb], allow_slow_non_contiguous=True)
                fw.dma(cbias[:, ag, :], ffn_conv_b[l, col:col + NJ * 128].rearrange("(c p) -> p c", p=128), writes=[vb], allow_slow_non_contiguous=True)
            cvt = [[es.enter_context(sbt("ff_cv%d_%d" % (ag, i), [128, SM], F32)) for i in range(3)] for ag in range(2)]
            cvtb = [bufs(3) for _ in range(2)]
            sa = [es.enter_context(sbt("ff_sa%d" % i, [128, SM], F32)) for i in range(3)]
            sab = bufs(3)
            mb = [es.enter_context(sbt("ff_m%d" % i, [128, NJ, SM], BF16)) for i in range(2)]
            mbb = [bufs(NJ) for _ in range(2)]
            if half == 1:
                xr = [es.enter_context(sbt("ff_xr%d" % i, [128, 8, SM], F32)) for i in range(2)]
                xrb = [bufs(8) for _ in range(2)]
            qs = {"q": 0}
            PRE = 3

            def up_unit(S, W, hb, hbb, z, jj):
                r = qs["q"] % 3
                qs["q"] += 1
                pab = []
                for ag in range(2):
                    pa, pb = bank()
                    for c in range(8):
                        fw.op("pe", lambda e: e.matmul(pa[:, 0:W], wup[:, c, ag, jj * 128:(jj + 1) * 128], hb[:, c, 0:W], start=(c == 0), stop=(c == 7)),
                              reads=[hbb] + wupb[0 if jj < 6 else 1], writes=[pb])
                    pab.append((pa, pb))
                for ag in range(2):
                    pa, pb = pab[ag]
                    fw.op("act", lambda e: e.activation(out=cvt[ag][r][:, 0:S], in_=pa[:, H:W], func=AF.Identity, scale=cw[:, ag, jj, 2:3], bias=cbias[:, ag, jj:jj + 1]),
                          reads=[pb, vb], writes=[cvtb[ag][r]])
                for kk in (1, 0):
                    sh = 2 - kk
                    for ag in range(2):
                        pa, pb = pab[ag]
                        fw.op("dve", lambda e: e.scalar_tensor_tensor(out=cvt[ag][r][:, 0:S], in0=pa[:, H - sh:W - sh], scalar=cw[:, ag, jj, kk:kk + 1],
                                                                      in1=cvt[ag][r][:, 0:S], op0=ALU.mult, op1=ALU.add),
                              reads=[pb, cvtb[ag][r], vb], writes=[cvtb[ag][r]])
                fw.op("act", lambda e: e.activation(out=sa[r][:, 0:S], in_=cvt[0][r][:, 0:S], func=AF.Silu), reads=[cvtb[0][r]], writes=[sab[r]])
                fw.op("pool", lambda e: e.tensor_tensor(out=mb[z][:, jj, 0:S], in0=sa[r][:, 0:S], in1=cvt[1][r][:, 0:S], op=ALU.mult),
                      reads=[sab[r], cvtb[1][r]], writes=[mbb[z][jj]])

            order = [(s, ti) for s in range(NSEQ) for ti in range(NT)]
            for o_ in order[0:2]:
                A.prefetch(src, srcb, o_[0], o_[1])
            slot_next = A.run(src, srcb, order[0][0], order[0][1], g_t, g_b)
            for oi, (s, ti) in enumerate(order):
                t0, S = tiles[ti]
                W = S + H
                sl = slot_next
                if oi + 2 < len(order):
                    A.prefetch(src, srcb, order[oi + 2][0], order[oi + 2][1])
                z = oi % 2
                if half == 1:
                    for hf in range(2):
                        c0 = hf * 4
                        fw.dma(xr[z][:, c0:c0 + 4, 0:S], dst[s, c0 * 128:(c0 + 4) * 128, t0:t0 + S].rearrange("(c p) t -> p c t", p=128),
                               reads=[dstb[s][ti]], writes=xrb[z][c0:c0 + 4])
                if oi + 1 < len(order):
                    slot_next = A.run(src, srcb, order[oi + 1][0], order[oi + 1][1], g_t, g_b)
                hb, hbb, xt, xtb = A.hb[sl[0]], A.hbb[sl[0]], A.xt[sl[1]], A.xtb[sl[1]]
                for jj in range(PRE if oi > 0 else 0, NJ):
                    up_unit(S, W, hb, hbb, z, jj)
                if oi + 1 < len(order):
                    Sn = tiles[order[oi + 1][1]][1]
                    for jj in range(PRE):
                        up_unit(Sn, Sn + H, A.hb[slot_next[0]], A.hbb[slot_next[0]], (oi + 1) % 2, jj)
                for m in range(8):
                    pa, pb = bank()
                    for jj in range(NJ):
                        fw.op("pe", lambda e: e.matmul(pa[:, 0:S], wdn[:, jj, m * 128:(m + 1) * 128], mb[z][:, jj, 0:S], start=(jj == 0), stop=(jj == NJ - 1)),
                              reads=[mbb[z][jj]] + wdnb, writes=[pb])
                    if half == 0:
                        fw.op("dve", lambda e: e.tensor_tensor(out=xt[:, m, H:W], in0=xt[:, m, H:W], in1=pa[:, 0:S], op=ALU.add),
                              reads=[pb, xtb[m]], writes=[xtb[m]])
                    else:
                        fw.op("dve", lambda e: e.tensor_tensor(out=xr[z][:, m, 0:S], in0=xr[z][:, m, 0:S], in1=pa[:, 0:S], op=ALU.add),
                              reads=[pb, xrb[z][m]], writes=[xrb[z][m]])
                for hf in range(2):
                    c0 = hf * 4
                    if half == 0:
                        fw.dma(dst[s, c0 * 128:(c0 + 4) * 128, t0:t0 + S].rearrange("(c p) t -> p c t", p=128), xt[:, c0:c0 + 4, H:W],
                               reads=xtb[c0:c0 + 4], writes=[dstb[s][ti]], q="pool")
                    else:
                        fw.dma(dst[s, c0 * 128:(c0 + 4) * 128, t0:t0 + S].rearrange("(c p) t -> p c t", p=128), xr[z][:, c0:c0 + 4, 0:S],
                               reads=xrb[z][c0:c0 + 4], writes=[dstb[s][ti]], q="pool")
        fw.barrier()

    def odd_proj_phase(j, src, srcb):
        with ExitStack() as es:
            A = StageA(es, 0)
            g_t, g_b = load_vec(es, "od_g", od_norm[j], 8)
            gq, gqb = load_vec(es, "od_gq", od_q_norm[j], 3)
            gkv, gkvb = load_vec(es, "od_gkv", od_kv_norm[j], 2)
            win = es.enter_context(sbt("od_win", [128, 8, ODD_IN + 32], BF16))
            winb = []
            for c in range(8):
                rows = od_w_in[j, c * 128:(c + 1) * 128, :]
                load_w_cast(win[:, c, 0:ODD_IN], rows, winb)
                load_w_cast(win[:, c, ODD_IN:ODD_IN + 16], rows[:, 656:672], winb)
                load_w_cast(win[:, c, ODD_IN + 16:ODD_IN + 32], rows[:, 640:656], winb)
            wqn = es.enter_context(sbt("od_wqn", [128, 3, NH, QKN], BF16))
            wqr = es.enter_context(sbt("od_wqr", [128, 3, 2, NH, QKR], BF16))
            wqb = []
            for c in range(3):
                rows = od_w_uq[j, c * 128:(c + 1) * 128, :].rearrange("p (h e) -> p h e", h=NH)
                load_w_cast(wqn[:, c, :, :], rows[:, :, 0:64], wqb)
                load_w_cast(wqr[:, c, 0, :, :], rows[:, :, 64:96], wqb)
                load_w_cast(wqr[:, c, 1, :, 0:16], rows[:, :, 80:96], wqb)
                load_w_cast(wqr[:, c, 1, :, 16:32], rows[:, :, 64:80], wqb)
            wkn = es.enter_context(sbt("od_wkn", [128, 2, NH, QKN], BF16))
            wv = es.enter_context(sbt("od_wv", [128, 2, NH, VH], BF16))
            wkvb = []
            for c in range(2):
                rows = od_w_ukv[j, c * 128:(c + 1) * 128, :].rearrange("p (h e) -> p h e", h=NH)
                load_w_cast(wkn[:, c, :, :], rows[:, :, 0:64], wkvb)
                load_w_cast(wv[:, c, :, :], rows[:, :, 64:128], wkvb)
            cs = es.enter_context(sbt("od_cs", [128, 2, T], F32))
            csb = Buf()
            fw.dma(cs[:, 0, :], rope_in[:, 0, :], writes=[csb])
            fw.dma(cs[:, 1, :], rope_in[:, 1, :], writes=[csb])
            S_ = SM
            def sb(name, shape, dt=F32, n=2):
                return [es.enter_context(sbt("%s%d" % (name, i), shape, dt)) for i in range(n)]
            cl = sb("od_cl", [128, 5, S_]); clb = [bufs(5) for _ in range(2)]
            csq = sb("od_csq", [128, 5, S_], BF16); csqb = [bufs(5) for _ in range(2)]
            rsq = sb("od_rsq", [128, 2, S_]); rsqb = [bufs(2) for _ in range(2)]
            cn = sb("od_cn", [128, 5, S_], BF16); cnb = [bufs(5) for _ in range(2)]
            krt = sb("od_kr", [32, 3, S_]); krb = [Buf() for _ in range(2)]
            krbf = sb("od_krbf", [32, S_], BF16); krbfb = [Buf() for _ in range(2)]
            qst = sb("od_qst", [128, S_], BF16, n=4); qstb = bufs(4)
            rt = sb("od_rt", [128, 2, S_], F32, n=2); rtb = bufs(2)
            vst = sb("od_vst", [128, NH * VH], BF16, n=2); vstb = bufs(2)
            k = 0
            qi = 0
            ri = 0
            vi = 0
            order = [(s, ti) for s in range(NSEQ) for ti in range(NT)]
            for o_ in order[0:2]:
                A.prefetch(src, srcb, o_[0], o_[1])
            slot_next = A.run(src, srcb, order[0][0], order[0][1], g_t, g_b)
            for oi, (s, ti) in enumerate(order):
                if True:
                    t0, S = tiles[ti]
                    sl = slot_next
                    if oi + 2 < len(order):
                        A.prefetch(src, srcb, order[oi + 2][0], order[oi + 2][1])
                    if oi + 1 < len(order):
                        slot_next = A.run(src, srcb, order[oi + 1][0], order[oi + 1][1], g_t, g_b)
                    hb, hbb = A.hb[sl[0]], A.hbb[sl[0]]
                    z = k % 2
                    k += 1
                    for c in range(5):
                        pa, pb = bank()
                        for kc in range(8):
                            fw.op("pe", lambda e: e.matmul(pa[:, 0:S], win[:, kc, c * 128:(c + 1) * 128], hb[:, kc, 0:S], start=(kc == 0), stop=(kc == 7)),
                                  reads=[hbb] + winb, writes=[pb])
                        fw.op("dve", lambda e: e.tensor_copy(cl[z][:, c, 0:S], pa[:, 0:S]), reads=[pb], writes=[clb[z][c]])
                        fw.op("act", lambda e: e.activation(out=csq[z][:, c, 0:S], in_=cl[z][:, c, 0:S], func=AF.Square), reads=[clb[z][c]], writes=[csqb[z][c]])
                    pk = []
                    for v in range(2):
                        pa, pb = bank()
                        col = 640 if v == 0 else ODD_IN
                        for kc in range(8):
                            fw.op("pe", lambda e: e.matmul(pa[0:32, 0:S], win[:, kc, col:col + 32], hb[:, kc, 0:S], start=(kc == 0), stop=(kc == 7)),
                                  reads=[hbb] + winb, writes=[pb])
                        pk.append((pa, pb))
                    fw.op("dve", lambda e: e.tensor_tensor(out=krt[z][:, 0, 0:S], in0=pk[0][0][0:32, 0:S], in1=cs[0:32, 0, t0:t0 + S], op=ALU.mult),
                          reads=[pk[0][1], csb], writes=[krb[z]])
                    fw.op("dve", lambda e: e.tensor_tensor(out=krt[z][:, 1, 0:S], in0=pk[1][0][0:32, 0:S], in1=cs[0:32, 1, t0:t0 + S], op=ALU.mult),
                          reads=[pk[1][1], csb], writes=[krb[z]])
                    fw.op("dve", lambda e: e.tensor_tensor(out=krbf[z][:, 0:S], in0=krt[z][:, 0, 0:S], in1=krt[z][:, 1, 0:S], op=ALU.add),
                          reads=[krb[z]], writes=[krbfb[z]])
                    fw.dma(KR[s, :, t0:t0 + S], krbf[z][:, 0:S], reads=[krbfb[z]], writes=[Buf()])
                    for li, (c0, ncx, dim, gv, gvb) in enumerate(((0, 3, QL, gq, gqb), (3, 2, KVL, gkv, gkvb))):
                        pa, pb = bank()
                        for c in range(ncx):
                            fw.op("pe", lambda e: e.matmul(pa[:, 0:S], ones_bf[:], csq[z][:, c0 + c, 0:S], start=(c == 0), stop=(c == ncx - 1)),
                                  reads=[csqb[z][c0 + c], cb], writes=[pb])
                        fw.op("act", lambda e: e.activation(out=rsq[z][:, li, 0:S], in_=pa[:, 0:S], func=AF.Sqrt, scale=1.0 / dim, bias=EPS),
                              reads=[pb], writes=[rsqb[z][li]])
                        fw.op("dve", lambda e: e.reciprocal(out=rsq[z][:, li, 0:S], in_=rsq[z][:, li, 0:S]), reads=[rsqb[z][li]], writes=[rsqb[z][li]])
                        for c in range(ncx):
                            fw.op("dve", lambda e: e.scalar_tensor_tensor(out=cn[z][:, c0 + c, 0:S], in0=cl[z][:, c0 + c, 0:S], scalar=gv[:, c:c + 1],
                                                                          in1=rsq[z][:, li, 0:S], op0=ALU.mult, op1=ALU.mult),
                                  reads=[clb[z][c0 + c], rsqb[z][li], gvb], writes=[cnb[z][c0 + c]])
                    for m in range(8):
                        pa, pb = bank()
                        for c in range(3):
                            fw.op("pe", lambda e: e.matmul(pa[:, 0:S], wqn[:, c, 2 * m:2 * m + 2, :], cn[z][:, c, 0:S], start=(c == 0), stop=(c == 2)),
                                  reads=[cnb[z][c]] + wqb, writes=[pb])
                        u = qi % 4
                        qi += 1
                        fw.op("act", lambda e: e.activation(out=qst[u][:, 0:S], in_=pa[:, 0:S], func=AF.Copy), reads=[pb], writes=[qstb[u]])
                        fw.dma(QS[s, m * 128:(m + 1) * 128, t0:t0 + S], qst[u][:, 0:S], reads=[qstb[u]], writes=[Buf()])
                    for m in range(4):
                        pr = []
                        for v in range(2):
                            pa, pb = bank()
                            for c in range(3):
                                fw.op("pe", lambda e: e.matmul(pa[:, 0:S], wqr[:, c, v, 4 * m:4 * m + 4, :], cn[z][:, c, 0:S], start=(c == 0), stop=(c == 2)),
                                      reads=[cnb[z][c]] + wqb, writes=[pb])
                            pr.append((pa, pb))
                        w_ = ri % 2
                        ri += 1
                        u = qi % 4
                        qi += 1
                        fw.op("dve", lambda e: e.tensor_tensor(out=rt[w_][:, 0, 0:S], in0=pr[0][0][:, 0:S], in1=cs[:, 0, t0:t0 + S], op=ALU.mult),
                              reads=[pr[0][1], csb], writes=[rtb[w_]])
                        fw.op("dve", lambda e: e.tensor_tensor(out=rt[w_][:, 1, 0:S], in0=pr[1][0][:, 0:S], in1=cs[:, 1, t0:t0 + S], op=ALU.mult),
                              reads=[pr[1][1], csb], writes=[rtb[w_]])
                        fw.op("dve", lambda e: e.tensor_tensor(out=qst[u][:, 0:S], in0=rt[w_][:, 0, 0:S], in1=rt[w_][:, 1, 0:S], op=ALU.add),
                              reads=[rtb[w_]], writes=[qstb[u]])
                        fw.dma(QR[s, m * 128:(m + 1) * 128, t0:t0 + S], qst[u][:, 0:S], reads=[qstb[u]], writes=[Buf()])
                    for m in range(8):
                        pa, pb = bank()
                        for c in range(2):
                            fw.op("pe", lambda e: e.matmul(pa[:, 0:S], wkn[:, c, 2 * m:2 * m + 2, :], cn[z][:, 3 + c, 0:S], start=(c == 0), stop=(c == 1)),
                                  reads=[cnb[z][3 + c]] + wkvb, writes=[pb])
                        u = qi % 4
                        qi += 1
                        fw.op("act", lambda e: e.activation(out=qst[u][:, 0:S], in_=pa[:, 0:S], func=AF.Copy), reads=[pb], writes=[qstb[u]])
                        fw.dma(KS[s, m * 128:(m + 1) * 128, t0:t0 + S], qst[u][:, 0:S], reads=[qstb[u]], writes=[Buf()])
                    for b0 in range(0, S, 128):
                        nb_ = min(128, S - b0)
                        u = vi % 2
                        vi += 1
                        for hv in range(2):
                            pa, pb = bank()
                            for c in range(2):
                                fw.op("pe", lambda e: e.matmul(pa[0:nb_, 0:512], cn[z][:, 3 + c, b0:b0 + nb_], wv[:, c, hv * 8:(hv + 1) * 8, :], start=(c == 0), stop=(c == 1)),
                                      reads=[cnb[z][3 + c]] + wkvb, writes=[pb])
                            if hv == 0:
                                fw.op("act", lambda e: e.activation(out=vst[u][0:nb_, 0:512], in_=pa[0:nb_, 0:512], func=AF.Copy), reads=[pb], writes=[vstb[u]])
                            else:
                                fw.op("dve", lambda e: e.tensor_copy(vst[u][0:nb_, 512:1024], pa[0:nb_, 0:512]), reads=[pb], writes=[vstb[u]])
                        fw.dma(VS[s, t0 + b0:t0 + b0 + nb_, :], vst[u][0:nb_, :], reads=[vstb[u]], writes=[Buf()])
        fw.barrier()

    def attn_phase():
        scale = QKH ** -0.5
        NKT = -(-T // 128)
        qtiles = [(q0, min(512, T - q0)) for q0 in range(0, T, 512)]
        with ExitStack() as es:
            qT = [es.enter_context(sbt("at_q%d" % i, [QKH, T], BF16)) for i in range(2)]
            kT = [es.enter_context(sbt("at_k%d" % i, [QKH, T], BF16)) for i in range(2)]
            va = [es.enter_context(sbt("at_v%d" % i, [128, NKT, VH + 1], BF16)) for i in range(2)]
            ib = [Buf() for _ in range(2)]
            ibq, ibq2, ibk, ibk2, ibv2 = (bufs(2) for _ in range(5))
            for i in range(2):
                fw.op("pool", lambda e: e.memset(va[i][:], 1.0), writes=[ib[i], ibv2[i]])
            ot = [es.enter_context(sbt("at_o%d" % i, [64, T], BF16)) for i in range(2)]
            otb = bufs(2)
            NP = 6
            LOOK = 3
            EDEF = 9
            pT = [es.enter_context(sbt("at_pp%d" % i, [128, 512], BF16)) for i in range(NP)]
            pTb = bufs(NP)
            rl = [es.enter_context(sbt("at_rll%d" % i, [128, 512], F32)) for i in range(3)]
            rlb = bufs(3)
            rb_ = [es.enter_context(sbt("at_rbb%d" % i, [64, 512], F32)) for i in range(3)]
            rbb = bufs(3)
            k = 0
            st = {"pi": 0, "ui": 0, "oi": 0}
            pstate["lo"] = 2
            pend = []

            def tick():
                for p_ in pend:
                    p_[0] -= 1
                while pend and pend[0][0] <= 0:
                    pend.pop(0)[1]()

            def flush():
                while pend:
                    pend.pop(0)[1]()

            heads = [(s, h) for s in range(NSEQ) for h in range(NH)]

            def emit_loads(idx):
                s, h = heads[idx]
                z = idx % 2
                fw.dma(qT[z][0:QKN, :], QS[s, h * QKN:(h + 1) * QKN, :], reads=[QKVB[s]], writes=[ibq[z]])
                fw.dma(qT[z][QKN:QKH, :], QR[s, h * QKR:(h + 1) * QKR, :], reads=[QKVB[s]], writes=[ibq2[z]])
                fw.dma(kT[z][0:QKN, :], KS[s, h * QKN:(h + 1) * QKN, :], reads=[QKVB[s]], writes=[ibk[z]])
                fw.dma(kT[z][QKN:QKH, :], KR[s, :, :], reads=[QKVB[s]], writes=[ibk2[z]])
                nfull = T // 128
                if nfull:
                    fw.dma(va[z][:, 0:nfull, 0:VH], VS[s, 0:nfull * 128, h * VH:(h + 1) * VH].rearrange("(n p) v -> p n v", p=128),
                           reads=[QKVB[s]], writes=[ib[z]])
                if T % 128:
                    r_ = T % 128
                    fw.dma(va[z][0:r_, nfull, 0:VH], VS[s, nfull * 128:T, h * VH:(h + 1) * VH], reads=[QKVB[s]], writes=[ibv2[z]])

            units = []
            for idx, (s, h) in enumerate(heads):
                for qi_, (q0, nq) in enumerate(qtiles):
                    kts = [kt for kt in range(NKT) if kt * 128 < q0 + nq]
                    bi = st["oi"] % 2
                    st["oi"] += 1
                    for j_, kt in enumerate(kts):
                        units.append(dict(q0=q0, nq=nq, kt=kt, first=(j_ == 0), last=(j_ == len(kts) - 1), bi=bi, idx=idx, z=idx % 2,
                                          head_last=(qi_ == len(qtiles) - 1 and j_ == len(kts) - 1)))

            def emit_s(u_):
                q0, nq, kt, z = u_["q0"], u_["nq"], u_["kt"], u_["z"]
                inb = [ib[z], ibq[z], ibq2[z], ibk[z], ibk2[z], ibv2[z]]
                k0 = kt * 128
                nk = min(128, T - k0)
                off = max(0, k0 - q0)
                n = nq - off
                pa, pb = bank()
                fw.op("pe", lambda e: e.matmul(pa[0:nk, 0:n], kT[z][:, k0:k0 + nk], qT[z][:, q0 + off:q0 + nq], start=True, stop=True),
                      reads=inb, writes=[pb])
                u = st["pi"] % NP
                st["pi"] += 1
                fw.op("act", lambda e: e.activation(out=pT[u][0:nk, 0:n], in_=pa[0:nk, 0:n], func=AF.Exp, scale=scale),
                      reads=[pb], writes=[pTb[u]])
                if k0 + nk - 1 > q0 + off:
                    nd = min(nk, n)
                    fw.op("pool", lambda e: e.tensor_tensor(out=pT[u][0:nk, 0:nd], in0=pT[u][0:nk, 0:nd], in1=tri[0:nk, 0:nd], op=ALU.mult),
                          reads=[pTb[u], cb], writes=[pTb[u]])
                u_.update(u=u, nk=nk, off=off, n=n)

            def emit_pv(u_):
                q0, nq, kt, u, nk, off, n, z = (u_[x] for x in ("q0", "nq", "kt", "u", "nk", "off", "n", "z"))
                inb = [ib[z], ibq[z], ibq2[z], ibk[z], ibk2[z], ibv2[z]]
                po, pob = ps[:, u_["bi"], :], psb[u_["bi"]]
                if u_["first"]:
                    for p_ in [p_ for p_ in pend if p_[2] == u_["bi"]]:
                        pend.remove(p_)
                        p_[1]()
                fw.op("pe", lambda e: e.matmul(po[0:VH + 1, off:nq], va[z][0:nk, kt, :], pT[u][0:nk, 0:n], start=u_["first"], stop=u_["last"]),
                      reads=[pTb[u]] + inb, writes=[pob])
                if u_["last"]:
                    w_ = st["ui"] % 3
                    st["ui"] += 1
                    fw.op("dve", lambda e: e.reciprocal(out=rl[w_][64:65, 0:nq], in_=po[64:65, 0:nq]), reads=[pob], writes=[rlb[w_]])
                    hl = u_["head_last"]
                    s_, h_ = heads[u_["idx"]]

                    def epi(w_=w_, po=po, pob=pob, q0=q0, nq=nq, zz=z, hl=hl, s_=s_, h_=h_):
                        pa, pb = bank()
                        fw.op("pe", lambda e: e.matmul(pa[0:64, 0:nq], ones_f[64:65, 0:64], rl[w_][64:65, 0:nq], start=True, stop=True),
                              reads=[rlb[w_], cb], writes=[pb])
                        fw.op("act", lambda e: e.activation(out=rb_[w_][:, 0:nq], in_=pa[0:64, 0:nq], func=AF.Copy), reads=[pb], writes=[rbb[w_]])
                        fw.op("dve", lambda e: e.tensor_tensor(out=ot[zz][:, q0:q0 + nq], in0=po[0:64, 0:nq], in1=rb_[w_][:, 0:nq], op=ALU.mult),
                              reads=[pob, rbb[w_]], writes=[otb[zz]])
                        if hl:
                            fw.dma(OS[s_, h_ * VH:(h_ + 1) * VH, :], ot[zz][:, :], reads=[otb[zz]], writes=[Buf()], q="pool")
                    pend.append([EDEF, epi, u_["bi"]])
                if u_["head_last"] and u_["idx"] + 2 < len(heads):
                    emit_loads(u_["idx"] + 2)

            emit_loads(0)
            if len(heads) > 1:
                emit_loads(1)
            nU = len(units)
            for i in range(nU + LOOK):
                if i < nU:
                    emit_s(units[i])
                if i >= LOOK:
                    emit_pv(units[i - LOOK])
                    tick()
            flush()
        pstate["lo"] = 0
        fw.barrier()

    def odd_out_phase(j, src, srcb, dst, dstb):
        with ExitStack() as es:
            wout = es.enter_context(sbt("oo_wout", [128, 8, D], BF16))
            woutb = []
            for c in range(8):
                load_w_cast(wout[:, c, :], od_w_out[j, c * 128:(c + 1) * 128, :], woutb)
            S_ = SM
            xt = [es.enter_context(sbt("oo_xt%d" % i, [128, 8, S_], F32)) for i in range(2)]
            xtb = [bufs(8) for _ in range(2)]
            o_t = [es.enter_context(sbt("oo_o%d" % i, [128, 8, S_], BF16)) for i in range(2)]
            otb = [Buf() for _ in range(2)]
            k = 0
            for s in range(NSEQ):
                for ti in range(NT):
                    t0, S = tiles[ti]
                    z = k % 2
                    k += 1
                    for hf in range(2):
                        c0 = hf * 4
                        fw.dma(xt[z][:, c0:c0 + 4, 0:S], src[s, c0 * 128:(c0 + 4) * 128, t0:t0 + S].rearrange("(c p) t -> p c t", p=128),
                               reads=[srcb[s][ti]], writes=xtb[z][c0:c0 + 4])
                    fw.dma(o_t[z][:, :, 0:S], OS[s, :, t0:t0 + S].rearrange("(c p) t -> p c t", p=128), reads=[OSB[s]], writes=[otb[z]])
                    for m in range(8):
                        pa, pb = bank()
                        for c in range(8):
                            fw.op("pe", lambda e: e.matmul(pa[:, 0:S], wout[:, c, m * 128:(m + 1) * 128], o_t[z][:, c, 0:S], start=(c == 0), stop=(c == 7)),
                                  reads=[otb[z]] + woutb, writes=[pb])
                        fw.op("dve", lambda e: e.tensor_tensor(out=xt[z][:, m, 0:S], in0=xt[z][:, m, 0:S], in1=pa[:, 0:S], op=ALU.add),
                              reads=[pb, xtb[z][m]], writes=[xtb[z][m]])
                    for hf in range(2):
                        c0 = hf * 4
                        fw.dma(dst[s, c0 * 128:(c0 + 4) * 128, t0:t0 + S].rearrange("(c p) t -> p c t", p=128), xt[z][:, c0:c0 + 4, 0:S],
                               reads=xtb[z][c0:c0 + 4], writes=[dstb[s][ti]], q="pool")
        fw.barrier()

    def final_phase(src, srcb):
        outb = Buf()
        with ExitStack() as es:
            g_t, g_b = load_vec(es, "fin_g", final_norm, 8)
            xt = [es.enter_context(sbt("fi_xt%d" % i, [128, 8, 512], F32)) for i in range(2)]
            xtb = [bufs(8) for _ in range(2)]
            sq = es.enter_context(sbt("fi_sq", [128, 8, 512], BF16))
            sqb = Buf()
            rs = [es.enter_context(sbt("fi_rs%d" % i, [128, 512], F32)) for i in range(2)]
            rsb = bufs(2)
            yo = [es.enter_context(sbt("fi_yo%d" % i, [128, D], F32)) for i in range(3)]
            yob = bufs(3)
            allsrc = lambda s: list(srcb[s])
            k = 0
            yi = 0
            for s in range(NSEQ):
                for t0 in range(0, SEQ, 512):
                    n = min(512, SEQ - t0)
                    z = k % 2
                    k += 1
                    for hf in range(2):
                        c0 = hf * 4
                        fw.dma(xt[z][:, c0:c0 + 4, 0:n], src[s, c0 * 128:(c0 + 4) * 128, NMETA + t0:NMETA + t0 + n].rearrange("(c p) t -> p c t", p=128),
                               reads=allsrc(s), writes=xtb[z][c0:c0 + 4])
                    for hf in range(2):
                        c0 = hf * 4
                        fw.op("act", lambda e: e.activation(out=sq[:, c0:c0 + 4, 0:n], in_=xt[z][:, c0:c0 + 4, 0:n], func=AF.Square),
                              reads=xtb[z][c0:c0 + 4], writes=[sqb])
                    pa, pb = bank()
                    for c in range(8):
                        fw.op("pe", lambda e: e.matmul(pa[:, 0:n], ones_bf[:], sq[:, c, 0:n], start=(c == 0), stop=(c == 7)), reads=[sqb, cb], writes=[pb])
                    fw.op("act", lambda e: e.activation(out=rs[z][:, 0:n], in_=pa[:, 0:n], func=AF.Sqrt, scale=1.0 / D, bias=EPS), reads=[pb], writes=[rsb[z]])
                    fw.op("dve", lambda e: e.reciprocal(out=rs[z][:, 0:n], in_=rs[z][:, 0:n]), reads=[rsb[z]], writes=[rsb[z]])
                    for c in range(8):
                        fw.op("dve", lambda e: e.scalar_tensor_tensor(out=xt[z][:, c, 0:n], in0=xt[z][:, c, 0:n], scalar=g_t[:, c:c + 1], in1=rs[z][:, 0:n],
                                                                      op0=ALU.mult, op1=ALU.mult),
                              reads=[xtb[z][c], rsb[z], g_b], writes=[xtb[z][c]])
                    for b0 in range(0, n, 128):
                        u = yi % 3
                        yi += 1
                        for hv in range(2):
                            pa, pb = bank()
                            for cc in range(4):
                                c = hv * 4 + cc
                                fw.op("pe", lambda e: e.transpose(pa[:, cc * 128:(cc + 1) * 128], xt[z][:, c, b0:b0 + 128], ident[:]),
                                      reads=[xtb[z][c], cb], writes=[pb])
                            if hv == 0:
                                fw.op("act", lambda e: e.activation(out=yo[u][:, 0:512], in_=pa[:, 0:512], func=AF.Copy), reads=[pb], writes=[yob[u]])
                            else:
                                fw.op("dve", lambda e: e.tensor_copy(yo[u][:, 512:1024], pa[:, 0:512]), reads=[pb], writes=[yob[u]])
                        fw.dma(out_ap[s, t0 + b0:t0 + b0 + 128, :], yo[u][:, :], reads=[yob[u]], writes=[Buf()])
        fw.barrier()

    prologue(XP[0], XB[0])
    cur = 0
    for layer in range(DEPTH):
        j = layer // 2
        a, b = cur, 1 - cur
        if layer % 2 == 0:
            even_phase(j, XP[a], XB[a], XP[b], XB[b])
        else:
            odd_proj_phase(j, XP[a], XB[a])
            if dbg != 1:
                attn_phase()
            if dbg == 0:
                odd_out_phase(j, XP[a], XB[a], XP[b], XB[b])
            else:
                a, b = b, a
                cur = 1 - cur
        ffn_phase(layer, 0, XP[b], XB[b], XP[a], XB[a])
        ffn_phase(layer, 1, XP[b], XB[b], XP[a], XB[a])
    final_phase(XP[cur], XB[cur])
    return nc, fw


def rope_tables(T):
    pos = np.arange(T, dtype=np.float32)
    inv_freq = (np.float32(10000.0) ** (-np.arange(0, QKR, 2, dtype=np.float32) / np.float32(QKR))).astype(np.float32)
    ang = (pos[None, :] * inv_freq[:, None]).astype(np.float32)
    cos = np.cos(ang).astype(np.float32)
    sin = np.sin(ang).astype(np.float32)
    tab = np.zeros((128, 2, T), np.float32)
    for p in range(128):
        f = p % 16
        tab[p, 0] = cos[f]
        tab[p, 1] = -sin[f] if (p % 32) < 16 else sin[f]
    return tab


_WEIGHT_NAMES = ["meta_tokens", "ev_norm", "ev_w_in", "ev_conv_a", "ev_conv_b", "ev_conv_b_bias", "ev_gate_r_w", "ev_gate_r_b",
                 "ev_gate_i_w", "ev_gate_i_b", "ev_lru_lambda", "ev_w_out", "od_norm", "od_w_in", "od_q_norm", "od_kv_norm",
                 "od_w_uq", "od_w_ukv", "od_w_out", "ffn_norm", "ffn_w_up", "ffn_conv_w", "ffn_conv_b", "ffn_w_down", "final_norm"]


def run(inputs, ncores=NCORES, depth=4, same_engine_sync=True, trace=False, dbg=0):
    x = np.ascontiguousarray(np.asarray(inputs["x"], dtype=np.float32))
    B, SEQ, _ = x.shape
    nseq = B // ncores
    nc, fw = build_program(NSEQ=nseq, SEQ=SEQ, DEPTH=depth, same_engine_sync=same_engine_sync, dbg=dbg)
    nod = depth // 2
    shared = {}
    for n in _WEIGHT_NAMES:
        if n.startswith("od_") and nod == 0:
            continue
        shared[n] = np.ascontiguousarray(np.asarray(inputs[n], dtype=np.float32))
    if nod:
        shared["rope_cs"] = rope_tables(SEQ + NMETA)
    in_maps = []
    for c in range(ncores):
        m = dict(shared)
        m["x"] = x[c * nseq:(c + 1) * nseq]
        in_maps.append(m)
    res = run_bass_kernel_spmd(nc, in_maps, core_ids=list(range(ncores)), trace=trace)
    out = np.concatenate([r["out"] for r in res.results], axis=0)
    return out, res


def kernel(**inputs):
    out, _ = run(inputs)
    return out.astype(np.float32)
```

```python
import numpy as np
from contextlib import ExitStack
import concourse.bass as bass
import concourse.mybir as mybir
from concourse.bass_utils import run_bass_kernel_spmd

F32 = mybir.dt.float32
BF16 = mybir.dt.bfloat16
AF = mybir.ActivationFunctionType
ALU = mybir.AluOpType

D = 1024
NMETA = 16
EPS = 1e-6
CW = 512
EVEN_IN = 2560
NH = 16
QKN, QKR, QKH, VH = 64, 32, 96, 64
QL, KVL = 384, 256
ODD_IN = QL + KVL + QKR
DFF = 2816
NCORES = 8


class Buf:
    __slots__ = ("lw", "rd", "excl")

    def __init__(self, excl=False):
        self.lw = None
        self.rd = {}
        self.excl = excl


def bufs(n):
    return [Buf() for _ in range(n)]


class FW:
    ENGS = ("pe", "dve", "act", "pool", "sp")

    def __init__(self, nc, n_dma_sems=32, same_engine_sync=True):
        self.nc = nc
        self.eng = {"pe": nc.tensor, "dve": nc.vector, "act": nc.scalar, "pool": nc.gpsimd, "sp": nc.sync}
        self.sems = {}
        self.cnt = {}
        for e in self.ENGS:
            self.sems[e] = nc.alloc_semaphore("clk_" + e)
            self.cnt[e] = 0
        self.dma_sems = []
        for i in range(n_dma_sems):
            k = "dma%d" % i
            self.sems[k] = nc.alloc_semaphore("clk_" + k)
            self.cnt[k] = 0
            self.dma_sems.append(k)
        self.dma_rr = 0
        self.seen = {e: {} for e in self.ENGS}
        self.same_engine_sync = same_engine_sync
        self.n_inst = 0

    def _need(self, e, key, val, needs):
        if key == e and (e == "pe" or not self.same_engine_sync):
            return
        if self.seen[e].get(key, 0) >= val:
            return
        if needs.get(key, 0) < val:
            needs[key] = val

    def _deps(self, e, reads, writes):
        needs = {}
        for b in reads:
            if b.lw is not None:
                self._need(e, b.lw[0], b.lw[1], needs)
            if b.excl:
                for k, v in b.rd.items():
                    if k != e:
                        self._need(e, k, v, needs)
        for b in writes:
            if b.lw is not None:
                self._need(e, b.lw[0], b.lw[1], needs)
            for k, v in b.rd.items():
                self._need(e, k, v, needs)
        return needs

    def _emit_waits(self, e, needs):
        eng = self.eng[e]
        for k, v in needs.items():
            eng.wait_ge(self.sems[k], v)
            self.seen[e][k] = v

    def _mark(self, key, val, reads, writes):
        for b in reads:
            if b.rd.get(key, 0) < val:
                b.rd[key] = val
        for b in writes:
            b.lw = (key, val)
            b.rd = {}

    def op(self, e, fn, reads=(), writes=()):
        needs = self._deps(e, reads, writes)
        self._emit_waits(e, needs)
        inst = fn(self.eng[e])
        self.cnt[e] += 1
        inst.then_inc(self.sems[e], 1)
        self._mark(e, self.cnt[e], reads, writes)
        self.n_inst += 1
        return inst

    def dma(self, out, in_, reads=(), writes=(), q="sp", **kw):
        key = self.dma_sems[self.dma_rr]
        self.dma_rr = (self.dma_rr + 1) % len(self.dma_sems)
        needs = self._deps(q, reads, writes)
        if self.cnt[key] > 0:
            self._need(q, key, self.cnt[key], needs)
        self._emit_waits(q, needs)
        inst = self.eng[q].dma_start(out=out, in_=in_, **kw)
        self.cnt[key] += 16
        inst.then_inc(self.sems[key], 16)
        self._mark(key, self.cnt[key], reads, writes)
        self.n_inst += 1
        return inst

    def barrier(self):
        for e in self.ENGS:
            needs = {}
            for k, v in self.cnt.items():
                if v > 0 and k != e:
                    self._need(e, k, v, needs)
            self._emit_waits(e, needs)


def make_tiles(T, smax=509):
    n = -(-T // smax)
    b = [round(i * T / n) for i in range(n + 1)]
    return [(b[i], b[i + 1] - b[i]) for i in range(n)]


def build_program(NSEQ=2, SEQ=4096, DEPTH=4, same_engine_sync=True, dbg=0):
    T = SEQ + NMETA
    NEV = (DEPTH + 1) // 2
    NOD = DEPTH // 2
    nc = bass.Bass("TRN2", target_bir_lowering=False)
    fw = FW(nc, same_engine_sync=same_engine_sync)

    uid = {"n": 0}

    def sbt(name, shape, dt):
        uid["n"] += 1
        return nc.sbuf_tensor("%s_u%d" % (name, uid["n"]), shape, dt)

    def din(name, shape):
        return nc.dram_tensor(name, list(shape), F32, kind="ExternalInput").ap()

    x_in = din("x", (NSEQ, SEQ, D))
    meta_in = din("meta_tokens", (NMETA, D))
    ev_norm = din("ev_norm", (NEV, D))
    ev_w_in = din("ev_w_in", (NEV, D, EVEN_IN))
    ev_conv_a = din("ev_conv_a", (NEV, 3, CW))
    ev_conv_b = din("ev_conv_b", (NEV, 4, CW))
    ev_conv_b_bias = din("ev_conv_b_bias", (NEV, CW))
    ev_gate_r_w = din("ev_gate_r_w", (NEV, 8, 64, 64))
    ev_gate_r_b = din("ev_gate_r_b", (NEV, CW))
    ev_gate_i_w = din("ev_gate_i_w", (NEV, 8, 64, 64))
    ev_gate_i_b = din("ev_gate_i_b", (NEV, CW))
    ev_lam = din("ev_lru_lambda", (NEV, CW))
    ev_w_out = din("ev_w_out", (NEV, D, D))
    if NOD:
        od_norm = din("od_norm", (NOD, D))
        od_w_in = din("od_w_in", (NOD, D, ODD_IN))
        od_q_norm = din("od_q_norm", (NOD, QL))
        od_kv_norm = din("od_kv_norm", (NOD, KVL))
        od_w_uq = din("od_w_uq", (NOD, QL, NH * QKH))
        od_w_ukv = din("od_w_ukv", (NOD, KVL, NH * (QKN + VH)))
        od_w_out = din("od_w_out", (NOD, D, D))
        rope_in = din("rope_cs", (128, 2, T))
    ffn_norm = din("ffn_norm", (DEPTH, D))
    ffn_w_up = din("ffn_w_up", (DEPTH, D, 2 * DFF))
    ffn_conv_w = din("ffn_conv_w", (DEPTH, 3, 2 * DFF))
    ffn_conv_b = din("ffn_conv_b", (DEPTH, 2 * DFF))
    ffn_w_down = din("ffn_w_down", (DEPTH, DFF, D))
    final_norm = din("final_norm", (D,))
    out_ap = nc.dram_tensor("out", [NSEQ, SEQ, D], F32, kind="ExternalOutput").ap()

    XP = [nc.dram_tensor("xres%d" % i, [NSEQ, D, T], F32, kind="Internal").ap() for i in range(2)]
    tiles = make_tiles(T)
    NT = len(tiles)
    SM = max(S for _, S in tiles)
    XB = [[bufs(NT) for _ in range(NSEQ)] for _ in range(2)]
    if NOD:
        QS = nc.dram_tensor("qs", [NSEQ, NH * QKN, T], BF16, kind="Internal").ap()
        QR = nc.dram_tensor("qr", [NSEQ, NH * QKR, T], BF16, kind="Internal").ap()
        KS = nc.dram_tensor("ks", [NSEQ, NH * QKN, T], BF16, kind="Internal").ap()
        KR = nc.dram_tensor("kr", [NSEQ, QKR, T], BF16, kind="Internal").ap()
        VS = nc.dram_tensor("vs", [NSEQ, T, NH * VH], BF16, kind="Internal").ap()
        OS = nc.dram_tensor("os", [NSEQ, D, T], BF16, kind="Internal").ap()
        QKVB = [Buf() for _ in range(NSEQ)]
        OSB = [Buf() for _ in range(NSEQ)]

    ps = nc.alloc_psum_tensor("ps", [128, 8, 512], F32)
    psb = [Buf(excl=True) for _ in range(8)]
    pstate = {"i": 0}

    def bank():
        lo = pstate.get("lo", 0)
        i = pstate["i"]
        if i < lo:
            i = lo
        pstate["i"] = i + 1 if i + 1 < 8 else lo
        return ps[:, i, :], psb[i]

    ones_bf = nc.alloc_sbuf_tensor("ones_bf", [128, 128], BF16)
    ones_f = nc.alloc_sbuf_tensor("ones_f", [128, 64], F32)
    ident = nc.alloc_sbuf_tensor("ident", [128, 128], F32)
    tri = nc.alloc_sbuf_tensor("tri", [128, 128], BF16)
    cb = Buf()
    fw.op("dve", lambda e: e.memset(ones_bf[:], 1.0), writes=[cb])
    fw.op("dve", lambda e: e.memset(ones_f[:], 1.0), writes=[cb])
    fw.op("dve", lambda e: e.memset(ident[:], 1.0), writes=[cb])
    fw.op("dve", lambda e: e.memset(tri[:], 1.0), writes=[cb])
    fw.op("pool", lambda e: e.affine_select(out=ident[:], in_=ident[:], pattern=[[-1, 128]], compare_op=ALU.is_equal,
                                            fill=0.0, base=0, channel_multiplier=1), reads=[cb], writes=[cb])
    fw.op("pool", lambda e: e.affine_select(out=tri[:], in_=tri[:], pattern=[[1, 128]], compare_op=ALU.is_ge,
                                            fill=0.0, base=0, channel_multiplier=-1), reads=[cb], writes=[cb])

    def load_vec(es, name, src, nchunk):
        t = es.enter_context(sbt(name, [128, nchunk], F32))
        b = Buf()
        fw.dma(t[:], src.rearrange("(c p) -> p c", p=128), writes=[b], allow_slow_non_contiguous=True)
        return t, b

    class StageA:
        def __init__(self, es, H, nslots=2):
            self.H = H
            W = SM + H
            self.W = W
            self.n = nslots
            self.nx = nslots + 1
            self.xt = [es.enter_context(sbt("xt%d" % i, [128, 8, W], F32)) for i in range(self.nx)]
            self.xtb = [bufs(8) for _ in range(self.nx)]
            self.hb = [es.enter_context(sbt("hb%d" % i, [128, 8, W], BF16)) for i in range(nslots)]
            self.hbb = [Buf() for _ in range(nslots)]
            self.sq = es.enter_context(sbt("sq", [128, 8, W], BF16))
            self.sqb = Buf()
            self.rs = [es.enter_context(sbt("rs%d" % i, [128, W], F32)) for i in range(nslots)]
            self.rsb = [Buf() for _ in range(nslots)]
            self.k = 0
            self.kx = 0
            self.pref = {}

        def prefetch(self, src, srcb, s, ti):
            t0, S = tiles[ti]
            H = self.H
            sx = self.kx % self.nx
            self.kx += 1
            xt, xtb = self.xt[sx], self.xtb[sx]
            W = S + H
            h0 = min(H, t0)
            if h0 < H:
                fw.op("pool", lambda e: e.memset(xt[:, :, 0:H - h0], 0.0), writes=xtb)
            rb = [srcb[s][ti]] + ([srcb[s][ti - 1]] if (h0 and ti > 0) else [])
            for half in range(2):
                c0 = half * 4
                fw.dma(xt[:, c0:c0 + 4, H - h0:W],
                       src[s, c0 * 128:(c0 + 4) * 128, t0 - h0:t0 + S].rearrange("(c p) t -> p c t", p=128),
                       reads=rb, writes=xtb[c0:c0 + 4])
            self.pref[(s, ti)] = sx

        def run(self, src, srcb, s, ti, gvec, gb, make_h=True, out_f32=None):
            t0, S = tiles[ti]
            H = self.H
            if (s, ti) not in self.pref:
                self.prefetch(src, srcb, s, ti)
            sx = self.pref.pop((s, ti))
            sl = self.k % self.n
            self.k += 1
            xt, xtb = self.xt[sx], self.xtb[sx]
            W = S + H
            for half in range(2):
                c0 = half * 4
                fw.op("act", lambda e: e.activation(out=self.sq[:, c0:c0 + 4, 0:W], in_=xt[:, c0:c0 + 4, 0:W], func=AF.Square),
                      reads=xtb[c0:c0 + 4], writes=[self.sqb])
            pa, pb = bank()
            for c in range(8):
                fw.op("pe", lambda e: e.matmul(pa[:, 0:W], ones_bf[:], self.sq[:, c, 0:W], start=(c == 0), stop=(c == 7)),
                      reads=[self.sqb, cb], writes=[pb])
            rs, rsb = self.rs[sl], self.rsb[sl]
            fw.op("act", lambda e: e.activation(out=rs[:, 0:W], in_=pa[:, 0:W], func=AF.Sqrt, scale=1.0 / D, bias=EPS),
                  reads=[pb], writes=[rsb])
            fw.op("dve", lambda e: e.reciprocal(out=rs[:, 0:W], in_=rs[:, 0:W]), reads=[rsb], writes=[rsb])
            if make_h:
                hb, hbb = self.hb[sl], self.hbb[sl]
                for c in range(8):
                    fw.op("dve", lambda e: e.scalar_tensor_tensor(out=hb[:, c, 0:W], in0=xt[:, c, 0:W], scalar=gvec[:, c:c + 1],
                                                                  in1=rs[:, 0:W], op0=ALU.mult, op1=ALU.mult),
                          reads=[xtb[c], rsb, gb], writes=[hbb])
            return sl, sx

    wl_hist = []

    def load_w_cast(dst, src, blist, rows_per=None):
        b = Buf()
        gate = [wl_hist[-4]] if len(wl_hist) >= 4 else []
        wl_hist.append(b)
        blist.append(b)
        fw.dma(dst, src, reads=gate, writes=[b], q="pool")

    def prologue(dst, dstb):
        with ExitStack() as es:
            xin = [es.enter_context(sbt("pxin%d" % i, [128, 4, D], F32)) for i in range(2)]
            xinb = [Buf() for _ in range(2)]
            st = [es.enter_context(sbt("pst%d" % i, [128, 8, 512], F32)) for i in range(2)]
            stb = [Buf() for _ in range(2)]
            mt = es.enter_context(sbt("pmeta", [16, D], F32))
            mtb = Buf()
            fw.dma(mt[:], meta_in[:, :], writes=[mtb])
            allb = [b for s in range(NSEQ) for b in dstb[s]]
            pa, pb = bank()
            for c in range(8):
                fw.op("pe", lambda e: e.transpose(pa[:, c * 16:(c + 1) * 16], mt[:, c * 128:(c + 1) * 128], ident[0:16, 0:16]),
                      reads=[mtb, cb], writes=[pb])
            fw.op("dve", lambda e: e.tensor_copy(st[0][:, :, 0:16], pa[:, 0:128].rearrange("p (c t) -> p c t", c=8)),
                  reads=[pb], writes=[stb[0]])
            for s in range(NSEQ):
                fw.dma(dst[s, :, 0:16].rearrange("(c p) t -> p c t", p=128), st[0][:, :, 0:16], reads=[stb[0]], writes=[Buf()])
            k = 0
            for s in range(NSEQ):
                for t0 in range(0, SEQ, 512):
                    n = min(512, SEQ - t0)
                    nb = n // 128
                    sl = k % 2
                    k += 1
                    fw.dma(xin[sl][:, 0:nb, :], x_in[s, t0:t0 + n, :].rearrange("(b p) d -> p b d", p=128), writes=[xinb[sl]])
                    for c in range(8):
                        pa, pb = bank()
                        for b_ in range(nb):
                            fw.op("pe", lambda e: e.transpose(pa[:, b_ * 128:(b_ + 1) * 128], xin[sl][:, b_, c * 128:(c + 1) * 128], ident[:]),
                                  reads=[xinb[sl], cb], writes=[pb])
                        eng = "dve" if c % 2 == 0 else "act"
                        if eng == "dve":
                            fw.op("dve", lambda e: e.tensor_copy(st[sl][:, c, 0:n], pa[:, 0:n]), reads=[pb], writes=[stb[sl]])
                        else:
                            fw.op("act", lambda e: e.activation(out=st[sl][:, c, 0:n], in_=pa[:, 0:n], func=AF.Copy), reads=[pb], writes=[stb[sl]])
                    fw.dma(dst[s, :, NMETA + t0:NMETA + t0 + n].rearrange("(c p) t -> p c t", p=128), st[sl][:, :, 0:n],
                           reads=[stb[sl]], writes=[Buf()])
        fw.barrier()

    def even_phase(j, src, srcb, dst, dstb):
        H = 3
        with ExitStack() as es:
            A = StageA(es, H)
            g_t, g_b = load_vec(es, "ev_g", ev_norm[j], 8)
            win = es.enter_context(sbt("ev_win", [128, 8, EVEN_IN], BF16))
            winb = []
            for c in range(8):
                load_w_cast(win[:, c, :], ev_w_in[j, c * 128:(c + 1) * 128, :], winb)
            wout = es.enter_context(sbt("ev_wout", [128, 8, D], BF16))
            woutb = []
            for c in range(8):
                load_w_cast(wout[:, c, :], ev_w_out[j, c * 128:(c + 1) * 128, :], woutb)
            wg = es.enter_context(sbt("ev_wg", [128, 2, 4, 128], BF16))
            wgb = Buf()
            fw.op("pool", lambda e: e.memset(wg[:], 0.0), writes=[wgb])
            for gi, gw in enumerate((ev_gate_r_w, ev_gate_i_w)):
                for c in range(4):
                    for hh in range(2):
                        fw.dma(wg[hh * 64:(hh + 1) * 64, gi, c, hh * 64:(hh + 1) * 64], gw[j, 2 * c + hh, :, :], writes=[wgb], q="pool")
            cva = es.enter_context(sbt("ev_cva", [128, 4, 3], F32))
            cvb = es.enter_context(sbt("ev_cvb", [128, 4, 4], F32))
            vb = Buf()
            for kk in range(3):
                fw.dma(cva[:, :, kk], ev_conv_a[j, kk].rearrange("(c p) -> p c", p=128), writes=[vb], allow_slow_non_contiguous=True)
            for kk in range(4):
                fw.dma(cvb[:, :, kk], ev_conv_b[j, kk].rearrange("(c p) -> p c", p=128), writes=[vb], allow_slow_non_contiguous=True)
            bbias, _b1 = load_vec(es, "ev_bb", ev_conv_b_bias[j], 4)
            rbias, _b2 = load_vec(es, "ev_rb", ev_gate_r_b[j], 4)
            ibias, _b3 = load_vec(es, "ev_ib", ev_gate_i_b[j], 4)
            lam, _b4 = load_vec(es, "ev_lam", ev_lam[j], 4)
            tv = [es.enter_context(sbt("ev_tv%d" % i, [128, 4], F32)) for i in range(5)]
            ca1 = es.enter_context(sbt("ev_ca1", [128, 4], F32))
            ca2 = es.enter_context(sbt("ev_ca2", [128, 4], F32))
            cab = Buf()
            y_, ay, e_, z_, p_ = tv
            def dv(fn, rd=()):
                fw.op("dve", fn, reads=[cab] + list(rd), writes=[cab])
            dv(lambda e: e.tensor_scalar(out=y_[:], in0=lam[:], scalar1=-1.0, scalar2=None, op0=ALU.mult), [_b4])
            dv(lambda e: e.tensor_tensor(out=ay[:], in0=y_[:], in1=lam[:], op=ALU.max), [_b4])
            fw.op("act", lambda e: e.activation(out=e_[:], in_=ay[:], func=AF.Exp, scale=-1.0), reads=[cab], writes=[cab])
            dv(lambda e: e.tensor_scalar(out=z_[:], in0=e_[:], scalar1=2.0, scalar2=None, op0=ALU.add))
            dv(lambda e: e.reciprocal(out=z_[:], in_=z_[:]))
            dv(lambda e: e.tensor_tensor(out=z_[:], in0=z_[:], in1=e_[:], op=ALU.mult))
            dv(lambda e: e.tensor_tensor(out=e_[:], in0=z_[:], in1=z_[:], op=ALU.mult))
            dv(lambda e: e.memset(p_[:], 1.0 / 15.0))
            for kk in (13, 11, 9, 7, 5, 3, 1):
                dv(lambda e: e.tensor_tensor(out=p_[:], in0=p_[:], in1=e_[:], op=ALU.mult))
                dv(lambda e: e.tensor_scalar(out=p_[:], in0=p_[:], scalar1=1.0 / kk, scalar2=None, op0=ALU.add))
            dv(lambda e: e.tensor_tensor(out=p_[:], in0=p_[:], in1=z_[:], op=ALU.mult))
            dv(lambda e: e.tensor_scalar(out=y_[:], in0=y_[:], scalar1=0.0, scalar2=None, op0=ALU.max))
            dv(lambda e: e.scalar_tensor_tensor(out=y_[:], in0=p_[:], scalar=2.0, in1=y_[:], op0=ALU.mult, op1=ALU.add))
            dv(lambda e: e.tensor_scalar(out=ca1[:], in0=y_[:], scalar1=-8.0, scalar2=None, op0=ALU.mult))
            dv(lambda e: e.tensor_scalar(out=ca2[:], in0=y_[:], scalar1=-16.0, scalar2=None, op0=ALU.mult))
            vecb = [vb, _b1, _b2, _b3, cab]

            WM = SM + H
            def sb1(name, dt=F32, shape=None):
                return es.enter_context(sbt(name, shape or [128, 4, WM], dt))
            gcs = sb1("gcs"); gcsb = bufs(4)
            cvt = sb1("cvat"); cvtb = bufs(4)
            xc = sb1("xc"); xcb = bufs(4)
            xcbf = sb1("xcbf", BF16); xcbfb = bufs(4)
            rr = sb1("rr"); rrb = bufs(4)
            ii = sb1("ii"); iib = bufs(4)
            mm = sb1("mm"); mmb = bufs(4)
            gg = sb1("gg"); ggb = bufs(4)
            hs = sb1("hs"); hsb = bufs(4)
            carry = es.enter_context(sbt("carry", [128, 4], F32)); carb = bufs(4)
            ymix = [es.enter_context(sbt("ymix%d" % i, [128, 8, SM], BF16)) for i in range(2)]
            ymb = [bufs(8) for _ in range(2)]

            order = [(s, ti) for s in range(NSEQ) for ti in range(NT)]
            for o_ in order[0:2]:
                A.prefetch(src, srcb, o_[0], o_[1])
            slot_next = A.run(src, srcb, order[0][0], order[0][1], g_t, g_b)
            for oi, (s, ti) in enumerate(order):
                t0, S = tiles[ti]
                W = S + H
                sl = slot_next
                if oi + 2 < len(order):
                    A.prefetch(src, srcb, order[oi + 2][0], order[oi + 2][1])
                if oi + 1 < len(order):
                    slot_next = A.run(src, srcb, order[oi + 1][0], order[oi + 1][1], g_t, g_b)
                hb, hbb, xt, xtb = A.hb[sl[0]], A.hbb[sl[0]], A.xt[sl[1]], A.xtb[sl[1]]
                z = oi % 2

                def proj_t(T_, col0):
                    W_, hb_, hbb_ = T_["W"], T_["hb"], T_["hbb"]
                    pa, pb = bank()
                    for c in range(8):
                        fw.op("pe", lambda e: e.matmul(pa[:, 0:W_], win[:, c, col0:col0 + 128], hb_[:, c, 0:W_], start=(c == 0), stop=(c == 7)),
                              reads=[hbb_] + winb, writes=[pb])
                    return pa, pb

                def gate_branch(T_):
                    S_, W_ = T_["S"], T_["W"]
                    for c in range(4):
                        pa, pb = proj_t(T_, 2048 + c * 128)
                        fw.op("act", lambda e: e.activation(out=gg[:, c, 0:S_], in_=pa[:, H:W_], func=AF.Gelu_apprx_tanh),
                              reads=[pb], writes=[ggb[c]])

                def mixer_a(T_, c):
                    S_, W_, z_ = T_["S"], T_["W"], T_["z"]
                    pa, pb = proj_t(T_, 512 + c * 128)
                    fw.op("act", lambda e: e.activation(out=gcs[:, c, 0:W_], in_=pa[:, 0:W_], func=AF.Copy), reads=[pb], writes=[gcsb[c]])
                    pa, pb = proj_t(T_, 1024 + c * 128)
                    fw.op("dve", lambda e: e.tensor_tensor(out=gcs[:, c, 0:W_], in0=pa[:, 0:W_], in1=gcs[:, c, 0:W_], op=ALU.mult),
                          reads=[pb, gcsb[c]], writes=[gcsb[c]])
                    fw.op("act", lambda e: e.activation(out=cvt[:, c, 0:S_], in_=gcs[:, c, H:W_], func=AF.Identity, scale=cva[:, c, 2:3]),
                          reads=[gcsb[c]] + vecb, writes=[cvtb[c]])
                    for kk in (1, 0):
                        sh = 2 - kk
                        fw.op("dve", lambda e: e.scalar_tensor_tensor(out=cvt[:, c, 0:S_], in0=gcs[:, c, H - sh:W_ - sh], scalar=cva[:, c, kk:kk + 1],
                                                                      in1=cvt[:, c, 0:S_], op0=ALU.mult, op1=ALU.add),
                              reads=[gcsb[c], cvtb[c]] + vecb, writes=[cvtb[c]])
                    pa, pb = proj_t(T_, c * 128)
                    fw.op("dve", lambda e: e.tensor_tensor(out=ymix[z_][:, c, 0:S_], in0=pa[:, H:W_], in1=cvt[:, c, 0:S_], op=ALU.mult),
                          reads=[pb, cvtb[c]], writes=[ymb[z_][c]])

                PREA = 1
                cur = dict(S=S, W=W, hb=hb, hbb=hbb, z=z)

                def proj(col0):
                    return proj_t(cur, col0)

                if oi == 0:
                    gate_branch(cur)
                for c in range(0 if oi == 0 else PREA, 4):
                    mixer_a(cur, c)
                for c in range(4):
                    pa, pb = proj(1536 + c * 128)
                    fw.op("act", lambda e: e.activation(out=xc[:, c, 0:S], in_=pa[:, H:W], func=AF.Identity, scale=cvb[:, c, 3:4], bias=bbias[:, c:c + 1]),
                          reads=[pb] + vecb, writes=[xcb[c]])
                    for kk in (2, 1, 0):
                        sh = 3 - kk
                        fw.op("dve", lambda e: e.scalar_tensor_tensor(out=xc[:, c, 0:S], in0=pa[:, H - sh:W - sh], scalar=cvb[:, c, kk:kk + 1],
                                                                      in1=xc[:, c, 0:S], op0=ALU.mult, op1=ALU.add),
                              reads=[pb, xcb[c]] + vecb, writes=[xcb[c]])
                    fw.op("act", lambda e: e.activation(out=xcbf[:, c, 0:S], in_=xc[:, c, 0:S], func=AF.Copy), reads=[xcb[c]], writes=[xcbfb[c]])
                for c in range(4):
                    for gi, (dstt, dstb_, bias_t) in enumerate(((rr, rrb, rbias), (ii, iib, ibias))):
                        pa, pb = bank()
                        fw.op("pe", lambda e: e.matmul(pa[:, 0:S], wg[:, gi, c, :], xcbf[:, c, 0:S], start=True, stop=True),
                              reads=[xcbfb[c], wgb], writes=[pb])
                        fw.op("act", lambda e: e.activation(out=dstt[:, c, 0:S], in_=pa[:, 0:S], func=AF.Sigmoid, bias=bias_t[:, c:c + 1]),
                              reads=[pb] + vecb, writes=[dstb_[c]])
                for c in range(4):
                    fw.op("act", lambda e: e.activation(out=mm[:, c, 0:S], in_=rr[:, c, 0:S], func=AF.Exp, scale=ca2[:, c:c + 1]),
                          reads=[rrb[c]] + vecb, writes=[mmb[c]])
                    fw.op("act", lambda e: e.activation(out=rr[:, c, 0:S], in_=rr[:, c, 0:S], func=AF.Exp, scale=ca1[:, c:c + 1]),
                          reads=[rrb[c]] + vecb, writes=[rrb[c]])
                for c in range(4):
                    fw.op("act", lambda e: e.activation(out=mm[:, c, 0:S], in_=mm[:, c, 0:S], func=AF.Sqrt, scale=-1.0, bias=1.0),
                          reads=[mmb[c]], writes=[mmb[c]])
                for c in range(4):
                    fw.op("dve", lambda e: e.tensor_tensor(out=ii[:, c, 0:S], in0=ii[:, c, 0:S], in1=xc[:, c, 0:S], op=ALU.mult),
                          reads=[iib[c], xcb[c]], writes=[iib[c]])
                for c in range(4):
                    fw.op("dve", lambda e: e.tensor_tensor(out=ii[:, c, 0:S], in0=ii[:, c, 0:S], in1=mm[:, c, 0:S], op=ALU.mult),
                          reads=[iib[c], mmb[c]], writes=[iib[c]])
                for c in range(4):
                    init = 0.0 if ti == 0 else carry[:, c:c + 1]
                    fw.op("dve", lambda e: e.tensor_tensor_scan(out=hs[:, c, 0:S], data0=rr[:, c, 0:S], data1=ii[:, c, 0:S],
                                                                initial=init, op0=ALU.mult, op1=ALU.add),
                          reads=[rrb[c], iib[c], carb[c]], writes=[hsb[c]])
                for c in range(4):
                    fw.op("pool", lambda e: e.tensor_copy(carry[:, c:c + 1], hs[:, c, S - 1:S]), reads=[hsb[c]], writes=[carb[c]])
                    fw.op("dve", lambda e: e.tensor_tensor(out=ymix[z][:, 4 + c, 0:S], in0=gg[:, c, 0:S], in1=hs[:, c, 0:S], op=ALU.mult),
                          reads=[ggb[c], hsb[c]], writes=[ymb[z][4 + c]])
                if oi + 1 < len(order):
                    Sn = tiles[order[oi + 1][1]][1]
                    nxt = dict(S=Sn, W=Sn + H, hb=A.hb[slot_next[0]], hbb=A.hbb[slot_next[0]], z=(oi + 1) % 2)
                    gate_branch(nxt)
                    for c in range(PREA):
                        mixer_a(nxt, c)
                for m in range(8):
                    pa, pb = bank()
                    for c in range(8):
                        fw.op("pe", lambda e: e.matmul(pa[:, 0:S], wout[:, c, m * 128:(m + 1) * 128], ymix[z][:, c, 0:S], start=(c == 0), stop=(c == 7)),
                              reads=[ymb[z][c]] + woutb, writes=[pb])
                    fw.op("dve", lambda e: e.tensor_tensor(out=xt[:, m, H:W], in0=xt[:, m, H:W], in1=pa[:, 0:S], op=ALU.add),
                          reads=[pb, xtb[m]], writes=[xtb[m]])
                for half in range(2):
                    c0 = half * 4
                    fw.dma(dst[s, c0 * 128:(c0 + 4) * 128, t0:t0 + S].rearrange("(c p) t -> p c t", p=128), xt[:, c0:c0 + 4, H:W],
                           reads=xtb[c0:c0 + 4], writes=[dstb[s][ti]], q="pool")
        fw.barrier()

    def ffn_phase(l, half, src, srcb, dst, dstb):
        H = 2
        NJ = 11
        j0 = half * NJ
        with ExitStack() as es:
            A = StageA(es, H)
            g_t, g_b = load_vec(es, "ff_g", ffn_norm[l], 8)
            wup = es.enter_context(sbt("ff_wup", [128, 8, 2, NJ * 128], BF16))
            JH = ((0, 6), (6, NJ))
            wupb = [[], []]
            for jh, (ja, jb) in enumerate(JH):
                for c in range(8):
                    for ag in range(2):
                        col = ag * DFF + (j0 + ja) * 128
                        load_w_cast(wup[:, c, ag, ja * 128:jb * 128], ffn_w_up[l, c * 128:(c + 1) * 128, col:col + (jb - ja) * 128], wupb[jh])
            wdn = es.enter_context(sbt("ff_wdn", [128, NJ, D], BF16))
            wdnb = []
            for jj in range(NJ):
                load_w_cast(wdn[:, jj, :], ffn_w_down[l, (j0 + jj) * 128:(j0 + jj + 1) * 128, :], wdnb)
            cw = es.enter_context(sbt("ff_cw", [128, 2, NJ, 3], F32))
            cbias = es.enter_context(sbt("ff_cb", [128, 2, NJ], F32))
            vb = Buf()
            for ag in range(2):
                col = ag * DFF + j0 * 128
                for kk in range(3):
                    fw.dma(cw[:, ag, :, kk], ffn_conv_w[l, kk, col:col + NJ * 128].rearrange("(c p) -> p c", p=128), writes=[vb], allow_slow_non_contiguous=True)
                fw.dma(cbias[:, ag, :], ffn_conv_b[l, col:col + NJ * 128].rearrange("(c p) -> p c", p=128), writes=[vb], allow_slow_non_contiguous=True)
            cvt = [[es.enter_context(sbt("ff_cv%d_%d" % (ag, i), [128, SM], F32)) for i in range(3)] for ag in range(2)]
            cvtb = [bufs(3) for _ in range(2)]
            sa = [es.enter_context(sbt("ff_sa%d" % i, [128, SM], F32)) for i in range(3)]
            sab = bufs(3)
            mb = [es.enter_context(sbt("ff_m%d" % i, [128, NJ, SM], BF16)) for i in range(2)]
            mbb = [bufs(NJ) for _ in range(2)]
            if half == 1:
                xr = [es.enter_context(sbt("ff_xr%d" % i, [128, 8, SM], F32)) for i in range(2)]
                xrb = [bufs(8) for _ in range(2)]
            qs = {"q": 0}
            PRE = 3

            def up_unit(S, W, hb, hbb, z, jj):
                r = qs["q"] % 3
                qs["q"] += 1
                pab = []
                for ag in range(2):
                    pa, pb = bank()
                    for c in range(8):
                        fw.op("pe", lambda e: e.matmul(pa[:, 0:W], wup[:, c, ag, jj * 128:(jj + 1) * 128], hb[:, c, 0:W], start=(c == 0), stop=(c == 7)),
                              reads=[hbb] + wupb[0 if jj < 6 else 1], writes=[pb])
                    pab.append((pa, pb))
                for ag in range(2):
                    pa, pb = pab[ag]
                    fw.op("act", lambda e: e.activation(out=cvt[ag][r][:, 0:S], in_=pa[:, H:W], func=AF.Identity, scale=cw[:, ag, jj, 2:3], bias=cbias[:, ag, jj:jj + 1]),
                          reads=[pb, vb], writes=[cvtb[ag][r]])
                for kk in (1, 0):
                    sh = 2 - kk
                    for ag in range(2):
                        pa, pb = pab[ag]
                        fw.op("dve", lambda e: e.scalar_tensor_tensor(out=cvt[ag][r][:, 0:S], in0=pa[:, H - sh:W - sh], scalar=cw[:, ag, jj, kk:kk + 1],
                                                                      in1=cvt[ag][r][:, 0:S], op0=ALU.mult, op1=ALU.add),
                              reads=[pb, cvtb[ag][r], vb], writes=[cvtb[ag][r]])
                fw.op("act", lambda e: e.activation(out=sa[r][:, 0:S], in_=cvt[0][r][:, 0:S], func=AF.Silu), reads=[cvtb[0][r]], writes=[sab[r]])
                fw.op("pool", lambda e: e.tensor_tensor(out=mb[z][:, jj, 0:S], in0=sa[r][:, 0:S], in1=cvt[1][r][:, 0:S], op=ALU.mult),
                      reads=[sab[r], cvtb[1][r]], writes=[mbb[z][jj]])

            order = [(s, ti) for s in range(NSEQ) for ti in range(NT)]
            for o_ in order[0:2]:
                A.prefetch(src, srcb, o_[0], o_[1])
            slot_next = A.run(src, srcb, order[0][0], order[0][1], g_t, g_b)
            for oi, (s, ti) in enumerate(order):
                t0, S = tiles[ti]
                W = S + H
                sl = slot_next
                if oi + 2 < len(order):
                    A.prefetch(src, srcb, order[oi + 2][0], order[oi + 2][1])
                z = oi % 2
                if half == 1:
                    for hf in range(2):
                        c0 = hf * 4
                        fw.dma(xr[z][:, c0:c0 + 4, 0:S], dst[s, c0 * 128:(c0 + 4) * 128, t0:t0 + S].rearrange("(c p) t -> p c t", p=128),
                               reads=[dstb[s][ti]], writes=xrb[z][c0:c0 + 4])
                if oi + 1 < len(order):
                    slot_next = A.run(src, srcb, order[oi + 1][0], order[oi + 1][1], g_t, g_b)
                hb, hbb, xt, xtb = A.hb[sl[0]], A.hbb[sl[0]], A.xt[sl[1]], A.xtb[sl[1]]
                for jj in range(PRE if oi > 0 else 0, NJ):
                    up_unit(S, W, hb, hbb, z, jj)
                if oi + 1 < len(order):
                    Sn = tiles[order[oi + 1][1]][1]
                    for jj in range(PRE):
                        up_unit(Sn, Sn + H, A.hb[slot_next[0]], A.hbb[slot_next[0]], (oi + 1) % 2, jj)
                for m in range(8):
                    pa, pb = bank()
                    for jj in range(NJ):
                        fw.op("pe", lambda e: e.matmul(pa[:, 0:S], wdn[:, jj, m * 128:(m + 1) * 128], mb[z][:, jj, 0:S], start=(jj == 0), stop=(jj == NJ - 1)),
                              reads=[mbb[z][jj]] + wdnb, writes=[pb])
                    if half == 0:
                        fw.op("dve", lambda e: e.tensor_tensor(out=xt[:, m, H:W], in0=xt[:, m, H:W], in1=pa[:, 0:S], op=ALU.add),
                              reads=[pb, xtb[m]], writes=[xtb[m]])
                    else:
                        fw.op("dve", lambda e: e.tensor_tensor(out=xr[z][:, m, 0:S], in0=xr[z][:, m, 0:S], in1=pa[:, 0:S], op=ALU.add),
                              reads=[pb, xrb[z][m]], writes=[xrb[z][m]])
                for hf in range(2):
                    c0 = hf * 4
                    if half == 0:
                        fw.dma(dst[s, c0 * 128:(c0 + 4) * 128, t0:t0 + S].rearrange("(c p) t -> p c t", p=128), xt[:, c0:c0 + 4, H:W],
                               reads=xtb[c0:c0 + 4], writes=[dstb[s][ti]], q="pool")
                    else:
                        fw.dma(dst[s, c0 * 128:(c0 + 4) * 128, t0:t0 + S].rearrange("(c p) t -> p c t", p=128), xr[z][:, c0:c0 + 4, 0:S],
                               reads=xrb[z][c0:c0 + 4], writes=[dstb[s][ti]], q="pool")
        fw.barrier()

    def odd_proj_phase(j, src, srcb):
        with ExitStack() as es:
            A = StageA(es, 0)
            g_t, g_b = load_vec(es, "od_g", od_norm[j], 8)
            gq, gqb = load_vec(es, "od_gq", od_q_norm[j], 3)
            gkv, gkvb = load_vec(es, "od_gkv", od_kv_norm[j], 2)
            win = es.enter_context(sbt("od_win", [128, 8, ODD_IN + 32], BF16))
            winb = []
            for c in range(8):
                rows = od_w_in[j, c * 128:(c + 1) * 128, :]
                load_w_cast(win[:, c, 0:ODD_IN], rows, winb)
                load_w_cast(win[:, c, ODD_IN:ODD_IN + 16], rows[:, 656:672], winb)
                load_w_cast(win[:, c, ODD_IN + 16:ODD_IN + 32], rows[:, 640:656], winb)
            wqn = es.enter_context(sbt("od_wqn", [128, 3, NH, QKN], BF16))
            wqr = es.enter_context(sbt("od_wqr", [128, 3, 2, NH, QKR], BF16))
            wqb = []
            for c in range(3):
                rows = od_w_uq[j, c * 128:(c + 1) * 128, :].rearrange("p (h e) -> p h e", h=NH)
                load_w_cast(wqn[:, c, :, :], rows[:, :, 0:64], wqb)
                load_w_cast(wqr[:, c, 0, :, :], rows[:, :, 64:96], wqb)
                load_w_cast(wqr[:, c, 1, :, 0:16], rows[:, :, 80:96], wqb)
                load_w_cast(wqr[:, c, 1, :, 16:32], rows[:, :, 64:80], wqb)
            wkn = es.enter_context(sbt("od_wkn", [128, 2, NH, QKN], BF16))
            wv = es.enter_context(sbt("od_wv", [128, 2, NH, VH], BF16))
            wkvb = []
            for c in range(2):
                rows = od_w_ukv[j, c * 128:(c + 1) * 128, :].rearrange("p (h e) -> p h e", h=NH)
                load_w_cast(wkn[:, c, :, :], rows[:, :, 0:64], wkvb)
                load_w_cast(wv[:, c, :, :], rows[:, :, 64:128], wkvb)
            cs = es.enter_context(sbt("od_cs", [128, 2, T], F32))
            csb = Buf()
            fw.dma(cs[:, 0, :], rope_in[:, 0, :], writes=[csb])
            fw.dma(cs[:, 1, :], rope_in[:, 1, :], writes=[csb])
            S_ = SM
            def sb(name, shape, dt=F32, n=2):
                return [es.enter_context(sbt("%s%d" % (name, i), shape, dt)) for i in range(n)]
            cl = sb("od_cl", [128, 5, S_]); clb = [bufs(5) for _ in range(2)]
            csq = sb("od_csq", [128, 5, S_], BF16); csqb = [bufs(5) for _ in range(2)]
            rsq = sb("od_rsq", [128, 2, S_]); rsqb = [bufs(2) for _ in range(2)]
            cn = sb("od_cn", [128, 5, S_], BF16); cnb = [bufs(5) for _ in range(2)]
            krt = sb("od_kr", [32, 3, S_]); krb = [Buf() for _ in range(2)]
            krbf = sb("od_krbf", [32, S_], BF16); krbfb = [Buf() for _ in range(2)]
            qst = sb("od_qst", [128, S_], BF16, n=4); qstb = bufs(4)
            rt = sb("od_rt", [128, 2, S_], F32, n=2); rtb = bufs(2)
            vst = sb("od_vst", [128, NH * VH], BF16, n=2); vstb = bufs(2)
            k = 0
            qi = 0
            ri = 0
            vi = 0
            order = [(s, ti) for s in range(NSEQ) for ti in range(NT)]
            for o_ in order[0:2]:
                A.prefetch(src, srcb, o_[0], o_[1])
            slot_next = A.run(src, srcb, order[0][0], order[0][1], g_t, g_b)
            for oi, (s, ti) in enumerate(order):
                if True:
                    t0, S = tiles[ti]
                    sl = slot_next
                    if oi + 2 < len(order):
                        A.prefetch(src, srcb, order[oi + 2][0], order[oi + 2][1])
                    if oi + 1 < len(order):
                        slot_next = A.run(src, srcb, order[oi + 1][0], order[oi + 1][1], g_t, g_b)
                    hb, hbb = A.hb[sl[0]], A.hbb[sl[0]]
                    z = k % 2
                    k += 1
                    for c in range(5):
                        pa, pb = bank()
                        for kc in range(8):
                            fw.op("pe", lambda e: e.matmul(pa[:, 0:S], win[:, kc, c * 128:(c + 1) * 128], hb[:, kc, 0:S], start=(kc == 0), stop=(kc == 7)),
                                  reads=[hbb] + winb, writes=[pb])
                        fw.op("dve", lambda e: e.tensor_copy(cl[z][:, c, 0:S], pa[:, 0:S]), reads=[pb], writes=[clb[z][c]])
                        fw.op("act", lambda e: e.activation(out=csq[z][:, c, 0:S], in_=cl[z][:, c, 0:S], func=AF.Square), reads=[clb[z][c]], writes=[csqb[z][c]])
                    pk = []
                    for v in range(2):
                        pa, pb = bank()
                        col = 640 if v == 0 else ODD_IN
                        for kc in range(8):
                            fw.op("pe", lambda e: e.matmul(pa[0:32, 0:S], win[:, kc, col:col + 32], hb[:, kc, 0:S], start=(kc == 0), stop=(kc == 7)),
                                  reads=[hbb] + winb, writes=[pb])
                        pk.append((pa, pb))
                    fw.op("dve", lambda e: e.tensor_tensor(out=krt[z][:, 0, 0:S], in0=pk[0][0][0:32, 0:S], in1=cs[0:32, 0, t0:t0 + S], op=ALU.mult),
                          reads=[pk[0][1], csb], writes=[krb[z]])
                    fw.op("dve", lambda e: e.tensor_tensor(out=krt[z][:, 1, 0:S], in0=pk[1][0][0:32, 0:S], in1=cs[0:32, 1, t0:t0 + S], op=ALU.mult),
                          reads=[pk[1][1], csb], writes=[krb[z]])
                    fw.op("dve", lambda e: e.tensor_tensor(out=krbf[z][:, 0:S], in0=krt[z][:, 0, 0:S], in1=krt[z][:, 1, 0:S], op=ALU.add),
                          reads=[krb[z]], writes=[krbfb[z]])
                    fw.dma(KR[s, :, t0:t0 + S], krbf[z][:, 0:S], reads=[krbfb[z]], writes=[Buf()])
                    for li, (c0, ncx, dim, gv, gvb) in enumerate(((0, 3, QL, gq, gqb), (3, 2, KVL, gkv, gkvb))):
                        pa, pb = bank()
                        for c in range(ncx):
                            fw.op("pe", lambda e: e.matmul(pa[:, 0:S], ones_bf[:], csq[z][:, c0 + c, 0:S], start=(c == 0), stop=(c == ncx - 1)),
                                  reads=[csqb[z][c0 + c], cb], writes=[pb])
                        fw.op("act", lambda e: e.activation(out=rsq[z][:, li, 0:S], in_=pa[:, 0:S], func=AF.Sqrt, scale=1.0 / dim, bias=EPS),
                              reads=[pb], writes=[rsqb[z][li]])
                        fw.op("dve", lambda e: e.reciprocal(out=rsq[z][:, li, 0:S], in_=rsq[z][:, li, 0:S]), reads=[rsqb[z][li]], writes=[rsqb[z][li]])
                        for c in range(ncx):
                            fw.op("dve", lambda e: e.scalar_tensor_tensor(out=cn[z][:, c0 + c, 0:S], in0=cl[z][:, c0 + c, 0:S], scalar=gv[:, c:c + 1],
                                                                          in1=rsq[z][:, li, 0:S], op0=ALU.mult, op1=ALU.mult),
                                  reads=[clb[z][c0 + c], rsqb[z][li], gvb], writes=[cnb[z][c0 + c]])
                    for m in range(8):
                        pa, pb = bank()
                        for c in range(3):
                            fw.op("pe", lambda e: e.matmul(pa[:, 0:S], wqn[:, c, 2 * m:2 * m + 2, :], cn[z][:, c, 0:S], start=(c == 0), stop=(c == 2)),
                                  reads=[cnb[z][c]] + wqb, writes=[pb])
                        u = qi % 4
                        qi += 1
                        fw.op("act", lambda e: e.activation(out=qst[u][:, 0:S], in_=pa[:, 0:S], func=AF.Copy), reads=[pb], writes=[qstb[u]])
                        fw.dma(QS[s, m * 128:(m + 1) * 128, t0:t0 + S], qst[u][:, 0:S], reads=[qstb[u]], writes=[Buf()])
                    for m in range(4):
                        pr = []
                        for v in range(2):
                            pa, pb = bank()
                            for c in range(3):
                                fw.op("pe", lambda e: e.matmul(pa[:, 0:S], wqr[:, c, v, 4 * m:4 * m + 4, :], cn[z][:, c, 0:S], start=(c == 0), stop=(c == 2)),
                                      reads=[cnb[z][c]] + wqb, writes=[pb])
                            pr.append((pa, pb))
                        w_ = ri % 2
                        ri += 1
                        u = qi % 4
                        qi += 1
                        fw.op("dve", lambda e: e.tensor_tensor(out=rt[w_][:, 0, 0:S], in0=pr[0][0][:, 0:S], in1=cs[:, 0, t0:t0 + S], op=ALU.mult),
                              reads=[pr[0][1], csb], writes=[rtb[w_]])
                        fw.op("dve", lambda e: e.tensor_tensor(out=rt[w_][:, 1, 0:S], in0=pr[1][0][:, 0:S], in1=cs[:, 1, t0:t0 + S], op=ALU.mult),
                              reads=[pr[1][1], csb], writes=[rtb[w_]])
                        fw.op("dve", lambda e: e.tensor_tensor(out=qst[u][:, 0:S], in0=rt[w_][:, 0, 0:S], in1=rt[w_][:, 1, 0:S], op=ALU.add),
                              reads=[rtb[w_]], writes=[qstb[u]])
                        fw.dma(QR[s, m * 128:(m + 1) * 128, t0:t0 + S], qst[u][:, 0:S], reads=[qstb[u]], writes=[Buf()])
                    for m in range(8):
                        pa, pb = bank()
                        for c in range(2):
                            fw.op("pe", lambda e: e.matmul(pa[:, 0:S], wkn[:, c, 2 * m:2 * m + 2, :], cn[z][:, 3 + c, 0:S], start=(c == 0), stop=(c == 1)),
                                  reads=[cnb[z][3 + c]] + wkvb, writes=[pb])
                        u = qi % 4
                        qi += 1
                        fw.op("act", lambda e: e.activation(out=qst[u][:, 0:S], in_=pa[:, 0:S], func=AF.Copy), reads=[pb], writes=[qstb[u]])
                        fw.dma(KS[s, m * 128:(m + 1) * 128, t0:t0 + S], qst[u][:, 0:S], reads=[qstb[u]], writes=[Buf()])
                    for b0 in range(0, S, 128):
                        nb_ = min(128, S - b0)
                        u = vi % 2
                        vi += 1
                        for hv in range(2):
                            pa, pb = bank()
                            for c in range(2):
                                fw.op("pe", lambda e: e.matmul(pa[0:nb_, 0:512], cn[z][:, 3 + c, b0:b0 + nb_], wv[:, c, hv * 8:(hv + 1) * 8, :], start=(c == 0), stop=(c == 1)),
                                      reads=[cnb[z][3 + c]] + wkvb, writes=[pb])
                            if hv == 0:
                                fw.op("act", lambda e: e.activation(out=vst[u][0:nb_, 0:512], in_=pa[0:nb_, 0:512], func=AF.Copy), reads=[pb], writes=[vstb[u]])
                            else:
                                fw.op("dve", lambda e: e.tensor_copy(vst[u][0:nb_, 512:1024], pa[0:nb_, 0:512]), reads=[pb], writes=[vstb[u]])
                        fw.dma(VS[s, t0 + b0:t0 + b0 + nb_, :], vst[u][0:nb_, :], reads=[vstb[u]], writes=[Buf()])
        fw.barrier()

    def attn_phase():
        scale = QKH ** -0.5
        NKT = -(-T // 128)
        qtiles = [(q0, min(512, T - q0)) for q0 in range(0, T, 512)]
        with ExitStack() as es:
            qT = [es.enter_context(sbt("at_q%d" % i, [QKH, T], BF16)) for i in range(2)]
            kT = [es.enter_context(sbt("at_k%d" % i, [QKH, T], BF16)) for i in range(2)]
            va = [es.enter_context(sbt("at_v%d" % i, [128, NKT, VH + 1], BF16)) for i in range(2)]
            ib = [Buf() for _ in range(2)]
            ibq, ibq2, ibk, ibk2, ibv2 = (bufs(2) for _ in range(5))
            for i in range(2):
                fw.op("pool", lambda e: e.memset(va[i][:], 1.0), writes=[ib[i], ibv2[i]])
            ot = [es.enter_context(sbt("at_o%d" % i, [64, T], BF16)) for i in range(2)]
            otb = bufs(2)
            NP = 6
            LOOK = 3
            EDEF = 9
            pT = [es.enter_context(sbt("at_pp%d" % i, [128, 512], BF16)) for i in range(NP)]
            pTb = bufs(NP)
            rl = [es.enter_context(sbt("at_rll%d" % i, [128, 512], F32)) for i in range(3)]
            rlb = bufs(3)
            rb_ = [es.enter_context(sbt("at_rbb%d" % i, [64, 512], F32)) for i in range(3)]
            rbb = bufs(3)
            k = 0
            st = {"pi": 0, "ui": 0, "oi": 0}
            pstate["lo"] = 2
            pend = []

            def tick():
                for p_ in pend:
                    p_[0] -= 1
                while pend and pend[0][0] <= 0:
                    pend.pop(0)[1]()

            def flush():
                while pend:
                    pend.pop(0)[1]()

            heads = [(s, h) for s in range(NSEQ) for h in range(NH)]

            def emit_loads(idx):
                s, h = heads[idx]
                z = idx % 2
                fw.dma(qT[z][0:QKN, :], QS[s, h * QKN:(h + 1) * QKN, :], reads=[QKVB[s]], writes=[ibq[z]])
                fw.dma(qT[z][QKN:QKH, :], QR[s, h * QKR:(h + 1) * QKR, :], reads=[QKVB[s]], writes=[ibq2[z]])
                fw.dma(kT[z][0:QKN, :], KS[s, h * QKN:(h + 1) * QKN, :], reads=[QKVB[s]], writes=[ibk[z]])
                fw.dma(kT[z][QKN:QKH, :], KR[s, :, :], reads=[QKVB[s]], writes=[ibk2[z]])
                nfull = T // 128
                if nfull:
                    fw.dma(va[z][:, 0:nfull, 0:VH], VS[s, 0:nfull * 128, h * VH:(h + 1) * VH].rearrange("(n p) v -> p n v", p=128),
                           reads=[QKVB[s]], writes=[ib[z]])
                if T % 128:
                    r_ = T % 128
                    fw.dma(va[z][0:r_, nfull, 0:VH], VS[s, nfull * 128:T, h * VH:(h + 1) * VH], reads=[QKVB[s]], writes=[ibv2[z]])

            units = []
            for idx, (s, h) in enumerate(heads):
                for qi_, (q0, nq) in enumerate(qtiles):
                    kts = [kt for kt in range(NKT) if kt * 128 < q0 + nq]
                    bi = st["oi"] % 2
                    st["oi"] += 1
                    for j_, kt in enumerate(kts):
                        units.append(dict(q0=q0, nq=nq, kt=kt, first=(j_ == 0), last=(j_ == len(kts) - 1), bi=bi, idx=idx, z=idx % 2,
                                          head_last=(qi_ == len(qtiles) - 1 and j_ == len(kts) - 1)))

            def emit_s(u_):
                q0, nq, kt, z = u_["q0"], u_["nq"], u_["kt"], u_["z"]
                inb = [ib[z], ibq[z], ibq2[z], ibk[z], ibk2[z], ibv2[z]]
                k0 = kt * 128
                nk = min(128, T - k0)
                off = max(0, k0 - q0)
                n = nq - off
                pa, pb = bank()
                fw.op("pe", lambda e: e.matmul(pa[0:nk, 0:n], kT[z][:, k0:k0 + nk], qT[z][:, q0 + off:q0 + nq], start=True, stop=True),
                      reads=inb, writes=[pb])
                u = st["pi"] % NP
                st["pi"] += 1
                fw.op("act", lambda e: e.activation(out=pT[u][0:nk, 0:n], in_=pa[0:nk, 0:n], func=AF.Exp, scale=scale),
                      reads=[pb], writes=[pTb[u]])
                if k0 + nk - 1 > q0 + off:
                    nd = min(nk, n)
                    fw.op("pool", lambda e: e.tensor_tensor(out=pT[u][0:nk, 0:nd], in0=pT[u][0:nk, 0:nd], in1=tri[0:nk, 0:nd], op=ALU.mult),
                          reads=[pTb[u], cb], writes=[pTb[u]])
                u_.update(u=u, nk=nk, off=off, n=n)

            def emit_pv(u_):
                q0, nq, kt, u, nk, off, n, z = (u_[x] for x in ("q0", "nq", "kt", "u", "nk", "off", "n", "z"))
                inb = [ib[z], ibq[z], ibq2[z], ibk[z], ibk2[z], ibv2[z]]
                po, pob = ps[:, u_["bi"], :], psb[u_["bi"]]
                if u_["first"]:
                    for p_ in [p_ for p_ in pend if p_[2] == u_["bi"]]:
                        pend.remove(p_)
                        p_[1]()
                fw.op("pe", lambda e: e.matmul(po[0:VH + 1, off:nq], va[z][0:nk, kt, :], pT[u][0:nk, 0:n], start=u_["first"], stop=u_["last"]),
                      reads=[pTb[u]] + inb, writes=[pob])
                if u_["last"]:
                    w_ = st["ui"] % 3
                    st["ui"] += 1
                    fw.op("dve", lambda e: e.reciprocal(out=rl[w_][64:65, 0:nq], in_=po[64:65, 0:nq]), reads=[pob], writes=[rlb[w_]])
                    hl = u_["head_last"]
                    s_, h_ = heads[u_["idx"]]

                    def epi(w_=w_, po=po, pob=pob, q0=q0, nq=nq, zz=z, hl=hl, s_=s_, h_=h_):
                        pa, pb = bank()
                        fw.op("pe", lambda e: e.matmul(pa[0:64, 0:nq], ones_f[64:65, 0:64], rl[w_][64:65, 0:nq], start=True, stop=True),
                              reads=[rlb[w_], cb], writes=[pb])
                        fw.op("act", lambda e: e.activation(out=rb_[w_][:, 0:nq], in_=pa[0:64, 0:nq], func=AF.Copy), reads=[pb], writes=[rbb[w_]])
                        fw.op("dve", lambda e: e.tensor_tensor(out=ot[zz][:, q0:q0 + nq], in0=po[0:64, 0:nq], in1=rb_[w_][:, 0:nq], op=ALU.mult),
                              reads=[pob, rbb[w_]], writes=[otb[zz]])
                        if hl:
                            fw.dma(OS[s_, h_ * VH:(h_ + 1) * VH, :], ot[zz][:, :], reads=[otb[zz]], writes=[Buf()], q="pool")
                    pend.append([EDEF, epi, u_["bi"]])
                if u_["head_last"] and u_["idx"] + 2 < len(heads):
                    emit_loads(u_["idx"] + 2)

            emit_loads(0)
            if len(heads) > 1:
                emit_loads(1)
            nU = len(units)
            for i in range(nU + LOOK):
                if i < nU:
                    emit_s(units[i])
                if i >= LOOK:
                    emit_pv(units[i - LOOK])
                    tick()
            flush()
        pstate["lo"] = 0
        fw.barrier()

    def odd_out_phase(j, src, srcb, dst, dstb):
        with ExitStack() as es:
            wout = es.enter_context(sbt("oo_wout", [128, 8, D], BF16))
            woutb = []
            for c in range(8):
                load_w_cast(wout[:, c, :], od_w_out[j, c * 128:(c + 1) * 128, :], woutb)
            S_ = SM
            xt = [es.enter_context(sbt("oo_xt%d" % i, [128, 8, S_], F32)) for i in range(2)]
            xtb = [bufs(8) for _ in range(2)]
            o_t = [es.enter_context(sbt("oo_o%d" % i, [128, 8, S_], BF16)) for i in range(2)]
            otb = [Buf() for _ in range(2)]
            k = 0
            for s in range(NSEQ):
                for ti in range(NT):
                    t0, S = tiles[ti]
                    z = k % 2
                    k += 1
                    for hf in range(2):
                        c0 = hf * 4
                        fw.dma(xt[z][:, c0:c0 + 4, 0:S], src[s, c0 * 128:(c0 + 4) * 128, t0:t0 + S].rearrange("(c p) t -> p c t", p=128),
                               reads=[srcb[s][ti]], writes=xtb[z][c0:c0 + 4])
                    fw.dma(o_t[z][:, :, 0:S], OS[s, :, t0:t0 + S].rearrange("(c p) t -> p c t", p=128), reads=[OSB[s]], writes=[otb[z]])
                    for m in range(8):
                        pa, pb = bank()
                        for c in range(8):
                            fw.op("pe", lambda e: e.matmul(pa[:, 0:S], wout[:, c, m * 128:(m + 1) * 128], o_t[z][:, c, 0:S], start=(c == 0), stop=(c == 7)),
                                  reads=[otb[z]] + woutb, writes=[pb])
                        fw.op("dve", lambda e: e.tensor_tensor(out=xt[z][:, m, 0:S], in0=xt[z][:, m, 0:S], in1=pa[:, 0:S], op=ALU.add),
                              reads=[pb, xtb[z][m]], writes=[xtb[z][m]])
                    for hf in range(2):
                        c0 = hf * 4
                        fw.dma(dst[s, c0 * 128:(c0 + 4) * 128, t0:t0 + S].rearrange("(c p) t -> p c t", p=128), xt[z][:, c0:c0 + 4, 0:S],
                               reads=xtb[z][c0:c0 + 4], writes=[dstb[s][ti]], q="pool")
        fw.barrier()

    def final_phase(src, srcb):
        outb = Buf()
        with ExitStack() as es:
            g_t, g_b = load_vec(es, "fin_g", final_norm, 8)
            xt = [es.enter_context(sbt("fi_xt%d" % i, [128, 8, 512], F32)) for i in range(2)]
            xtb = [bufs(8) for _ in range(2)]
            sq = es.enter_context(sbt("fi_sq", [128, 8, 512], BF16))
            sqb = Buf()
            rs = [es.enter_context(sbt("fi_rs%d" % i, [128, 512], F32)) for i in range(2)]
            rsb = bufs(2)
            yo = [es.enter_context(sbt("fi_yo%d" % i, [128, D], F32)) for i in range(3)]
            yob = bufs(3)
            allsrc = lambda s: list(srcb[s])
            k = 0
            yi = 0
            for s in range(NSEQ):
                for t0 in range(0, SEQ, 512):
                    n = min(512, SEQ - t0)
                    z = k % 2
                    k += 1
                    for hf in range(2):
                        c0 = hf * 4
                        fw.dma(xt[z][:, c0:c0 + 4, 0:n], src[s, c0 * 128:(c0 + 4) * 128, NMETA + t0:NMETA + t0 + n].rearrange("(c p) t -> p c t", p=128),
                               reads=allsrc(s), writes=xtb[z][c0:c0 + 4])
                    for hf in range(2):
                        c0 = hf * 4
                        fw.op("act", lambda e: e.activation(out=sq[:, c0:c0 + 4, 0:n], in_=xt[z][:, c0:c0 + 4, 0:n], func=AF.Square),
                              reads=xtb[z][c0:c0 + 4], writes=[sqb])
                    pa, pb = bank()
                    for c in range(8):
                        fw.op("pe", lambda e: e.matmul(pa[:, 0:n], ones_bf[:], sq[:, c, 0:n], start=(c == 0), stop=(c == 7)), reads=[sqb, cb], writes=[pb])
                    fw.op("act", lambda e: e.activation(out=rs[z][:, 0:n], in_=pa[:, 0:n], func=AF.Sqrt, scale=1.0 / D, bias=EPS), reads=[pb], writes=[rsb[z]])
                    fw.op("dve", lambda e: e.reciprocal(out=rs[z][:, 0:n], in_=rs[z][:, 0:n]), reads=[rsb[z]], writes=[rsb[z]])
                    for c in range(8):
                        fw.op("dve", lambda e: e.scalar_tensor_tensor(out=xt[z][:, c, 0:n], in0=xt[z][:, c, 0:n], scalar=g_t[:, c:c + 1], in1=rs[z][:, 0:n],
                                                                      op0=ALU.mult, op1=ALU.mult),
                              reads=[xtb[z][c], rsb[z], g_b], writes=[xtb[z][c]])
                    for b0 in range(0, n, 128):
                        u = yi % 3
                        yi += 1
                        for hv in range(2):
                            pa, pb = bank()
                            for cc in range(4):
                                c = hv * 4 + cc
                                fw.op("pe", lambda e: e.transpose(pa[:, cc * 128:(cc + 1) * 128], xt[z][:, c, b0:b0 + 128], ident[:]),
                                      reads=[xtb[z][c], cb], writes=[pb])
                            if hv == 0:
                                fw.op("act", lambda e: e.activation(out=yo[u][:, 0:512], in_=pa[:, 0:512], func=AF.Copy), reads=[pb], writes=[yob[u]])
                            else:
                                fw.op("dve", lambda e: e.tensor_copy(yo[u][:, 512:1024], pa[:, 0:512]), reads=[pb], writes=[yob[u]])
                        fw.dma(out_ap[s, t0 + b0:t0 + b0 + 128, :], yo[u][:, :], reads=[yob[u]], writes=[Buf()])
        fw.barrier()

    prologue(XP[0], XB[0])
    cur = 0
    for layer in range(DEPTH):
        j = layer // 2
        a, b = cur, 1 - cur
        if layer % 2 == 0:
            even_phase(j, XP[a], XB[a], XP[b], XB[b])
        else:
            odd_proj_phase(j, XP[a], XB[a])
            if dbg != 1:
                attn_phase()
            if dbg == 0:
                odd_out_phase(j, XP[a], XB[a], XP[b], XB[b])
            else:
                a, b = b, a
                cur = 1 - cur
        ffn_phase(layer, 0, XP[b], XB[b], XP[a], XB[a])
        ffn_phase(layer, 1, XP[b], XB[b], XP[a], XB[a])
    final_phase(XP[cur], XB[cur])
    return nc, fw


def rope_tables(T):
    pos = np.arange(T, dtype=np.float32)
    inv_freq = (np.float32(10000.0) ** (-np.arange(0, QKR, 2, dtype=np.float32) / np.float32(QKR))).astype(np.float32)
    ang = (pos[None, :] * inv_freq[:, None]).astype(np.float32)
    cos = np.cos(ang).astype(np.float32)
    sin = np.sin(ang).astype(np.float32)
    tab = np.zeros((128, 2, T), np.float32)
    for p in range(128):
        f = p % 16
        tab[p, 0] = cos[f]
        tab[p, 1] = -sin[f] if (p % 32) < 16 else sin[f]
    return tab


_WEIGHT_NAMES = ["meta_tokens", "ev_norm", "ev_w_in", "ev_conv_a", "ev_conv_b", "ev_conv_b_bias", "ev_gate_r_w", "ev_gate_r_b",
                 "ev_gate_i_w", "ev_gate_i_b", "ev_lru_lambda", "ev_w_out", "od_norm", "od_w_in", "od_q_norm", "od_kv_norm",
                 "od_w_uq", "od_w_ukv", "od_w_out", "ffn_norm", "ffn_w_up", "ffn_conv_w", "ffn_conv_b", "ffn_w_down", "final_norm"]


def run(inputs, ncores=NCORES, depth=4, same_engine_sync=True, trace=False, dbg=0):
    x = np.ascontiguousarray(np.asarray(inputs["x"], dtype=np.float32))
    B, SEQ, _ = x.shape
    nseq = B // ncores
    nc, fw = build_program(NSEQ=nseq, SEQ=SEQ, DEPTH=depth, same_engine_sync=same_engine_sync, dbg=dbg)
    nod = depth // 2
    shared = {}
    for n in _WEIGHT_NAMES:
        if n.startswith("od_") and nod == 0:
            continue
        shared[n] = np.ascontiguousarray(np.asarray(inputs[n], dtype=np.float32))
    if nod:
        shared["rope_cs"] = rope_tables(SEQ + NMETA)
    in_maps = []
    for c in range(ncores):
        m = dict(shared)
        m["x"] = x[c * nseq:(c + 1) * nseq]
        in_maps.append(m)
    res = run_bass_kernel_spmd(nc, in_maps, core_ids=list(range(ncores)), trace=trace)
    out = np.concatenate([r["out"] for r in res.results], axis=0)
    return out, res


def kernel(**inputs):
    out, _ = run(inputs)
    return out.astype(np.float32)
```
